# Optimizing a Trainium2 kernel written in Bass

```python
import math
import jax, jax.numpy as jnp
from jax import lax
import numpy as np

D_MODEL = 1024
BATCH = 8
SEQ = 4096
DEPTH = 1
DEC_BATCH = 8
DEC_SEQ = 64
PAST_LEN = 4096

CHUNK = 64
HEAD_DIM = 64
SSM_WIDTH = D_MODEL // 2
SSM_HEADS = SSM_WIDTH // HEAD_DIM
SSM_GROUPS = 2
D_STATE = 128
CONV_W = 4
CONV_DIM = SSM_WIDTH + 2 * SSM_GROUPS * D_STATE
ATTN_WIDTH = D_MODEL - SSM_WIDTH
ATTN_HEADS = ATTN_WIDTH // HEAD_DIM
KV_HEADS = 2
Q_PER_KV = ATTN_HEADS // KV_HEADS
WINDOW = 128
WIN_CHUNKS = WINDOW // CHUNK
D_FF = 256 * math.ceil(8 * D_MODEL / (3 * 256))
IN_COLS = SSM_WIDTH + CONV_DIM + SSM_HEADS + ATTN_WIDTH + 2 * KV_HEADS * HEAD_DIM
V_COL_START = IN_COLS - KV_HEADS * HEAD_DIM
LN_EPS = 1e-5
RMS_EPS = 1e-5
DEEPNORM_ALPHA = (2.0 * DEPTH) ** 0.25
DEEPNORM_BETA = (8.0 * DEPTH) ** -0.25

kernel_name = 'hymba_ssd_swa_sink_alibi_deepnorm_stream_step'


def in_proj_split_points():
    sizes = [SSM_WIDTH, CONV_DIM, SSM_HEADS, ATTN_WIDTH, KV_HEADS * HEAD_DIM]
    return [int(s) for s in np.cumsum(sizes)]


def alibi_slopes():
    return jnp.exp2(-8.0 * jnp.arange(1, ATTN_HEADS + 1, dtype=jnp.float32) / ATTN_HEADS)


def layer_norm(x, g, b):
    xf = x.astype(jnp.float32)
    mu = jnp.mean(xf, axis=-1, keepdims=True)
    var = jnp.mean(jnp.square(xf - mu), axis=-1, keepdims=True)
    out = (xf - mu) * lax.rsqrt(var + LN_EPS) * g.astype(jnp.float32) + b.astype(jnp.float32)
    return out.astype(x.dtype)


def gated_rms_norm(y, z, w):
    g = y * jax.nn.silu(z.astype(jnp.float32))
    g = g.reshape(*g.shape[:-1], SSM_GROUPS, SSM_WIDTH // SSM_GROUPS)
    g = g * lax.rsqrt(jnp.mean(jnp.square(g), axis=-1, keepdims=True) + RMS_EPS)
    return g.reshape(*y.shape) * w.astype(jnp.float32)


def ssd_scan(xs, dt, a, bm, cm, s0):
    bsz, seq = xs.shape[:2]
    cl = min(CHUNK, seq)
    nc = seq // cl
    r = SSM_HEADS // SSM_GROUPS
    xdt = (xs * dt[..., None]).reshape(bsz, nc, cl, SSM_GROUPS, r, HEAD_DIM)
    cum = jnp.cumsum((dt * a).reshape(bsz, nc, cl, SSM_GROUPS, r), axis=2)
    bm = bm.reshape(bsz, nc, cl, SSM_GROUPS, D_STATE)
    cm = cm.reshape(bsz, nc, cl, SSM_GROUPS, D_STATE)
    seg = cum[:, :, :, None] - cum[:, :, None, :]
    causal = jnp.tril(jnp.ones((cl, cl), dtype=bool))[:, :, None, None]
    decay = jnp.exp(jnp.where(causal, seg, -jnp.inf))
    cb = jnp.einsum('bclgn,bcsgn->bclsg', cm, bm)
    y_diag = jnp.einsum('bclsgr,bcsgrp->bclgrp', cb[..., None] * decay, xdt)
    to_end = jnp.exp(cum[:, :, -1:] - cum)
    block_states = jnp.einsum('bclgn,bclgrp->bcgrpn', bm, xdt * to_end[..., None])
    block_decay = jnp.exp(cum[:, :, -1])

    def carry_step(s, inp):
        st, dec = inp
        return dec[..., None, None] * s + st, s

    s_final, s_prev = lax.scan(
        carry_step, s0.reshape(bsz, SSM_GROUPS, r, HEAD_DIM, D_STATE),
        (jnp.moveaxis(block_states, 1, 0), jnp.moveaxis(block_decay, 1, 0)))
    s_prev = jnp.moveaxis(s_prev, 0, 1)
    y_off = jnp.einsum('bclgn,bcgrpn->bclgrp', cm, s_prev) * jnp.exp(cum)[..., None]
    y = (y_diag + y_off).reshape(bsz, seq, SSM_HEADS, HEAD_DIM)
    return y, s_final.reshape(bsz, SSM_HEADS, HEAD_DIM, D_STATE)


def sink_attention(q, k, v, dist, valid, slopes, sinks):
    f32 = jnp.float32
    s = jnp.einsum('...qgrd,...sgd->...grqs', q.astype(f32), k.astype(f32)) * (HEAD_DIM ** -0.5)
    s = s - slopes.reshape(KV_HEADS, Q_PER_KV, 1, 1) * dist
    s = jnp.where(valid, s, -jnp.inf)
    sink = sinks.astype(f32).reshape(KV_HEADS, Q_PER_KV, 1, 1)
    m = jnp.maximum(jnp.max(s, axis=-1, keepdims=True), sink)
    p = jnp.exp(s - m)
    denom = jnp.sum(p, axis=-1, keepdims=True) + jnp.exp(sink - m)
    return jnp.einsum('...grqs,...sgd->...qgrd', p / denom, v.astype(f32))


def band_attention(q, k, v, slopes, sinks):
    bsz, seq = q.shape[:2]
    nc = seq // CHUNK
    band = (WIN_CHUNKS + 1) * CHUNK
    qc = q.reshape(bsz, nc, CHUNK, KV_HEADS, Q_PER_KV, HEAD_DIM)

    def banded(t):
        tp = jnp.pad(t, ((0, 0), (WIN_CHUNKS * CHUNK, 0), (0, 0), (0, 0)))
        tp = tp.reshape(bsz, nc + WIN_CHUNKS, CHUNK, KV_HEADS, HEAD_DIM)
        return jnp.concatenate([tp[:, j:j + nc] for j in range(WIN_CHUNKS + 1)], axis=2)

    kb, vb = banded(k), banded(v)
    qi = jnp.arange(CHUNK)[:, None] + WIN_CHUNKS * CHUNK
    kj = jnp.arange(band)[None, :]
    dist = jnp.abs(qi - kj).astype(jnp.float32)
    key_chunk = jnp.arange(nc)[:, None] - WIN_CHUNKS + kj // CHUNK
    valid = (key_chunk >= 0)[:, None, None, None, :]
    return sink_attention(qc, kb, vb, dist, valid, slopes, sinks)


def window_step_attention(q, kk, vv, slopes, sinks):
    t = q.shape[1]
    dist = jnp.abs(jnp.arange(t)[:, None] + WINDOW - jnp.arange(WINDOW + t)[None, :]).astype(jnp.float32)
    valid = jnp.ones((t, WINDOW + t), dtype=bool)
    return sink_attention(q, kk, vv, dist, valid, slopes, sinks)


def token_mixers(x, conv_hist, ssm_state, win_k, win_v, slopes, w_in, conv_w, conv_b, dt_bias, a_log,
                 d_skip, ssm_norm_w, attn_sinks, w_out):
    f32 = jnp.float32
    bsz, seq = x.shape[:2]
    proj = jnp.einsum('bld,de->ble', x, w_in)
    z, xbc, dt_raw, q, k, v = jnp.split(proj, in_proj_split_points(), axis=-1)
    xbc_hist = jnp.concatenate([conv_hist.astype(xbc.dtype), xbc], axis=1)
    conv = conv_b
    for i in range(CONV_W):
        conv = conv + conv_w[i] * xbc_hist[:, i:i + seq]
    xbc = jax.nn.silu(conv)
    new_conv = xbc_hist[:, -(CONV_W - 1):]
    xs, bm, cm = jnp.split(xbc.astype(f32), [SSM_WIDTH, SSM_WIDTH + SSM_GROUPS * D_STATE], axis=-1)
    xs = xs.reshape(bsz, seq, SSM_HEADS, HEAD_DIM)
    dt = jax.nn.softplus(dt_raw.astype(f32) + dt_bias.astype(f32))
    a = -jnp.exp(a_log.astype(f32))
    y, new_ssm = ssd_scan(xs, dt, a, bm.reshape(bsz, seq, SSM_GROUPS, D_STATE),
                          cm.reshape(bsz, seq, SSM_GROUPS, D_STATE), ssm_state.astype(f32))
    y = y + d_skip.astype(f32)[:, None] * xs
    y = gated_rms_norm(y.reshape(bsz, seq, SSM_WIDTH), z, ssm_norm_w)
    q = q.reshape(bsz, seq, KV_HEADS, Q_PER_KV, HEAD_DIM)
    k = k.reshape(bsz, seq, KV_HEADS, HEAD_DIM)
    v = v.reshape(bsz, seq, KV_HEADS, HEAD_DIM)
    if win_k is None:
        o = band_attention(q, k, v, slopes, attn_sinks)
        new_k, new_v = k[:, -WINDOW:], v[:, -WINDOW:]
    else:
        kk = jnp.concatenate([win_k.astype(k.dtype), k], axis=1)
        vv = jnp.concatenate([win_v.astype(v.dtype), v], axis=1)
        o = window_step_attention(q, kk, vv, slopes, attn_sinks)
        new_k, new_v = kk[:, -WINDOW:], vv[:, -WINDOW:]
    o = o.reshape(bsz, seq, ATTN_WIDTH)
    mixed = jnp.concatenate([y.astype(x.dtype), o.astype(x.dtype)], axis=-1)
    out = jnp.einsum('ble,ed->bld', mixed, w_out)
    return out, new_conv, new_ssm, new_k, new_v


def swiglu(h, w_gate, w_up, w_down):
    g = jnp.einsum('bld,df->blf', h, w_gate)
    u = jnp.einsum('bld,df->blf', h, w_up)
    return jnp.einsum('blf,fd->bld', jax.nn.silu(g) * u, w_down)


def trunk_layer(x, conv_hist, ssm_state, win_k, win_v, slopes, w_in, conv_w, conv_b, dt_bias, a_log,
                d_skip, ssm_norm_w, attn_sinks, w_out, ln1_g, ln1_b, w_gate, w_up, w_down, ln2_g, ln2_b):
    mix, new_conv, new_ssm, new_k, new_v = token_mixers(
        x, conv_hist, ssm_state, win_k, win_v, slopes, w_in, conv_w, conv_b, dt_bias, a_log,
        d_skip, ssm_norm_w, attn_sinks, w_out)
    h = layer_norm(DEEPNORM_ALPHA * x + mix, ln1_g, ln1_b)
    y = layer_norm(DEEPNORM_ALPHA * h + swiglu(h, w_gate, w_up, w_down), ln2_g, ln2_b)
    return y, new_conv, new_ssm, new_k, new_v


def setup_inputs(seed: int = 0) -> dict:
    key = jax.random.key(seed)
    ks = jax.random.split(key, 24)
    f32 = jnp.float32

    def nrm(k, shape, scale):
        return scale * jax.random.normal(k, shape, f32)

    x_prompt = nrm(ks[0], (BATCH, SEQ, D_MODEL), 1.0)
    x_sample = nrm(ks[1], (DEC_BATCH, DEC_SEQ, D_MODEL), 1.0)
    state_conv = nrm(ks[2], (DEPTH, DEC_BATCH, CONV_W - 1, CONV_DIM), 1.0)
    state_ssm = nrm(ks[3], (DEPTH, DEC_BATCH, SSM_HEADS, HEAD_DIM, D_STATE), 0.1)
    cache_k = nrm(ks[4], (DEPTH, DEC_BATCH, WINDOW, KV_HEADS, HEAD_DIM), 1.0)
    cache_v = nrm(ks[5], (DEPTH, DEC_BATCH, WINDOW, KV_HEADS, HEAD_DIM), 1.0)
    w_in = nrm(ks[6], (DEPTH, D_MODEL, IN_COLS), D_MODEL ** -0.5)
    w_in = w_in * jnp.where(jnp.arange(IN_COLS) >= V_COL_START, DEEPNORM_BETA, 1.0).astype(f32)
    conv_w = nrm(ks[7], (DEPTH, CONV_W, CONV_DIM), CONV_W ** -0.5)
    conv_b = nrm(ks[8], (DEPTH, CONV_DIM), 0.01)
    dt0 = jnp.exp(jax.random.uniform(ks[9], (DEPTH, SSM_HEADS), f32, math.log(1e-3), math.log(1e-1)))
    dt_bias = dt0 + jnp.log(-jnp.expm1(-dt0))
    a_log = jnp.log(jax.random.uniform(ks[10], (DEPTH, SSM_HEADS), f32, 1.0, 16.0))
    d_skip = 1.0 + nrm(ks[11], (DEPTH, SSM_HEADS), 0.1)
    ssm_norm_w = 1.0 + nrm(ks[12], (DEPTH, SSM_WIDTH), 0.02)
    attn_sinks = nrm(ks[13], (DEPTH, ATTN_HEADS), 0.5)
    w_out = nrm(ks[14], (DEPTH, D_MODEL, D_MODEL), DEEPNORM_BETA * D_MODEL ** -0.5)
    ln1_g = 1.0 + nrm(ks[15], (DEPTH, D_MODEL), 0.02)
    ln1_b = nrm(ks[16], (DEPTH, D_MODEL), 0.02)
    w_gate = nrm(ks[17], (DEPTH, D_MODEL, D_FF), D_MODEL ** -0.5)
    w_up = nrm(ks[18], (DEPTH, D_MODEL, D_FF), D_MODEL ** -0.5)
    w_down = nrm(ks[19], (DEPTH, D_FF, D_MODEL), DEEPNORM_BETA * D_FF ** -0.5)
    ln2_g = 1.0 + nrm(ks[20], (DEPTH, D_MODEL), 0.02)
    ln2_b = nrm(ks[21], (DEPTH, D_MODEL), 0.02)
    return {'x_prompt': x_prompt, 'x_sample': x_sample, 'state_conv': state_conv, 'state_ssm': state_ssm,
            'cache_k': cache_k, 'cache_v': cache_v, 'w_in': w_in, 'conv_w': conv_w, 'conv_b': conv_b,
            'dt_bias': dt_bias, 'a_log': a_log, 'd_skip': d_skip, 'ssm_norm_w': ssm_norm_w,
            'attn_sinks': attn_sinks, 'w_out': w_out, 'ln1_g': ln1_g, 'ln1_b': ln1_b, 'w_gate': w_gate,
            'w_up': w_up, 'w_down': w_down, 'ln2_g': ln2_g, 'ln2_b': ln2_b}


def reference(x_prompt, x_sample, state_conv, state_ssm, cache_k, cache_v, w_in, conv_w, conv_b, dt_bias,
              a_log, d_skip, ssm_norm_w, attn_sinks, w_out, ln1_g, ln1_b, w_gate, w_up, w_down, ln2_g, ln2_b):
    slopes = alibi_slopes()
    bsz = x_prompt.shape[0]
    h_p, h_s = x_prompt, x_sample
    new_p, new_s = [], []
    for layer in range(DEPTH):
        lw = (w_in[layer], conv_w[layer], conv_b[layer], dt_bias[layer], a_log[layer], d_skip[layer],
              ssm_norm_w[layer], attn_sinks[layer], w_out[layer], ln1_g[layer], ln1_b[layer],
              w_gate[layer], w_up[layer], w_down[layer], ln2_g[layer], ln2_b[layer])
        zero_conv = jnp.zeros((bsz, CONV_W - 1, CONV_DIM), x_prompt.dtype)
        zero_ssm = jnp.zeros((bsz, SSM_HEADS, HEAD_DIM, D_STATE), jnp.float32)
        h_p, *st_p = trunk_layer(h_p, zero_conv, zero_ssm, None, None, slopes, *lw)
        h_s, *st_s = trunk_layer(h_s, state_conv[layer], state_ssm[layer], cache_k[layer], cache_v[layer],
                                 slopes, *lw)
        new_p.append(st_p)
        new_s.append(st_s)
    conv_p = jnp.stack([s[0] for s in new_p])
    ssm_p = jnp.stack([s[1] for s in new_p])
    k_p = jnp.stack([s[2] for s in new_p])
    v_p = jnp.stack([s[3] for s in new_p])
    conv_s = jnp.stack([s[0] for s in new_s])
    ssm_s = jnp.stack([s[1] for s in new_s])
    k_s = jnp.stack([s[2] for s in new_s])
    v_s = jnp.stack([s[3] for s in new_s])
    return (h_p, h_s, conv_p, ssm_p, k_p, v_p, conv_s, ssm_s, k_s, v_s)
```

```python
import numpy as np
import ml_dtypes
from contextlib import ExitStack
import concourse.bass as bass
import concourse.mybir as mybir
from concourse.bass_utils import run_bass_kernel_spmd

F32 = mybir.dt.float32
BF16 = mybir.dt.bfloat16
F32R = mybir.dt.float32r
AF = mybir.ActivationFunctionType
ALU = mybir.AluOpType

ALPHA = 2.0 ** 0.25
LN_EPS = 1e-5
RMS_EPS = 1e-5

IDF, TRI, MST, CAU, ONE, CW, CBI, DTB, ALOG, TOKM, DCOL = 0, 128, 256, 384, 512, 640, 672, 680, 688, 696, 697
DFULL, WN, L1G, L1B, L2G, L2B, NCF = 704, 1216, 1728, 2752, 3776, 4800, 5824
IDB, ONB, BIAS, NCB = 0, 128, 256, 4352


class Ev:
    __slots__ = ("sem", "val")

    def __init__(self, sem, val):
        self.sem = sem
        self.val = val


class Buf:
    def __init__(self, name, excl=False):
        self.name = name
        self.excl = excl
        self.w = None
        self.r = {}
        self.dsem = {}
        self.dcnt = {}


class TK:
    def __init__(self, nc, es):
        self.nc = nc
        self.es = es
        self.E = {"pe": nc.tensor, "act": nc.scalar, "dve": nc.vector, "pool": nc.gpsimd, "sp": nc.sync}
        self.sem = {k: es.enter_context(nc.semaphore("sem_" + k)) for k in ("pe", "act", "dve", "pool")}
        self.cnt = {k: 0 for k in self.sem}
        self.seen = {k: {} for k in self.E}
        self.dbufs = []

    def _wait(self, e, ev):
        if ev is None:
            return
        if e == "pe" and ev.sem is self.sem["pe"]:
            return
        k = id(ev.sem)
        if self.seen[e].get(k, 0) >= ev.val:
            return
        self.E[e].wait_ge(ev.sem, ev.val)
        self.seen[e][k] = ev.val

    def _deps(self, e, reads, writes):
        for b in reads:
            self._wait(e, b.w)
            if b.excl:
                for r in b.r.values():
                    self._wait(e, r)
        for b in writes:
            self._wait(e, b.w)
            for r in b.r.values():
                self._wait(e, r)

    def _rec(self, ev, reads, writes):
        for b in reads:
            b.r[id(ev.sem)] = ev
        for b in writes:
            b.w = ev
            b.r = {}

    def op(self, e, fn, reads, writes):
        self._deps(e, reads, writes)
        inst = fn(self.E[e])
        self.cnt[e] += 1
        inst.then_inc(self.sem[e], 1)
        self._rec(Ev(self.sem[e], self.cnt[e]), reads, writes)

    def group(self, e, fns, reads, writes):
        self._deps(e, reads, writes)
        inst = None
        for fn in fns:
            inst = fn(self.E[e])
        self.cnt[e] += 1
        inst.then_inc(self.sem[e], 1)
        self._rec(Ev(self.sem[e], self.cnt[e]), reads, writes)

    def dma(self, q, out, in_, reads, writes, sb, chain=False):
        if chain and sb.dsem.get(q) is not None:
            saved = [(b, b.w) for b in writes if b.w is not None and b.w.sem is sb.dsem[q]]
            for b, _ in saved:
                b.w = None
            self._deps(q, reads, writes)
            for b, w in saved:
                b.w = w
        else:
            self._deps(q, reads, writes)
        if q not in sb.dsem:
            sb.dsem[q] = self.es.enter_context(self.nc.semaphore("d_" + q + "_" + sb.name))
            sb.dcnt[q] = 0
            self.dbufs.append((sb, q))
        inst = self.E[q].dma_start(out=out, in_=in_)
        sb.dcnt[q] += 16
        inst.then_inc(sb.dsem[q], 16)
        self._rec(Ev(sb.dsem[q], sb.dcnt[q]), reads, writes)

    def finish(self):
        for e in ("pool", "sp"):
            for sb, q in self.dbufs:
                self._wait(e, Ev(sb.dsem[q], sb.dcnt[q]))


class _Stop(Exception):
    pass


def build(NP=32):
    import os
    KSTOP = float(os.environ.get("KSTOP", "99"))

    def ckpt(n):
        if n >= KSTOP:
            raise _Stop()

    nc = bass.Bass("TRN2", target_bir_lowering=False)
    TALL = NP * 128

    def din(name, shape, dt=F32):
        return nc.dram_tensor(name, shape, dt, kind="ExternalInput").ap()

    def dout(name, shape, dt=F32):
        return nc.dram_tensor(name, shape, dt, kind="ExternalOutput").ap()

    x_d = din("x", [TALL, 1024])
    xs_d = din("xs", [64, 1024])
    w_in_d = din("w_in", [1024, 2312])
    w_out_d = din("w_out", [1024, 1024])
    w_gate_d = din("w_gate", [1024, 2816])
    w_up_d = din("w_up", [1024, 2816])
    w_down_d = din("w_down", [2816, 1024])
    cf_d = din("cf", [128, NCF])
    cb_d = din("cb", [128, NCB], BF16)
    sink_d = din("sink", [128, 1024])
    halo_d = din("halo_s", [128, 24])
    sinit_d = din("sinit", [128, 512])
    ck_d = din("ck", [128, 128])
    cv_d = din("cv", [128, 128])
    y_d = dout("y", [TALL, 1024])
    ys_d = dout("ys", [64, 1024])
    convp_d = dout("conv_p", [3, 1024])
    ssmp_d = dout("ssm_p", [128, 512])
    kp_d = dout("k_p", [128, 128])
    vp_d = dout("v_p", [128, 128])
    convs_d = dout("conv_s", [3, 1024])
    ssms_d = dout("ssm_s", [128, 512])
    ks_d = dout("k_s", [128, 128])
    vs_d = dout("v_s", [128, 128])

    NPIECE = 11 + 22 + 12
    wscr = nc.dram_tensor("wscr", [NPIECE, 128, 2048], BF16, kind="Internal").ap()

    es = ExitStack()
    with es:
        import os as _os
        for _i in range(int(_os.environ.get("KDUMMY", "0"))):
            es.enter_context(nc.semaphore(f"dummy{_i}"))
        tk = TK(nc, es)

        def sb(name, shape, dt=F32):
            return es.enter_context(nc.sbuf_tensor("sb_" + name, shape, dt))

        cf = sb("cf", [128, NCF]); B_cf = Buf("cf")
        cb = sb("cb", [128, NCB], BF16); B_cb = Buf("cb")
        sinkrow = sb("sinkrow", [128, 1024], BF16); B_sink = Buf("sinkrow")
        wout = sb("wout", [128, 8, 1024], BF16); B_wout = Buf("wout")
        slots = [sb(f"slot{i}", [128, 8, 256], BF16) for i in range(3)]
        B_slot = [Buf(f"slot{i}") for i in range(3)]
        NSTG = 3
        stages = [sb(f"stage{i}", [128, 8, 128]) for i in range(NSTG)]
        B_stage = [Buf(f"stage{i}") for i in range(NSTG)]
        xin = [sb(f"xin{i}", [128, 1024]) for i in range(4)]
        B_xin = [Buf(f"xin{i}") for i in range(4)]
        hb = [sb(f"hb{i}", [128, 1024]) for i in range(4)]
        B_hb = [Buf(f"hb{i}") for i in range(4)]
        xT = sb("xT", [128, 8, 512], BF16)
        B_xT = [Buf(f"xT{i}") for i in range(4)]
        U = sb("U", [128, 24, 515], BF16)
        B_UH = [[Buf(f"U{i}a"), Buf(f"U{i}b")] for i in range(24)]

        def BU(r, b=None):
            return list(B_UH[r]) if b is None else [B_UH[r][b // 2]]
        kT = sb("kT", [128, 640], BF16); B_kT = Buf("kT")
        Vt = sb("Vt", [128, 5, 128], BF16); B_Vt = Buf("Vt")
        ktok = sb("ktok", [128, 128]); B_ktok = Buf("ktok")
        vtok = sb("vtok", [128, 128]); B_vtok = Buf("vtok")
        dtpre = sb("dtpre", [128, 4, 8]); B_dtpre = Buf("dtpre")
        halo = sb("halo", [128, 8, 3], BF16); B_halo = Buf("halo")
        halo_s = sb("halo_s", [128, 8, 3]); B_halos = Buf("halos")
        S = sb("S", [128, 512]); B_S = Buf("S")
        Sb_ = sb("Sb", [128, 512], BF16); B_Sb = Buf("Sb")
        Fb = [sb(f"F{i}", [128, 512]) for i in range(6)]
        B_F = [Buf(f"F{i}") for i in range(6)]
        H = sb("H", [128, 1024]); B_H = Buf("H")
        H2 = sb("H2", [128, 1024]); B_H2 = Buf("H2")
        HS = [(H, B_H), (H2, B_H2)]
        L = sb("L", [128, 8, 128]); B_L = Buf("L")
        decay = sb("decay", [128, 8, 128]); B_decay = Buf("decay")
        MT = sb("MT", [128, 8, 128], BF16); B_MT = Buf("MT")
        xdt = sb("xdt", [128, 512], BF16); B_xdt = Buf("xdt")
        xdte = sb("xdte", [128, 512], BF16); B_xdte = Buf("xdte")
        xsD = sb("xsD", [128, 512], BF16); B_xsD = Buf("xsD")
        Btok = sb("Btok", [128, 256], BF16); B_Btok = Buf("Btok")
        cbm = sb("cbm", [128, 2, 128]); B_cbm = Buf("cbm")
        gn = sb("gn", [128, 512], BF16); B_gn = Buf("gn")
        PT = [sb(f"PT{i}", [128, 512], BF16) for i in range(4)]
        B_PT = [Buf(f"PT{i}") for i in range(4)]
        mixS = [sb(f"mixS{i}", [128, 4, 128], BF16) for i in range(2)]
        B_mixS = [Buf(f"mixS{i}") for i in range(2)]
        attT = sb("attT", [128, 4, 512], BF16)
        B_attT = [Buf(f"attT{i}") for i in range(4)]
        Ab = sb("Ab", [128, 8]); B_Ab = Buf("Ab")
        sm = {}
        B_sm = {}
        SMW = {}
        for nm, w in (("dt", 8), ("ab", 8), ("e1", 8), ("l1", 8), ("dtA", 8), ("cum", 8), ("ecum", 8), ("dd", 8),
                      ("toend", 8), ("bdec", 8), ("st", 12), ("mv", 2), ("ve", 1), ("lnv", 1), ("rstd", 1),
                      ("nmr", 1), ("st2", 12), ("mv2", 4), ("ms", 2), ("lms", 2), ("rs2", 2)):
            sm[nm] = sb("sm_" + nm, [128, w])
            SMW[nm] = w
            B_sm[nm] = Buf("sm_" + nm)

        sout = Fb[0]; B_sout = B_F[0]
        sinit = Fb[0]; B_sinit = B_F[0]
        ck_sb = Fb[1][:, 0:128]; cv_sb = Fb[1][:, 128:256]; B_ck = B_F[1]; B_cv = B_F[1]
        tri_rt = sb("tri_r", [128, 128]); B_trir = Buf("tri_r")
        diagD = sb("diagD", [128, 4, 128], BF16); B_diagD = Buf("diagD")
        dtw = sb("dtw", [128, 4, 8]); B_dtw = Buf("dtw")
        dtall = sb("dtall", [128, 4, 8]); B_dtall = Buf("dtall")
        dtAall = sb("dtAall", [128, 4, 8])
        ecumall = sb("ecumall", [128, 4, 8])
        toendall = sb("toendall", [128, 4, 8])
        bdecall = sb("bdecall", [128, 4, 8])
        L1 = sb("L1", [128, 8, 128]); B_L1 = Buf("L1")
        MT1 = sb("MT1", [128, 8, 128], BF16); B_MT1 = Buf("MT1")
        xdt1 = sb("xdt1", [128, 512], BF16); B_xdt1 = Buf("xdt1")
        xdte1 = sb("xdte1", [128, 512], BF16); B_xdte1 = Buf("xdte1")
        xsD1 = sb("xsD1", [128, 512], BF16); B_xsD1 = Buf("xsD1")
        Btok1 = sb("Btok1", [128, 256], BF16); B_Btok1 = Buf("Btok1")
        cbm1 = sb("cbm1", [128, 2, 128]); B_cbm1 = Buf("cbm1")
        gn1 = sb("gn1", [128, 512], BF16); B_gn1 = Buf("gn1")
        sm1 = {}
        B_sm1 = {}
        for nm in list(sm.keys()):
            sm1[nm] = sb("sm1_" + nm, [128, SMW[nm]])
            B_sm1[nm] = Buf("sm1_" + nm)
        SSDSET = [
            (L, B_L, decay, B_decay, MT, B_MT, xdt, B_xdt, xdte, B_xdte, xsD, B_xsD, Btok, B_Btok,
             cbm, B_cbm, gn, B_gn, Fb[2], B_F[2], Fb[3], B_F[3]),
            (L1, B_L1, stages[1], B_stage[1], MT1, B_MT1, xdt1, B_xdt1, xdte1, B_xdte1, xsD1, B_xsD1,
             Btok1, B_Btok1, cbm1, B_cbm1, gn1, B_gn1, Fb[4], B_F[4], Fb[5], B_F[5]),
        ]
        SMSETS = [(sm, B_sm), (sm1, B_sm1)]
        LA = {}
        LNSETS = []
        for k_ in range(4):
            d_, B_d = {}, {}
            for nm in ("st", "mv", "ve", "lnv", "rstd", "nmr"):
                d_[nm] = sb(f"ln{k_}_{nm}", [128, SMW[nm]])
                B_d[nm] = Buf(f"ln{k_}_{nm}")
            LNSETS.append((d_, B_d))

        banks = [es.enter_context(nc.psum_tensor(f"bank{i}", [128, 512], F32)) for i in range(8)]
        B_bank = [Buf(f"bank{i}", excl=True) for i in range(8)]
        bank_ctr = [0]

        def ps():
            i = bank_ctr[0] % 8
            bank_ctr[0] += 1
            return banks[i], B_bank[i]

        def V(fn, r, w):
            tk.op("dve", fn, r, w)

        def A(fn, r, w):
            tk.op("act", fn, r, w)

        def G(fn, r, w):
            tk.op("pool", fn, r, w)

        def PE(fns, r, w):
            tk.group("pe", fns, r, w)

        def mm(out, lhsT, rhs, start, stop):
            return lambda e: e.matmul(out, lhsT, rhs, start=start, stop=stop)

        def tp(out, in_, ident):
            return lambda e: e.transpose(out, in_, ident)

        def cfc(off, n):
            return cf[:, off:off + n]

        ident_f = cfc(IDF, 128)
        tri_f = cfc(TRI, 128)
        mst_f = cfc(MST, 128)
        cau_f = tri_f
        tri_r = tri_rt[:]
        ones_f = cfc(ONE, 128)
        ident_b = cb[:, IDB:IDB + 128]
        ones_b = cb[:, ONB:ONB + 128]
        Cr = [B_cf]
        Cbr = [B_cb]

        tk.dma("sp", cf[:], cf_d[:, :], [], [B_cf], B_cf)
        tk.dma("sp", cb[:], cb_d[:, :], [], [B_cb], B_cb)
        tk.dma("sp", H[:], sink_d[:, :], [], [B_H], B_H)
        tk.dma("sp", halo_s[:].rearrange("p a b -> p (a b)"), halo_d[:, :], [], [B_halos], B_halos)
        A(lambda e: e.activation(out=Ab[:], in_=cfc(ALOG, 8), func=AF.Exp), Cr, [B_Ab])
        V(lambda e: e.tensor_scalar(out=Ab[:], in0=Ab[:], scalar1=-1.0, scalar2=None, op0=ALU.mult), [B_Ab], [B_Ab])
        MTf = MT[:].rearrange("p a b -> p (a b)")
        A(lambda e: e.activation(out=H[:], in_=H[:], func=AF.Exp), [B_H], [B_H])
        V(lambda e: e.tensor_copy(out=MTf, in_=H[:]), [B_H], [B_MT])
        V(lambda e: e.tensor_tensor(out=H[:], in0=H[:], in1=MTf, op=ALU.subtract), [B_H, B_MT], [B_H])
        V(lambda e: e.memset(sinkrow[:], 0.0), [], [B_sink])
        V(lambda e: e.tensor_copy(out=sinkrow[0:1, :], in_=MTf[0:1, :]), [B_MT], [B_sink])
        V(lambda e: e.tensor_copy(out=sinkrow[32:33, :], in_=H[32:33, :]), [B_H], [B_sink])
        for i_ in range(len(slots)):
            V(lambda e: e.memset(slots[i_][:], 0.0), [], [B_slot[i_]])
        V(lambda e: e.memset(S[:], 0.0), [], [B_S])
        V(lambda e: e.memset(Sb_[:], 0.0), [], [B_Sb])
        V(lambda e: e.memset(halo[:], 0.0), [], [B_halo])
        for c_ in range(4):
            A(lambda e: e.activation(out=diagD[:, c_, :], in_=ident_b, func=AF.Identity, scale=cf[:, DCOL + c_:DCOL + c_ + 1]),
              Cr + Cbr, [B_diagD])
        V(lambda e: e.tensor_scalar(out=tri_r.bitcast(F32R), in0=tri_f, scalar1=1.0, scalar2=None, op0=ALU.mult),
          Cr, [B_trir])

        wv = w_in_d.rearrange("(kc p) c -> p kc c", p=128)
        qv = w_in_d[:, 1544:2056].rearrange("(kc p) (g j d) -> p kc j g d", p=128, g=2, j=4, d=64)
        gv_ = w_gate_d.rearrange("(kc p) c -> p kc c", p=128)
        uv_ = w_up_d.rearrange("(kc p) c -> p kc c", p=128)
        dv_ = w_down_d.rearrange("(fc p) c -> p fc c", p=128)

        pieces = []

        def simple(name, view, c0, ncols):
            sts = []
            for o in range(0, ncols, 128):
                n = min(128, ncols - o)
                sts.append((o, n, 8, [(view[:, :, c0 + o:c0 + o + n], 0, n)]))
            pieces.append((name, sts))

        simple("z0", wv, 0, 256)
        simple("z1", wv, 256, 256)
        simple("x0", wv, 512, 256)
        simple("x1", wv, 768, 256)
        simple("B", wv, 1024, 256)
        simple("C", wv, 1280, 256)
        for qi in range(2):
            sts = []
            for jj in range(2):
                j = qi * 2 + jj
                sts.append((jj * 128, 128, 8, [(qv[:, :, j, 0, :], 0, 64), (qv[:, :, j, 1, :], 64, 128)]))
            pieces.append((f"q{qi}", sts))
        simple("k", wv, 2056, 128)
        simple("kv", wv, 2056, 256)
        simple("dt", wv, 1536, 8)
        P_IN = {n: i for i, (n, _) in enumerate(pieces)}
        for i in range(11):
            simple(f"g{i}", gv_, i * 256, 256)
            simple(f"u{i}", uv_, i * 256, 256)
        P_G0 = 11
        for dq in range(4):
            for fp in range(3):
                f0 = fp * 8
                nf = 8 if fp < 2 else 6
                sts = []
                for o in (0, 128):
                    sts.append((o, 128, nf, [(dv_[:, f0:f0 + nf, dq * 256 + o:dq * 256 + o + 128], 0, 128)]))
                pieces.append((f"d{dq}_{fp}", sts))
        P_D0 = 33
        assert len(pieces) == NPIECE
        B_scr = [Buf(f"scr{i}") for i in range(NPIECE)]
        wscr_v = [wscr[i].rearrange("p (k c) -> p k c", c=256) for i in range(NPIECE)]

        st_ctr = [0]
        slot_ctr = [0]

        def stage_load(srcs, nk):
            i = st_ctr[0] % NSTG
            st_ctr[0] += 1
            for n_, (src, lo, hi) in enumerate(srcs):
                tk.dma("sp", stages[i][:, 0:nk, lo:hi], src, [], [B_stage[i]], B_stage[i], chain=n_ > 0)
            return i

        def cast(eng, out, in_, r, w):
            if eng == "act":
                A(lambda e: e.copy(out=out, in_=in_), r, w)
            elif eng == "dve":
                V(lambda e: e.tensor_copy(out=out, in_=in_), r, w)
            else:
                G(lambda e: e.tensor_copy(out=out, in_=in_), r, w)

        def convert_piece(pi, engs):
            name, sts = pieces[pi]
            si = slot_ctr[0] % 3
            slot_ctr[0] += 1
            for n_, (o, n, nk, srcs) in enumerate(sts):
                i = stage_load(srcs, nk)
                cast(engs[n_ % len(engs)], slots[si][:, 0:nk, o:o + n], stages[i][:, 0:nk, 0:n],
                     [B_stage[i]], [B_slot[si]])
            tk.dma("pool", wscr_v[pi], slots[si][:], [B_slot[si]], [B_scr[pi]], B_slot[si])
            return slots[si], B_slot[si]

        def convert_wout():
            for cbk in range(8):
                cs = slice(cbk * 128, (cbk + 1) * 128)
                i = st_ctr[0] % NSTG
                st_ctr[0] += 1
                v0 = w_out_d[0:512, :].rearrange("(kc p) c -> p kc c", p=128)
                tk.dma("sp", stages[i][:, 0:4, :], v0[:, :, cs], [], [B_stage[i]], B_stage[i])
                v1 = w_out_d[512:1024, :].rearrange("(g r d) c -> d g r c", g=2, r=4, d=64)
                for g in range(2):
                    tk.dma("sp", stages[i][g * 64:(g + 1) * 64, 4:8, :], v1[:, g, :, cs], [], [B_stage[i]], B_stage[i],
                           chain=True)
                cast(("act", "dve", "pool")[cbk % 3], wout[:, :, cs], stages[i][:], [B_stage[i]], [B_wout])

        def get_piece(pi):
            if first_flag[0]:
                return convert_piece(pi, E3)
            return load_piece(pi)

        def load_piece(pi):
            si = slot_ctr[0] % 3
            slot_ctr[0] += 1
            tk.dma("sp", slots[si][:], wscr_v[pi], [B_scr[pi]], [B_slot[si]], B_slot[si])
            return slots[si], B_slot[si]

        tiles = [list(range(i, min(i + 4, NP))) for i in range(0, NP, 4)] + [[NP]]
        E3 = ("act", "dve", "pool")

        def x_loads(tile):
            for b, gb in enumerate(tile):
                if gb == NP:
                    V(lambda e: e.memset(xin[b][64:128, :], 0.0), [], [B_xin[b]])
                    tk.dma("sp", xin[b][0:64, :], xs_d[:, :], [], [B_xin[b]], B_xin[b])
                else:
                    tk.dma("sp", xin[b][:], x_d[gb * 128:(gb + 1) * 128, :], [], [B_xin[b]], B_xin[b])

        def transposes_to(src, B_src, b, B_dst, act_only=False):
            for half in range(2):
                pt, Bp = ps()
                PE([tp(pt[:, k * 128:(k + 1) * 128], src[:, (half * 4 + k) * 128:(half * 4 + k + 1) * 128], ident_f)
                    for k in range(4)], [B_src] + Cr, [Bp])
                o = xT[:, half * 4:(half + 1) * 4, b * 128:(b + 1) * 128]
                i_ = pt[:].rearrange("p (k t) -> p k t", k=4)
                if half == 0 or act_only:
                    A(lambda e: e.copy(out=o, in_=i_), [Bp], [B_dst])
                else:
                    V(lambda e: e.tensor_copy(out=o, in_=i_), [Bp], [B_dst])

        def ln_gen(items):
            for (src, B_src, gcol, bcol, dst, B_dst, k, pool_b, post) in items:
                sm, B_sm = LNSETS[k]
                st, mv, ve, lnv = sm["st"], sm["mv"], sm["ve"], sm["lnv"]
                for hh in range(2):
                    V(lambda e: e.bn_stats(out=st[:, hh * 6:(hh + 1) * 6], in_=src[:, hh * 512:(hh + 1) * 512]),
                      [B_src], [B_sm["st"]])
                V(lambda e: e.bn_aggr(out=mv[:], in_=st[:]), [B_sm["st"]], [B_sm["mv"]])
                V(lambda e: e.tensor_scalar(out=ve[:], in0=mv[:, 1:2], scalar1=LN_EPS, scalar2=None, op0=ALU.add),
                  [B_sm["mv"]], [B_sm["ve"]])
                V(lambda e: e.tensor_scalar(out=lnv[:], in0=mv[:, 0:1], scalar1=-1.0, scalar2=None, op0=ALU.mult),
                  [B_sm["mv"]], [B_sm["lnv"]])
            yield
            for (src, B_src, gcol, bcol, dst, B_dst, k, pool_b, post) in items:
                sm, B_sm = LNSETS[k]
                ve, lnv, rstd, nmr = sm["ve"], sm["lnv"], sm["rstd"], sm["nmr"]
                A(lambda e: e.activation(out=ve[:], in_=ve[:], func=AF.Ln), [B_sm["ve"]], [B_sm["ve"]])
                A(lambda e: e.activation(out=rstd[:], in_=ve[:], func=AF.Exp, scale=-0.5), [B_sm["ve"]], [B_sm["rstd"]])
                A(lambda e: e.activation(out=nmr[:], in_=lnv[:], func=AF.Identity, scale=rstd[:]),
                  [B_sm["lnv"], B_sm["rstd"]], [B_sm["nmr"]])
                A(lambda e: e.activation(out=dst[:], in_=src[:], func=AF.Identity, bias=nmr[:], scale=rstd[:]),
                  [B_src, B_sm["nmr"], B_sm["rstd"]], [B_dst])
            yield
            for (src, B_src, gcol, bcol, dst, B_dst, k, pool_b, post) in items:
                V(lambda e: e.tensor_tensor(out=dst[:], in0=dst[:], in1=cfc(gcol, 1024), op=ALU.mult), [B_dst] + Cr, [B_dst])
                (G if pool_b else V)(lambda e: e.tensor_tensor(out=dst[:], in0=dst[:], in1=cfc(bcol, 1024), op=ALU.add),
                                     [B_dst] + Cr, [B_dst])
                if post is not None:
                    post()
            yield

        ckpt(1)
        first = True
        first_flag = [True]
        E3 = ("act", "dve", "pool")
        pending = []

        deferred = []

        def run_deferred(n):
            for _ in range(n):
                if deferred:
                    try:
                        next(deferred[0])
                    except StopIteration:
                        deferred.pop(0)

        def drain(n):
            for _ in range(n):
                if pending:
                    convert_piece(pending.pop(0), E3)

        def drain_until(pi_end):
            while pending and pending[0] < pi_end:
                convert_piece(pending.pop(0), E3)

        for ti, tile in enumerate(tiles):
          try:
            nb = len(tile)
            T = nb * 128
            is_s = tile[0] == NP
            if first:
                x_loads(tile)
                for b in range(nb):
                    transposes_to(xin[b], B_xin[b], b, B_xT[b])
            Bx_all = [B_xT[b] for b in range(nb)]

            if is_s:
                pt, Bp = ps()
                tk.dma("sp", ck_sb, ck_d[:, :], [], [B_ck], B_ck)
                tk.dma("sp", cv_sb, cv_d[:, :], [], [B_cv], B_cv, chain=True)
                PE([tp(pt[:, 0:128], ck_sb, ident_f)], [B_ck] + Cr, [Bp])
                V(lambda e: e.tensor_copy(out=kT[:, 0:128], in_=pt[:, 0:128]), [Bp], [B_kT])
                V(lambda e: e.tensor_copy(out=Vt[:, 0, :], in_=cv_sb), [B_cv], [B_Vt])
            elif ti > 0:
                pnb = len(tiles[ti - 1])
                V(lambda e: e.tensor_copy(out=kT[:, 0:128], in_=kT[:, pnb * 128:(pnb + 1) * 128]), [B_kT], [B_kT])
                V(lambda e: e.tensor_copy(out=Vt[:, 0, :], in_=Vt[:, pnb, :]), [B_Vt], [B_Vt])

            ev_ctr = [0]

            def evac(out, in_, r, w):
                ev_ctr[0] += 1
                if ev_ctr[0] % 2:
                    A(lambda e: e.copy(out=out, in_=in_), r, w)
                else:
                    V(lambda e: e.tensor_copy(out=out, in_=in_), r, w)

            cs_blk = None
            for b, gb in enumerate(tile):
                if gb == NP - 1 or gb == NP:
                    cs_blk = b
            need_kv = cs_blk is not None

            def fm_chunk(slot, Bs, cc, out_ap, B_out):
                pt, Bp = ps()
                PE([mm(pt[:, 0:T], slot[:, kc, cc * 128:(cc + 1) * 128], xT[:, kc, 0:T], kc == 0, kc == 7)
                    for kc in range(8)], [Bs] + Bx_all, [Bp])
                evac(out_ap, pt[:, 0:T], [Bp], B_out)

            def tm_block(slot, Bs, b, ncols):
                pt, Bp = ps()
                PE([mm(pt[:, 0:ncols], xT[:, kc, b * 128:(b + 1) * 128], slot[:, kc, 0:ncols], kc == 0, kc == 7)
                    for kc in range(8)], [Bs, B_xT[b]], [Bp])
                return pt, Bp

            slot, Bs = get_piece(P_IN["kv"])
            for b in range(nb):
                pt, Bp = tm_block(slot, Bs, b, 256)
                if "noV" not in _os.environ.get("KVAR", ""):
                    V(lambda e: e.tensor_copy(out=Vt[:, b + 1, :], in_=pt[:, 128:256]), [Bp], [B_Vt])
                if need_kv and b == cs_blk and "noA" not in _os.environ.get("KVAR", ""):
                    A(lambda e: e.copy(out=ktok[:], in_=pt[:, 0:128]), [Bp], [B_ktok])
                    A(lambda e: e.copy(out=vtok[:], in_=pt[:, 128:256]), [Bp], [B_vtok])
                    if is_s:
                        tk.dma("pool", ks_d[0:64, :], ck_d[64:128, :], [], [], Buf("c2ck"))
                        tk.dma("pool", vs_d[0:64, :], cv_d[64:128, :], [], [], Buf("c2cv"))
                        tk.dma("pool", ks_d[64:128, :], ktok[0:64, :], [B_ktok], [], B_ktok)
                        tk.dma("pool", vs_d[64:128, :], vtok[0:64, :], [B_vtok], [], B_vtok)
                    elif _os.environ.get("KVAR", "") != "nodma":
                        tk.dma("pool", kp_d[:, :], ktok[:], [B_ktok], [], B_ktok)
                        tk.dma("pool", vp_d[:, :], vtok[:], [B_vtok], [], B_vtok)
            slot, Bs = get_piece(P_IN["dt"])
            for b in range(nb):
                pt, Bp = tm_block(slot, Bs, b, 8)
                V(lambda e: e.tensor_tensor(out=dtpre[:, b, :], in0=pt[:, 0:8], in1=cfc(DTB, 8), op=ALU.add),
                  [Bp] + Cr, [B_dtpre])
            nb8 = nb * 8
            dp = dtpre[:, 0:nb, :]
            V(lambda e: e.scalar_tensor_tensor(out=dtw[:, 0:nb, :], in0=dp, scalar=-1.0, in1=dp, op0=ALU.mult, op1=ALU.max),
              [B_dtpre], [B_dtw])
            A(lambda e: e.activation(out=dtw[:, 0:nb, :], in_=dtw[:, 0:nb, :], func=AF.Exp, scale=-1.0), [B_dtw], [B_dtw])
            V(lambda e: e.tensor_scalar(out=dtw[:, 0:nb, :], in0=dtw[:, 0:nb, :], scalar1=1.0, scalar2=None, op0=ALU.add),
              [B_dtw], [B_dtw])
            A(lambda e: e.activation(out=dtw[:, 0:nb, :], in_=dtw[:, 0:nb, :], func=AF.Ln), [B_dtw], [B_dtw])
            V(lambda e: e.scalar_tensor_tensor(out=dtall[:, 0:nb, :], in0=dp, scalar=0.0, in1=dtw[:, 0:nb, :],
                                               op0=ALU.max, op1=ALU.add), [B_dtpre, B_dtw], [B_dtall])
            if is_s:
                V(lambda e: e.tensor_scalar(out=dtall[:, 0:nb, :], in0=dtall[:, 0:nb, :], scalar1=cf[:, TOKM:TOKM + 1],
                                            scalar2=None, op0=ALU.mult), [B_dtall] + Cr, [B_dtall])
            V(lambda e: e.tensor_tensor(out=dtAall[:, 0:nb, :], in0=dtall[:, 0:nb, :],
                                        in1=Ab[:].unsqueeze(1).broadcast_to([128, nb, 8]), op=ALU.mult),
              [B_dtall, B_Ab], [B_dtall])
            pc, Bpc = ps()
            dA2 = dtAall[:, 0:nb, :].rearrange("p b h -> p (b h)")
            PE([mm(pc[:, 0:nb8], tri_f, dA2, True, True), mm(pc[:, 32:32 + nb8], ones_f, dA2, True, True)],
               [B_dtall] + Cr, [Bpc])
            fl = lambda t_: t_[:, 0:nb, :].rearrange("p b h -> p (b h)")
            V(lambda e: e.tensor_copy(out=fl(dtw), in_=pc[:, 0:nb8]), [Bpc], [B_dtw])
            A(lambda e: e.activation(out=fl(ecumall), in_=pc[:, 0:nb8], func=AF.Exp), [Bpc], [B_dtall])
            A(lambda e: e.activation(out=fl(bdecall), in_=pc[:, 32:32 + nb8], func=AF.Exp), [Bpc], [B_dtall])
            V(lambda e: e.tensor_tensor(out=fl(dtw), in0=pc[:, 32:32 + nb8], in1=fl(dtw), op=ALU.subtract),
              [Bpc, B_dtw], [B_dtw])
            A(lambda e: e.activation(out=fl(toendall), in_=fl(dtw), func=AF.Exp), [B_dtw], [B_dtall])
            for zi in range(2):
                slot, Bs = get_piece(P_IN[f"z{zi}"])
                for b in range(nb):
                    pt, Bp = tm_block(slot, Bs, b, 256)
                    A(lambda e: e.activation(out=U[:, 20 + b, zi * 256:(zi + 1) * 256], in_=pt[:, 0:256], func=AF.Silu),
                      [Bp], BU(20 + b, b))
                if zi == 0:
                    run_deferred(1)
            ckpt(3.1)
            if first:
                convert_wout()
            rows07 = sum([BU(c) for c in range(8)], [])
            if is_s:
                V(lambda e: e.tensor_copy(out=U[:, 0:8, 0:3], in_=halo_s[:]), [B_halos], rows07)
            else:
                V(lambda e: e.tensor_copy(out=U[:, 0:8, 0:3], in_=halo[:]), [B_halo], rows07)

            def conv_chunk(c):
                acc, Ba = Fb[4 + c % 2], B_F[4 + c % 2]
                A(lambda e: e.activation(out=acc[:, 0:T], in_=U[:, c, 0:T], func=AF.Identity,
                                         bias=cf[:, CBI + c:CBI + c + 1], scale=cf[:, CW + c * 4:CW + c * 4 + 1]),
                  BU(c) + Cr, [Ba])
                for i in range(1, 4):
                    V(lambda e: e.scalar_tensor_tensor(out=acc[:, 0:T], in0=U[:, c, i:i + T],
                                                       scalar=cf[:, CW + c * 4 + i:CW + c * 4 + i + 1], in1=acc[:, 0:T],
                                                       op0=ALU.mult, op1=ALU.add), BU(c) + [Ba] + Cr, [Ba])
                conv_tail.append((c, acc, Ba))

            conv_tail = []

            def conv_silu():
                if conv_tail:
                    c, acc, Ba = conv_tail.pop(0)
                    A(lambda e: e.activation(out=U[:, 8 + c, 0:T], in_=acc[:, 0:T], func=AF.Silu), [Ba], BU(8 + c))

            for xi, nm in enumerate(("x0", "x1", "B", "C")):
                slot, Bs = get_piece(P_IN[nm])
                for cc in range(2):
                    c = xi * 2 + cc
                    fm_chunk(slot, Bs, cc, U[:, c, 3:3 + T], BU(c))
                    conv_silu()
                    conv_chunk(c)
                if xi < 2:
                    run_deferred(1)
                if cs_blk is not None:
                    pt, Bp = tm_block(slot, Bs, cs_blk, 256)
                    evac(H[:, xi * 256:(xi + 1) * 256], pt[:, 0:256], [Bp], [B_H])
            conv_silu()
            if not is_s:
                V(lambda e: e.tensor_copy(out=halo[:], in_=U[:, 0:8, T:T + 3]), rows07, [B_halo])
            ckpt(3.2)
            if cs_blk is not None:
                r0 = 61 if is_s else 125
                tk.dma("pool", (convs_d if is_s else convp_d)[:, :], H[r0:r0 + 3, :], [B_H], [], B_H)
            ckpt(3.3)
            for qi in range(2):
                slot, Bs = get_piece(P_IN[f"q{qi}"])
                for jj in range(2):
                    j = qi * 2 + jj
                    fm_chunk(slot, Bs, jj, U[:, 16 + j, 0:T], BU(16 + j))
            slot, Bs = get_piece(P_IN["k"])
            fm_chunk(slot, Bs, 0, kT[:, 128:128 + T], [B_kT])
            ckpt(3.4)

            run_deferred(99)
            ckpt(4)
            if is_s:
                pt, Bp = ps()
                tk.dma("sp", sinit[:], sinit_d[:, :], [], [B_sinit], B_sinit)
                PE([tp(pt[:, c * 128:(c + 1) * 128], sinit[:, c * 128:(c + 1) * 128], ident_f) for c in range(4)],
                   [B_sinit] + Cr, [Bp])
                V(lambda e: e.tensor_copy(out=S[:], in_=pt[:]), [Bp], [B_S])
                A(lambda e: e.copy(out=Sb_[:], in_=pt[:]), [Bp], [B_Sb])

            emitted = set()

            def ssd_front(b, gb):
                t0 = b * 128
                ts = slice(t0, t0 + 128)
                par = b % 2
                (L, B_L, decay, B_decay, MT, B_MT, xdt, B_xdt, xdte, B_xdte, xsD, B_xsD, Btok, B_Btok,
                 cbm, B_cbm, gn, B_gn, FT, B_FT, FY, B_FY) = SSDSET[par]
                sm, B_sm = SMSETS[par]
                dt_ = dtall[:, b, :]
                dtA = dtAall[:, b, :]
                ecum = ecumall[:, b, :]
                toend = toendall[:, b, :]
                bdec = bdecall[:, b, :]
                B_dtT = [B_dtall]
                while b >= 2 and ('S', b - 2) not in emitted:
                    yield
                pxs, Bpxs = ps()
                pxs_b = pxs[:].bitcast(BF16)
                PE([tp(pxs_b[:, c * 128:(c + 1) * 128], U[:, 8 + c, ts], ident_b) for c in range(6)],
                   sum([BU(8 + c, b) for c in range(6)], []) + Cbr, [Bpxs])
                xs3 = pxs_b[:, 0:512].rearrange("p (h d) -> p h d", h=8)
                V(lambda e: e.tensor_tensor(out=xdt[:].rearrange("p (h d) -> p h d", h=8), in0=xs3,
                                            in1=dt_.unsqueeze(2).broadcast_to([128, 8, 64]), op=ALU.mult),
                  [Bpxs] + B_dtT, [B_xdt])
                A(lambda e: e.copy(out=Btok[:], in_=pxs_b[:, 512:768]), [Bpxs], [B_Btok])
                yield
                V(lambda e: e.tensor_tensor(out=xdte[:].rearrange("p (h d) -> p h d", h=8),
                                            in0=xdt[:].rearrange("p (h d) -> p h d", h=8),
                                            in1=toend.unsqueeze(2).broadcast_to([128, 8, 64]), op=ALU.mult),
                  [B_xdt] + B_dtT, [B_xdte])
                use_r = True
                V(lambda e: e.tensor_tensor(out=L[:, 0:4, :].bitcast(F32R),
                                            in0=mst_f.unsqueeze(1).broadcast_to([128, 4, 128]),
                                            in1=dtA[:, 0:4].unsqueeze(2).broadcast_to([128, 4, 128]), op=ALU.mult),
                  B_dtT + Cr, [B_L])
                B_La = LA.setdefault(id(B_L), Buf("La"))
                for h_ in range(4, 8):
                    A(lambda e: e.activation(out=L[:, h_, :].bitcast(F32R), in_=mst_f, func=AF.Identity,
                                             scale=dtA[:, h_:h_ + 1]), B_dtT + Cr, [B_La])
                yield
                for hh in range(2):
                    pg, Bpg = ps()
                    PE([mm(pg[:, k * 128:(k + 1) * 128],
                           (L[:, hh * 4 + k, :].bitcast(F32R) if use_r else L[:, hh * 4 + k, :]),
                           (tri_r.bitcast(F32R) if use_r else tri_f), True, True)
                        for k in range(4)], [B_L if hh == 0 else B_La, B_trir] + Cr, [Bpg])
                    A(lambda e: e.activation(out=decay[:, hh * 4:(hh + 1) * 4, :].rearrange("p a b -> p (a b)"),
                                             in_=pg[:], func=AF.Exp), [Bpg], [B_decay])
                    yield
                pcb, Bpcb = ps()
                PE([mm(pcb[:, g * 128:(g + 1) * 128], U[:, 12 + g, ts], U[:, 14 + g, ts], True, True) for g in range(2)],
                   BU(12, b) + BU(13, b) + BU(14, b) + BU(15, b), [Bpcb])
                V(lambda e: e.tensor_tensor(out=cbm[:], in0=pcb[:, 0:256].rearrange("p (g l) -> p g l", g=2),
                                            in1=cau_f.unsqueeze(1).broadcast_to([128, 2, 128]), op=ALU.mult),
                  [Bpcb] + Cr, [B_cbm])
                yield
                for g in range(2):
                    V(lambda e: e.tensor_tensor(out=MT[:, g * 4:(g + 1) * 4, :], in0=decay[:, g * 4:(g + 1) * 4, :],
                                                in1=cbm[:, g:g + 1, :].broadcast_to([128, 4, 128]), op=ALU.mult),
                      [B_decay, B_cbm], [B_MT])
                yield
                emitted.add(('front', b))
                yield

            def ssd_back(b, gb):
                t0 = b * 128
                ts = slice(t0, t0 + 128)
                par = b % 2
                (L, B_L, decay, B_decay, MT, B_MT, xdt, B_xdt, xdte, B_xdte, xsD, B_xsD, Btok, B_Btok,
                 cbm, B_cbm, gn, B_gn, FT, B_FT, FY, B_FY) = SSDSET[par]
                sm, B_sm = SMSETS[par]
                dt_ = dtall[:, b, :]
                dtA = dtAall[:, b, :]
                ecum = ecumall[:, b, :]
                toend = toendall[:, b, :]
                bdec = bdecall[:, b, :]
                B_dtT = [B_dtall]
                while ('front', b) not in emitted:
                    yield
                while b >= 1 and ('S', b - 1) not in emitted:
                    yield
                py, Bpy = ps()
                fns = [mm(py[:, c_ * 128:(c_ + 1) * 128], U[:, 8 + c_, ts], diagD[:, c_, :], c_ == 0, False) for c_ in range(4)]
                for h in range(8):
                    fns.append(mm(py[:, h * 64:(h + 1) * 64], MT[:, h, :], xdt[:, h * 64:(h + 1) * 64], False, h == 7))
                PE(fns, [B_diagD, B_MT, B_xdt] + sum([BU(8 + c_, b) for c_ in range(4)], []), [Bpy])
                pz, Bpz = ps()
                PE([mm(pz[:, g * 256:(g + 1) * 256], U[:, 14 + g, ts], Sb_[:, g * 256:(g + 1) * 256], True, True)
                    for g in range(2)], BU(14, b) + BU(15, b) + [B_Sb], [Bpz])
                t1, yv, gvv = FT, FY, FT
                V(lambda e: e.tensor_tensor(out=t1[:].rearrange("p (h d) -> p h d", h=8),
                                            in0=pz[:].rearrange("p (h d) -> p h d", h=8),
                                            in1=ecum.unsqueeze(2).broadcast_to([128, 8, 64]), op=ALU.mult),
                  [Bpz] + B_dtT, [B_FT])
                V(lambda e: e.tensor_tensor(out=yv[:], in0=py[:], in1=t1[:], op=ALU.add), [Bpy, B_FT], [B_FY])
                yield
                pn, Bpn = ps()
                PE([mm(pn[:, g * 256:(g + 1) * 256], Btok[:, g * 128:(g + 1) * 128], xdte[:, g * 256:(g + 1) * 256],
                       True, True) for g in range(2)], [B_Btok, B_xdte], [Bpn])
                V(lambda e: e.tensor_tensor(out=S[:].rearrange("p (h d) -> p h d", h=8),
                                            in0=S[:].rearrange("p (h d) -> p h d", h=8),
                                            in1=bdec.unsqueeze(2).broadcast_to([128, 8, 64]), op=ALU.mult),
                  [B_S] + B_dtT, [B_S])
                V(lambda e: e.tensor_tensor(out=S[:], in0=pn[:], in1=S[:], op=ALU.add), [Bpn, B_S], [B_S])
                A(lambda e: e.copy(out=Sb_[:], in_=S[:]), [B_S], [B_Sb])
                emitted.add(('S', b))
                if gb == NP - 1 or gb == NP:
                    pt, Bp = ps()
                    PE([tp(pt[:, c * 128:(c + 1) * 128], S[:, c * 128:(c + 1) * 128], ident_f) for c in range(4)],
                       [B_S] + Cr, [Bp])
                    V(lambda e: e.tensor_copy(out=sout[:], in_=pt[:]), [Bp], [B_sout])
                    tk.dma("pool", (ssms_d if is_s else ssmp_d)[:, :], sout[:], [B_sout], [], B_sout)
                yield
                V(lambda e: e.tensor_tensor(out=gvv[:], in0=yv[:], in1=U[:, 20 + b, 0:512], op=ALU.mult),
                  [B_FY] + BU(20 + b, b), [B_FT])
                st2, mv2, ms, lms, rs2 = sm["st2"], sm["mv2"], sm["ms"], sm["lms"], sm["rs2"]
                for g in range(2):
                    V(lambda e: e.bn_stats(out=st2[:, g * 6:(g + 1) * 6], in_=gvv[:, g * 256:(g + 1) * 256]),
                      [B_FT], [B_sm["st2"]])
                    V(lambda e: e.bn_aggr(out=mv2[:, g * 2:(g + 1) * 2], in_=st2[:, g * 6:(g + 1) * 6]),
                      [B_sm["st2"]], [B_sm["mv2"]])
                    V(lambda e: e.scalar_tensor_tensor(out=ms[:, g:g + 1], in0=mv2[:, g * 2:g * 2 + 1],
                                                       scalar=mv2[:, g * 2:g * 2 + 1], in1=mv2[:, g * 2 + 1:g * 2 + 2],
                                                       op0=ALU.mult, op1=ALU.add), [B_sm["mv2"]], [B_sm["ms"]])
                V(lambda e: e.tensor_scalar(out=ms[:], in0=ms[:], scalar1=RMS_EPS, scalar2=None, op0=ALU.add),
                  [B_sm["ms"]], [B_sm["ms"]])
                A(lambda e: e.activation(out=lms[:], in_=ms[:], func=AF.Ln), [B_sm["ms"]], [B_sm["lms"]])
                A(lambda e: e.activation(out=rs2[:], in_=lms[:], func=AF.Exp, scale=-0.5), [B_sm["lms"]], [B_sm["rs2"]])
                yield
                for g in range(2):
                    V(lambda e: e.scalar_tensor_tensor(out=gn[:, g * 256:(g + 1) * 256], in0=gvv[:, g * 256:(g + 1) * 256],
                                                       scalar=rs2[:, g:g + 1], in1=cf[:, WN + g * 256:WN + (g + 1) * 256],
                                                       op0=ALU.mult, op1=ALU.mult), [B_FT, B_sm["rs2"]] + Cr, [B_gn])
                yield
                while b >= 2 and ('outmm', b - 2) not in emitted:
                    yield
                pgt, Bpgt = ps()
                pgt_b = pgt[:].bitcast(BF16)
                PE([tp(pgt_b[:, c * 128:(c + 1) * 128], gn[:, c * 128:(c + 1) * 128], ident_b) for c in range(4)],
                   [B_gn] + Cbr, [Bpgt])
                A(lambda e: e.copy(out=mixS[b % 2][:], in_=pgt_b[:, 0:512].rearrange("p (c t) -> p c t", c=4)),
                  [Bpgt], [B_mixS[b % 2]])

                emitted.add(('ssd', b))
                yield

            def att_chain(b, gb):
                t0 = b * 128
                ts = slice(t0, t0 + 128)
                kbs = [1] if gb == 0 else [0, 1]
                for g in range(2):
                    gs = slice(g * 64, (g + 1) * 64)
                    for kb in kbs:
                        gkb = g * 2 + kb
                        kc0 = t0 + kb * 128
                        psc, Bpsc = ps()
                        PE([mm(psc[:], kT[gs, kc0:kc0 + 128], U[gs, 16:20, ts], True, False),
                            mm(psc[:], ident_b, cb[:, BIAS + (gkb * 2) * 512:BIAS + (gkb * 2 + 1) * 512], False, False),
                            mm(psc[:], ident_b, cb[:, BIAS + (gkb * 2 + 1) * 512:BIAS + (gkb * 2 + 2) * 512], False, True)],
                           [B_kT] + BU(16, b) + BU(17, b) + BU(18, b) + BU(19, b) + Cbr, [Bpsc])
                        A(lambda e: e.activation(out=PT[gkb][:], in_=psc[:], func=AF.Exp, scale=0.125),
                          [Bpsc], [B_PT[gkb]])
                        yield
                po, Bpo = ps()
                pd, Bpd = ps()
                fo, fd = [], []
                for g in range(2):
                    gs = slice(g * 64, (g + 1) * 64)
                    for ki, kb in enumerate(kbs):
                        fo.append(mm(po[gs, :], Vt[:, b + kb, gs], PT[g * 2 + kb][:], ki == 0, ki == len(kbs) - 1))
                        fd.append(mm(pd[gs, :], ones_b[:, 0:64], PT[g * 2 + kb][:], ki == 0, False))
                    fd.append(mm(pd[gs, :], ones_b[:, 0:64], sinkrow[:, g * 512:(g + 1) * 512], False, True))
                PE(fo, [B_Vt] + B_PT, [Bpo])
                PE(fd, B_PT + [B_sink] + Cbr, [Bpd])
                lnd, rcp = Fb[0], Fb[1]
                A(lambda e: e.activation(out=lnd[:], in_=pd[:], func=AF.Ln), [Bpd], [B_F[0]])
                A(lambda e: e.activation(out=rcp[:], in_=lnd[:], func=AF.Exp, scale=-1.0), [B_F[0]], [B_F[1]])
                V(lambda e: e.tensor_tensor(out=attT[:, :, ts], in0=po[:].rearrange("p (r q) -> p r q", r=4),
                                            in1=rcp[:].rearrange("p (r q) -> p r q", r=4), op=ALU.mult),
                  [Bpo, B_F[1]], [B_attT[b]])

                emitted.add(('att', b))
                yield

            def out_chain(b, gb):
                while ('ssd', b) not in emitted or ('att', b) not in emitted:
                    yield
                t0 = b * 128
                ts = slice(t0, t0 + 128)
                Hx, B_Hx = HS[b % 2]
                for dh in range(2):
                    pm, Bpm = ps()
                    PE([mm(pm[:], (mixS[b % 2][:, kc, :] if kc < 4 else attT[:, kc - 4, ts]),
                           wout[:, kc, dh * 512:(dh + 1) * 512], kc == 0, kc == 7)
                        for kc in range(8)], [B_mixS[b % 2], B_attT[b], B_wout], [Bpm])
                    V(lambda e: e.scalar_tensor_tensor(out=Hx[:, dh * 512:(dh + 1) * 512],
                                                       in0=xin[b][:, dh * 512:(dh + 1) * 512], scalar=ALPHA, in1=pm[:],
                                                       op0=ALU.mult, op1=ALU.add), [B_xin[b], Bpm], [B_Hx])
                    yield
                emitted.add(('outmm', b))
                yield from ln_gen([(Hx, B_Hx, L1G, L1B, hb[b], B_hb[b], b, False, None)])
                transposes_to(hb[b], B_hb[b], b, B_xT[b], act_only=True)
                emitted.add(('out', b))
                yield

            def p4_gen(c0, c1, Bxs, bsel, use_load=False):
                n_ = c1 - c0
                for i in range(11):
                    gslot, Bg = get_piece(P_G0 + 2 * i)
                    uslot, Bu = get_piece(P_G0 + 2 * i + 1)
                    for jj in range(2):
                        j = i * 2 + jj
                        pgg, Bpgg = ps()
                        puu, Bpuu = ps()
                        PE([mm(pgg[:, 0:n_], gslot[:, kc, jj * 128:(jj + 1) * 128], xT[:, kc, c0:c1], kc == 0, kc == 7)
                            for kc in range(8)], [Bg] + Bxs, [Bpgg])
                        PE([mm(puu[:, 0:n_], uslot[:, kc, jj * 128:(jj + 1) * 128], xT[:, kc, c0:c1], kc == 0, kc == 7)
                            for kc in range(8)], [Bu] + Bxs, [Bpuu])
                        sg, Bsg = Fb[j % 2], B_F[j % 2]
                        A(lambda e: e.activation(out=sg[:, 0:n_], in_=pgg[:, 0:n_], func=AF.Silu), [Bpgg], [Bsg])
                        V(lambda e: e.tensor_tensor(out=U[:, j, c0:c1], in0=sg[:, 0:n_], in1=puu[:, 0:n_], op=ALU.mult),
                          [Bsg, Bpuu], BU(j, bsel))
                        yield

            def p4h0_chain():
                while ('out', 0) not in emitted or ('out', 1) not in emitted:
                    yield
                yield from p4_gen(0, 256, [B_xT[0], B_xT[1]], 0)

            interleave_p4 = False

            def seq(gens):
                for g_ in gens:
                    yield from g_

            chains = [seq([ssd_front(b, gb) for b, gb in enumerate(tile) if b % 2 == 0]),
                      seq([ssd_front(b, gb) for b, gb in enumerate(tile) if b % 2 == 1]),
                      seq([ssd_back(b, gb) for b, gb in enumerate(tile) if b % 2 == 0]),
                      seq([ssd_back(b, gb) for b, gb in enumerate(tile) if b % 2 == 1]),
                      seq([att_chain(b, gb) for b, gb in enumerate(tile)]),
                      seq([out_chain(b, gb) for b, gb in enumerate(tile) if b % 2 == 0]),
                      seq([out_chain(b, gb) for b, gb in enumerate(tile) if b % 2 == 1])]
            if interleave_p4:
                chains.append(p4h0_chain())
            steps_per_round = [1] * 8
            while chains:
                for ci_, c_ in enumerate(list(chains)):
                    try:
                        for _ in range(steps_per_round[ci_] if len(chains) >= 3 else 1):
                            next(c_)
                    except StopIteration:
                        chains.remove(c_)
            ckpt(6)

            drain_until(P_D0)

            if not interleave_p4:
                for _ in p4_gen(0, T, Bx_all, None):
                    pass
            else:
                for _ in p4_gen(256, 512, [B_xT[2], B_xT[3]], 2, use_load=True):
                    pass

            ckpt(7)
            drain_until(NPIECE)

            if ti + 1 < len(tiles):
                nxt = tiles[ti + 1]
                x_loads(nxt)
                for b in range(len(nxt)):
                    transposes_to(xin[b], B_xin[b], b, B_xT[b])

            for dq in range(4):
                pbs = [ps() for _ in range(nb)]
                for fp in range(3):
                    dslot, Bd = get_piece(P_D0 + dq * 3 + fp)
                    nf = 8 if fp < 2 else 6
                    for b in range(nb):
                        pb, Bpb = pbs[b]
                        PE([mm(pb[:, 0:256], U[:, fp * 8 + k, b * 128:(b + 1) * 128], dslot[:, k, :],
                               fp == 0 and k == 0, fp == 2 and k == nf - 1) for k in range(nf)],
                           [Bd] + sum([BU(fp * 8 + k, b) for k in range(nf)], []), [Bpb])
                for b in range(nb):
                    pb, Bpb = pbs[b]
                    V(lambda e: e.scalar_tensor_tensor(out=hb[b][:, dq * 256:(dq + 1) * 256],
                                                       in0=hb[b][:, dq * 256:(dq + 1) * 256], scalar=ALPHA,
                                                       in1=pb[:, 0:256], op0=ALU.mult, op1=ALU.add),
                      [B_hb[b], Bpb], [B_hb[b]])
            def mk_store(b, gb, is_s_):
                def post():
                    if is_s_:
                        tk.dma("pool", ys_d[:, :], hb[b][0:64, :], [B_hb[b]], [], B_hb[b])
                    else:
                        tk.dma("pool", y_d[gb * 128:(gb + 1) * 128, :], hb[b][:], [B_hb[b]], [], B_hb[b])
                return post

            deferred.append(ln_gen([(hb[b], B_hb[b], L2G, L2B, hb[b], B_hb[b], b, False, mk_store(b, gb, is_s))
                                    for b, gb in enumerate(tile)]))
            run_deferred(3)
            first = False
            first_flag[0] = False
          except _Stop:
            break

        run_deferred(99)
        tk.finish()
    return nc


def _consts():
    cf = np.zeros((128, NCF), np.float32)
    i = np.arange(128)
    cf[:, IDF:IDF + 128] = np.eye(128)
    cf[:, TRI:TRI + 128] = (i[:, None] <= i[None, :])
    cf[:, MST:MST + 128] = (i[None, :] < i[:, None])
    cf[:, CAU:CAU + 128] = (i[:, None] <= i[None, :])
    cf[:, ONE:ONE + 128] = 1.0
    cf[:, TOKM] = (i < 64)
    cb = np.zeros((128, NCB), np.float32)
    cb[:, IDB:IDB + 128] = np.eye(128)
    cb[:, ONB:ONB + 128] = 1.0
    lo_part = np.zeros((128, NCB), np.float32)
    s = i[:, None].astype(np.float64)
    q = i[None, :].astype(np.float64)
    for g in range(2):
        for kb in range(2):
            gkb = g * 2 + kb
            tile_ = np.zeros((128, 512), np.float64)
            for r in range(4):
                m = 2.0 ** -(g * 4 + r + 1)
                if kb == 0:
                    dist = 128.0 + q - s
                    bad = (q >= 64) & (s < 64)
                else:
                    dist = np.abs(q - s)
                    bad = (q < 64) & (s >= 64)
                v = -8.0 * m * dist
                v = np.where(bad, -32768.0, v)
                tile_[:, r * 128:(r + 1) * 128] = v
            hi = tile_.astype(np.float32).astype(ml_dtypes.bfloat16)
            lo = (tile_ - hi.astype(np.float64)).astype(np.float32).astype(ml_dtypes.bfloat16)
            o = BIAS + (gkb * 2) * 512
            cb[:, o:o + 512] = hi.astype(np.float32)
            cb[:, o + 512:o + 1024] = lo.astype(np.float32)
    return cf, cb.astype(ml_dtypes.bfloat16)


_NC_CACHE = {}
_SIM_HOOK = None


def _run(NP, x_prompt, x_sample, state_conv, state_ssm, cache_k, cache_v, w_in, conv_w, conv_b, dt_bias,
         a_log, d_skip, ssm_norm_w, attn_sinks, w_out, ln1_g, ln1_b, w_gate, w_up, w_down, ln2_g, ln2_b):
    f = np.float32
    ncores = x_prompt.shape[0]
    cf, cb = _consts()
    cf[:, CW:CW + 32] = np.asarray(conv_w[0], f).reshape(4, 8, 128).transpose(2, 1, 0).reshape(128, 32)
    cf[:, CBI:CBI + 8] = np.asarray(conv_b[0], f).reshape(8, 128).T
    cf[:, DTB:DTB + 8] = np.broadcast_to(np.asarray(dt_bias[0], f), (128, 8))
    cf[:, ALOG:ALOG + 8] = np.broadcast_to(np.asarray(a_log[0], f), (128, 8))
    cf[:, DFULL:DFULL + 512] = np.broadcast_to(np.repeat(np.asarray(d_skip[0], f), 64), (128, 512))
    cf[:, DCOL:DCOL + 4] = np.repeat(np.asarray(d_skip[0], f), 64).reshape(4, 128).T
    cf[:, WN:WN + 512] = np.broadcast_to(np.asarray(ssm_norm_w[0], f), (128, 512))
    cf[:, L1G:L1G + 1024] = np.broadcast_to(np.asarray(ln1_g[0], f), (128, 1024))
    cf[:, L1B:L1B + 1024] = np.broadcast_to(np.asarray(ln1_b[0], f), (128, 1024))
    cf[:, L2G:L2G + 1024] = np.broadcast_to(np.asarray(ln2_g[0], f), (128, 1024))
    cf[:, L2B:L2B + 1024] = np.broadcast_to(np.asarray(ln2_b[0], f), (128, 1024))
    sink = np.ascontiguousarray(np.broadcast_to(np.repeat(np.asarray(attn_sinks[0], f), 128), (128, 1024)))
    if NP not in _NC_CACHE:
        _NC_CACHE[NP] = build(NP)
    nc = _NC_CACHE[NP]
    shared = {"w_in": np.ascontiguousarray(w_in[0], f), "w_out": np.ascontiguousarray(w_out[0], f),
              "w_gate": np.ascontiguousarray(w_gate[0], f), "w_up": np.ascontiguousarray(w_up[0], f),
              "w_down": np.ascontiguousarray(w_down[0], f), "cf": cf, "cb": cb, "sink": sink}
    in_maps = []
    for c in range(ncores):
        m = dict(shared)
        m["x"] = np.ascontiguousarray(x_prompt[c], f)
        m["xs"] = np.ascontiguousarray(x_sample[c], f)
        m["halo_s"] = np.ascontiguousarray(np.asarray(state_conv[0, c], f).reshape(3, 8, 128).transpose(2, 1, 0)).reshape(128, 24)
        m["sinit"] = np.ascontiguousarray(np.asarray(state_ssm[0, c], f).reshape(4, 128, 128).transpose(1, 0, 2)).reshape(128, 512)
        m["ck"] = np.ascontiguousarray(np.asarray(cache_k[0, c], f).reshape(128, 128))
        m["cv"] = np.ascontiguousarray(np.asarray(cache_v[0, c], f).reshape(128, 128))
        in_maps.append(m)
    if _SIM_HOOK is not None:
        R = _SIM_HOOK(nc, in_maps)
    else:
        res = run_bass_kernel_spmd(nc, in_maps, core_ids=list(range(ncores)))
        R = res.results

    def ssm(a):
        return np.asarray(a, f).reshape(128, 4, 128).transpose(1, 0, 2).reshape(8, 64, 128)

    y_p = np.stack([np.asarray(r["y"], f) for r in R])
    y_s = np.stack([np.asarray(r["ys"], f) for r in R])
    conv_p = np.stack([np.asarray(r["conv_p"], f) for r in R])[None]
    ssm_p = np.stack([ssm(r["ssm_p"]) for r in R])[None]
    k_p = np.stack([np.asarray(r["k_p"], f).reshape(128, 2, 64) for r in R])[None]
    v_p = np.stack([np.asarray(r["v_p"], f).reshape(128, 2, 64) for r in R])[None]
    conv_s = np.stack([np.asarray(r["conv_s"], f) for r in R])[None]
    ssm_s = np.stack([ssm(r["ssm_s"]) for r in R])[None]
    k_s = np.stack([np.asarray(r["k_s"], f).reshape(128, 2, 64) for r in R])[None]
    v_s = np.stack([np.asarray(r["v_s"], f).reshape(128, 2, 64) for r in R])[None]
    return (y_p, y_s, conv_p, ssm_p, k_p, v_p, conv_s, ssm_s, k_s, v_s)


def kernel(**inputs):
    inputs = {k: np.asarray(v) for k, v in inputs.items()}
    NP = inputs["x_prompt"].shape[1] // 128
    return _run(NP, **inputs)
```

```python
import numpy as np
import ml_dtypes
from contextlib import ExitStack
import concourse.bass as bass
import concourse.mybir as mybir
from concourse.bass_utils import run_bass_kernel_spmd

F32 = mybir.dt.float32
BF16 = mybir.dt.bfloat16
F32R = mybir.dt.float32r
AF = mybir.ActivationFunctionType
ALU = mybir.AluOpType

ALPHA = 2.0 ** 0.25
LN_EPS = 1e-5
RMS_EPS = 1e-5

IDF, TRI, MST, CAU, ONE, CW, CBI, DTB, ALOG, TOKM, DCOL = 0, 128, 256, 384, 512, 640, 672, 680, 688, 696, 697
DFULL, WN, L1G, L1B, L2G, L2B, NCF = 704, 1216, 1728, 2752, 3776, 4800, 5824
IDB, ONB, BIAS, NCB = 0, 128, 256, 4352


class Ev:
    __slots__ = ("sem", "val")

    def __init__(self, sem, val):
        self.sem = sem
        self.val = val


class Buf:
    def __init__(self, name, excl=False):
        self.name = name
        self.excl = excl
        self.w = None
        self.r = {}
        self.dsem = {}
        self.dcnt = {}


class TK:
    def __init__(self, nc, es):
        self.nc = nc
        self.es = es
        self.E = {"pe": nc.tensor, "act": nc.scalar, "dve": nc.vector, "pool": nc.gpsimd, "sp": nc.sync}
        self.sem = {k: es.enter_context(nc.semaphore("sem_" + k)) for k in ("pe", "act", "dve", "pool")}
        self.cnt = {k: 0 for k in self.sem}
        self.seen = {k: {} for k in self.E}
        self.dbufs = []

    def _wait(self, e, ev):
        if ev is None:
            return
        if e == "pe" and ev.sem is self.sem["pe"]:
            return
        k = id(ev.sem)
        if self.seen[e].get(k, 0) >= ev.val:
            return
        self.E[e].wait_ge(ev.sem, ev.val)
        self.seen[e][k] = ev.val

    def _deps(self, e, reads, writes):
        for b in reads:
            self._wait(e, b.w)
            if b.excl:
                for r in b.r.values():
                    self._wait(e, r)
        for b in writes:
            self._wait(e, b.w)
            for r in b.r.values():
                self._wait(e, r)

    def _rec(self, ev, reads, writes):
        for b in reads:
            b.r[id(ev.sem)] = ev
        for b in writes:
            b.w = ev
            b.r = {}

    def op(self, e, fn, reads, writes):
        self._deps(e, reads, writes)
        inst = fn(self.E[e])
        self.cnt[e] += 1
        inst.then_inc(self.sem[e], 1)
        self._rec(Ev(self.sem[e], self.cnt[e]), reads, writes)

    def group(self, e, fns, reads, writes):
        self._deps(e, reads, writes)
        inst = None
        for fn in fns:
            inst = fn(self.E[e])
        self.cnt[e] += 1
        inst.then_inc(self.sem[e], 1)
        self._rec(Ev(self.sem[e], self.cnt[e]), reads, writes)

    def dma(self, q, out, in_, reads, writes, sb, chain=False):
        if chain and sb.dsem.get(q) is not None:
            saved = [(b, b.w) for b in writes if b.w is not None and b.w.sem is sb.dsem[q]]
            for b, _ in saved:
                b.w = None
            self._deps(q, reads, writes)
            for b, w in saved:
                b.w = w
        else:
            self._deps(q, reads, writes)
        if q not in sb.dsem:
            sb.dsem[q] = self.es.enter_context(self.nc.semaphore("d_" + q + "_" + sb.name))
            sb.dcnt[q] = 0
            self.dbufs.append((sb, q))
        inst = self.E[q].dma_start(out=out, in_=in_)
        sb.dcnt[q] += 16
        inst.then_inc(sb.dsem[q], 16)
        self._rec(Ev(sb.dsem[q], sb.dcnt[q]), reads, writes)

    def finish(self):
        for e in ("pool", "sp"):
            for sb, q in self.dbufs:
                self._wait(e, Ev(sb.dsem[q], sb.dcnt[q]))


class _Stop(Exception):
    pass


def build(NP=32):
    import os
    KSTOP = float(os.environ.get("KSTOP", "99"))

    def ckpt(n):
        if n >= KSTOP:
            raise _Stop()

    nc = bass.Bass("TRN2", target_bir_lowering=False)
    TALL = NP * 128

    def din(name, shape, dt=F32):
        return nc.dram_tensor(name, shape, dt, kind="ExternalInput").ap()

    def dout(name, shape, dt=F32):
        return nc.dram_tensor(name, shape, dt, kind="ExternalOutput").ap()

    x_d = din("x", [TALL, 1024])
    xs_d = din("xs", [64, 1024])
    w_in_d = din("w_in", [1024, 2312])
    w_out_d = din("w_out", [1024, 1024])
    w_gate_d = din("w_gate", [1024, 2816])
    w_up_d = din("w_up", [1024, 2816])
    w_down_d = din("w_down", [2816, 1024])
    cf_d = din("cf", [128, NCF])
    cb_d = din("cb", [128, NCB], BF16)
    sink_d = din("sink", [128, 1024])
    halo_d = din("halo_s", [128, 24])
    sinit_d = din("sinit", [128, 512])
    ck_d = din("ck", [128, 128])
    cv_d = din("cv", [128, 128])
    y_d = dout("y", [TALL, 1024])
    ys_d = dout("ys", [64, 1024])
    convp_d = dout("conv_p", [3, 1024])
    ssmp_d = dout("ssm_p", [128, 512])
    kp_d = dout("k_p", [128, 128])
    vp_d = dout("v_p", [128, 128])
    convs_d = dout("conv_s", [3, 1024])
    ssms_d = dout("ssm_s", [128, 512])
    ks_d = dout("k_s", [128, 128])
    vs_d = dout("v_s", [128, 128])

    NPIECE = 11 + 22 + 12
    wscr = nc.dram_tensor("wscr", [NPIECE, 128, 2048], BF16, kind="Internal").ap()

    es = ExitStack()
    with es:
        import os as _os
        for _i in range(int(_os.environ.get("KDUMMY", "0"))):
            es.enter_context(nc.semaphore(f"dummy{_i}"))
        tk = TK(nc, es)

        def sb(name, shape, dt=F32):
            return es.enter_context(nc.sbuf_tensor("sb_" + name, shape, dt))

        cf = sb("cf", [128, NCF]); B_cf = Buf("cf")
        cb = sb("cb", [128, NCB], BF16); B_cb = Buf("cb")
        sinkrow = sb("sinkrow", [128, 1024], BF16); B_sink = Buf("sinkrow")
        wout = sb("wout", [128, 8, 1024], BF16); B_wout = Buf("wout")
        slots = [sb(f"slot{i}", [128, 8, 256], BF16) for i in range(3)]
        B_slot = [Buf(f"slot{i}") for i in range(3)]
        NSTG = 3
        stages = [sb(f"stage{i}", [128, 8, 128]) for i in range(NSTG)]
        B_stage = [Buf(f"stage{i}") for i in range(NSTG)]
        xin = [sb(f"xin{i}", [128, 1024]) for i in range(4)]
        B_xin = [Buf(f"xin{i}") for i in range(4)]
        hb = [sb(f"hb{i}", [128, 1024]) for i in range(4)]
        B_hb = [Buf(f"hb{i}") for i in range(4)]
        xT = sb("xT", [128, 8, 512], BF16)
        B_xT = [Buf(f"xT{i}") for i in range(4)]
        U = sb("U", [128, 24, 515], BF16)
        B_UH = [[Buf(f"U{i}a"), Buf(f"U{i}b")] for i in range(24)]

        def BU(r, b=None):
            return list(B_UH[r]) if b is None else [B_UH[r][b // 2]]
        kT = sb("kT", [128, 640], BF16); B_kT = Buf("kT")
        Vt = sb("Vt", [128, 5, 128], BF16); B_Vt = Buf("Vt")
        ktok = sb("ktok", [128, 128]); B_ktok = Buf("ktok")
        vtok = sb("vtok", [128, 128]); B_vtok = Buf("vtok")
        dtpre = sb("dtpre", [128, 4, 8]); B_dtpre = Buf("dtpre")
        halo = sb("halo", [128, 8, 3], BF16); B_halo = Buf("halo")
        halo_s = sb("halo_s", [128, 8, 3]); B_halos = Buf("halos")
        S = sb("S", [128, 512]); B_S = Buf("S")
        Sb_ = sb("Sb", [128, 512], BF16); B_Sb = Buf("Sb")
        Fb = [sb(f"F{i}", [128, 512]) for i in range(6)]
        B_F = [Buf(f"F{i}") for i in range(6)]
        H = sb("H", [128, 1024]); B_H = Buf("H")
        H2 = sb("H2", [128, 1024]); B_H2 = Buf("H2")
        HS = [(H, B_H), (H2, B_H2)]
        L = sb("L", [128, 8, 128]); B_L = Buf("L")
        decay = sb("decay", [128, 8, 128]); B_decay = Buf("decay")
        MT = sb("MT", [128, 8, 128], BF16); B_MT = Buf("MT")
        xdt = sb("xdt", [128, 512], BF16); B_xdt = Buf("xdt")
        xdte = sb("xdte", [128, 512], BF16); B_xdte = Buf("xdte")
        xsD = sb("xsD", [128, 512], BF16); B_xsD = Buf("xsD")
        Btok = sb("Btok", [128, 256], BF16); B_Btok = Buf("Btok")
        cbm = sb("cbm", [128, 2, 128]); B_cbm = Buf("cbm")
        gn = sb("gn", [128, 512], BF16); B_gn = Buf("gn")
        PT = [sb(f"PT{i}", [128, 512], BF16) for i in range(4)]
        B_PT = [Buf(f"PT{i}") for i in range(4)]
        mixS = [sb(f"mixS{i}", [128, 4, 128], BF16) for i in range(2)]
        B_mixS = [Buf(f"mixS{i}") for i in range(2)]
        attT = sb("attT", [128, 4, 512], BF16)
        B_attT = [Buf(f"attT{i}") for i in range(4)]
        Ab = sb("Ab", [128, 8]); B_Ab = Buf("Ab")
        sm = {}
        B_sm = {}
        SMW = {}
        for nm, w in (("dt", 8), ("ab", 8), ("e1", 8), ("l1", 8), ("dtA", 8), ("cum", 8), ("ecum", 8), ("dd", 8),
                      ("toend", 8), ("bdec", 8), ("st", 12), ("mv", 2), ("ve", 1), ("lnv", 1), ("rstd", 1),
                      ("nmr", 1), ("st2", 12), ("mv2", 4), ("ms", 2), ("lms", 2), ("rs2", 2)):
            sm[nm] = sb("sm_" + nm, [128, w])
            SMW[nm] = w
            B_sm[nm] = Buf("sm_" + nm)

        sout = Fb[0]; B_sout = B_F[0]
        sinit = Fb[0]; B_sinit = B_F[0]
        ck_sb = Fb[1][:, 0:128]; cv_sb = Fb[1][:, 128:256]; B_ck = B_F[1]; B_cv = B_F[1]
        tri_rt = sb("tri_r", [128, 128]); B_trir = Buf("tri_r")
        diagD = sb("diagD", [128, 4, 128], BF16); B_diagD = Buf("diagD")
        aident = sb("aident", [128, 128]); B_aident = Buf("aident")
        dtw = sb("dtw", [128, 4, 8]); B_dtw = Buf("dtw")
        dtall = sb("dtall", [128, 4, 8]); B_dtall = Buf("dtall")
        dtAall = sb("dtAall", [128, 4, 8])
        ecumall = sb("ecumall", [128, 4, 8])
        toendall = sb("toendall", [128, 4, 8])
        bdecall = sb("bdecall", [128, 4, 8])
        L1 = sb("L1", [128, 8, 128]); B_L1 = Buf("L1")
        MT1 = sb("MT1", [128, 8, 128], BF16); B_MT1 = Buf("MT1")
        xdt1 = sb("xdt1", [128, 512], BF16); B_xdt1 = Buf("xdt1")
        xdte1 = sb("xdte1", [128, 512], BF16); B_xdte1 = Buf("xdte1")
        xsD1 = sb("xsD1", [128, 512], BF16); B_xsD1 = Buf("xsD1")
        Btok1 = sb("Btok1", [128, 256], BF16); B_Btok1 = Buf("Btok1")
        cbm1 = sb("cbm1", [128, 2, 128]); B_cbm1 = Buf("cbm1")
        gn1 = sb("gn1", [128, 512], BF16); B_gn1 = Buf("gn1")
        sm1 = {}
        B_sm1 = {}
        for nm in list(sm.keys()):
            sm1[nm] = sb("sm1_" + nm, [128, SMW[nm]])
            B_sm1[nm] = Buf("sm1_" + nm)
        SSDSET = [
            (L, B_L, decay, B_decay, MT, B_MT, xdt, B_xdt, xdte, B_xdte, xsD, B_xsD, Btok, B_Btok,
             cbm, B_cbm, gn, B_gn, Fb[2], B_F[2], Fb[3], B_F[3]),
            (L1, B_L1, stages[1], B_stage[1], MT1, B_MT1, xdt1, B_xdt1, xdte1, B_xdte1, xsD1, B_xsD1,
             Btok1, B_Btok1, cbm1, B_cbm1, gn1, B_gn1, Fb[4], B_F[4], Fb[5], B_F[5]),
        ]
        SMSETS = [(sm, B_sm), (sm1, B_sm1)]
        LA = {}
        LNSETS = []
        for k_ in range(4):
            d_, B_d = {}, {}
            for nm in ("st", "mv", "ve", "lnv", "rstd", "nmr"):
                d_[nm] = sb(f"ln{k_}_{nm}", [128, SMW[nm]])
                B_d[nm] = Buf(f"ln{k_}_{nm}")
            LNSETS.append((d_, B_d))

        banks = [es.enter_context(nc.psum_tensor(f"bank{i}", [128, 512], F32)) for i in range(8)]
        B_bank = [Buf(f"bank{i}", excl=True) for i in range(8)]
        bank_ctr = [0]

        def ps():
            i = bank_ctr[0] % 8
            bank_ctr[0] += 1
            return banks[i], B_bank[i]

        def V(fn, r, w):
            tk.op("dve", fn, r, w)

        def A(fn, r, w):
            tk.op("act", fn, r, w)

        def G(fn, r, w):
            tk.op("pool", fn, r, w)

        def PE(fns, r, w):
            tk.group("pe", fns, r, w)

        def mm(out, lhsT, rhs, start, stop):
            return lambda e: e.matmul(out, lhsT, rhs, start=start, stop=stop)

        def tp(out, in_, ident):
            return lambda e: e.transpose(out, in_, ident)

        def cfc(off, n):
            return cf[:, off:off + n]

        ident_f = cfc(IDF, 128)
        tri_f = cfc(TRI, 128)
        mst_f = cfc(MST, 128)
        cau_f = tri_f
        tri_r = tri_rt[:]
        ones_f = cfc(ONE, 128)
        ident_b = cb[:, IDB:IDB + 128]
        ones_b = cb[:, ONB:ONB + 128]
        Cr = [B_cf]
        Cbr = [B_cb]

        tk.dma("sp", cf[:], cf_d[:, :], [], [B_cf], B_cf)
        tk.dma("sp", cb[:], cb_d[:, :], [], [B_cb], B_cb)
        tk.dma("sp", H[:], sink_d[:, :], [], [B_H], B_H)
        tk.dma("sp", halo_s[:].rearrange("p a b -> p (a b)"), halo_d[:, :], [], [B_halos], B_halos)
        A(lambda e: e.activation(out=Ab[:], in_=cfc(ALOG, 8), func=AF.Exp), Cr, [B_Ab])
        V(lambda e: e.tensor_scalar(out=Ab[:], in0=Ab[:], scalar1=-1.0, scalar2=None, op0=ALU.mult), [B_Ab], [B_Ab])
        MTf = MT[:].rearrange("p a b -> p (a b)")
        A(lambda e: e.activation(out=H[:], in_=H[:], func=AF.Exp), [B_H], [B_H])
        V(lambda e: e.tensor_copy(out=MTf, in_=H[:]), [B_H], [B_MT])
        V(lambda e: e.tensor_tensor(out=H[:], in0=H[:], in1=MTf, op=ALU.subtract), [B_H, B_MT], [B_H])
        V(lambda e: e.memset(sinkrow[:], 0.0), [], [B_sink])
        V(lambda e: e.tensor_copy(out=sinkrow[0:1, :], in_=MTf[0:1, :]), [B_MT], [B_sink])
        V(lambda e: e.tensor_copy(out=sinkrow[32:33, :], in_=H[32:33, :]), [B_H], [B_sink])
        V(lambda e: e.tensor_scalar(out=aident[:], in0=ident_f, scalar1=ALPHA, scalar2=None, op0=ALU.mult), Cr, [B_aident])
        V(lambda e: e.memset(S[:], 0.0), [], [B_S])
        V(lambda e: e.memset(Sb_[:], 0.0), [], [B_Sb])
        V(lambda e: e.memset(halo[:], 0.0), [], [B_halo])
        for c_ in range(4):
            A(lambda e: e.activation(out=diagD[:, c_, :], in_=ident_b, func=AF.Identity, scale=cf[:, DCOL + c_:DCOL + c_ + 1]),
              Cr + Cbr, [B_diagD])
        V(lambda e: e.tensor_scalar(out=tri_r.bitcast(F32R), in0=tri_f, scalar1=1.0, scalar2=None, op0=ALU.mult),
          Cr, [B_trir])

        wv = w_in_d.rearrange("(kc p) c -> p kc c", p=128)
        qv = w_in_d[:, 1544:2056].rearrange("(kc p) (g j d) -> p kc j g d", p=128, g=2, j=4, d=64)
        gv_ = w_gate_d.rearrange("(kc p) c -> p kc c", p=128)
        uv_ = w_up_d.rearrange("(kc p) c -> p kc c", p=128)
        dv_ = w_down_d.rearrange("(fc p) c -> p fc c", p=128)

        pieces = []

        def simple(name, view, c0, ncols):
            sts = []
            for o in range(0, ncols, 128):
                n = min(128, ncols - o)
                sts.append((o, n, 8, [(view[:, :, c0 + o:c0 + o + n], 0, n)]))
            pieces.append((name, sts))

        simple("z0", wv, 0, 256)
        simple("z1", wv, 256, 256)
        simple("x0", wv, 512, 256)
        simple("x1", wv, 768, 256)
        simple("B", wv, 1024, 256)
        simple("C", wv, 1280, 256)
        for qi in range(2):
            sts = []
            for jj in range(2):
                j = qi * 2 + jj
                sts.append((jj * 128, 128, 8, [(qv[:, :, j, 0, :], 0, 64), (qv[:, :, j, 1, :], 64, 128)]))
            pieces.append((f"q{qi}", sts))
        simple("k", wv, 2056, 128)
        simple("kv", wv, 2056, 256)
        simple("dt", wv, 1536, 8)
        P_IN = {n: i for i, (n, _) in enumerate(pieces)}
        for i in range(11):
            simple(f"g{i}", gv_, i * 256, 256)
            simple(f"u{i}", uv_, i * 256, 256)
        P_G0 = 11
        for dq in range(4):
            for fp in range(3):
                f0 = fp * 8
                nf = 8 if fp < 2 else 6
                sts = []
                for o in (0, 128):
                    sts.append((o, 128, nf, [(dv_[:, f0:f0 + nf, dq * 256 + o:dq * 256 + o + 128], 0, 128)]))
                pieces.append((f"d{dq}_{fp}", sts))
        P_D0 = 33
        assert len(pieces) == NPIECE
        B_scr = [Buf(f"scr{i}") for i in range(NPIECE)]
        wscr_v = [wscr[i].rearrange("p (k c) -> p k c", c=256) for i in range(NPIECE)]

        st_ctr = [0]
        slot_ctr = [0]

        def stage_load(srcs, nk):
            i = st_ctr[0] % NSTG
            st_ctr[0] += 1
            for n_, (src, lo, hi) in enumerate(srcs):
                tk.dma("sp", stages[i][:, 0:nk, lo:hi], src, [], [B_stage[i]], B_stage[i], chain=n_ > 0)
            return i

        def cast(eng, out, in_, r, w):
            if eng == "act":
                A(lambda e: e.copy(out=out, in_=in_), r, w)
            elif eng == "dve":
                V(lambda e: e.tensor_copy(out=out, in_=in_), r, w)
            else:
                G(lambda e: e.tensor_copy(out=out, in_=in_), r, w)

        def convert_piece(pi, engs):
            name, sts = pieces[pi]
            si = slot_ctr[0] % 3
            slot_ctr[0] += 1
            for n_, (o, n, nk, srcs) in enumerate(sts):
                i = stage_load(srcs, nk)
                cast(engs[n_ % len(engs)], slots[si][:, 0:nk, o:o + n], stages[i][:, 0:nk, 0:n],
                     [B_stage[i]], [B_slot[si]])
            tk.dma("pool", wscr_v[pi], slots[si][:], [B_slot[si]], [B_scr[pi]], B_slot[si])
            return slots[si], B_slot[si]

        def convert_wout():
            for cbk in range(8):
                cs = slice(cbk * 128, (cbk + 1) * 128)
                i = st_ctr[0] % NSTG
                st_ctr[0] += 1
                v0 = w_out_d[0:512, :].rearrange("(kc p) c -> p kc c", p=128)
                tk.dma("sp", stages[i][:, 0:4, :], v0[:, :, cs], [], [B_stage[i]], B_stage[i])
                v1 = w_out_d[512:1024, :].rearrange("(g r d) c -> d g r c", g=2, r=4, d=64)
                for g in range(2):
                    tk.dma("sp", stages[i][g * 64:(g + 1) * 64, 4:8, :], v1[:, g, :, cs], [], [B_stage[i]], B_stage[i],
                           chain=True)
                cast(("act", "dve", "pool")[cbk % 3], wout[:, :, cs], stages[i][:], [B_stage[i]], [B_wout])

        def get_piece(pi):
            if first_flag[0]:
                return convert_piece(pi, E3)
            return load_piece(pi)

        def load_piece(pi):
            si = slot_ctr[0] % 3
            slot_ctr[0] += 1
            tk.dma("sp", slots[si][:], wscr_v[pi], [B_scr[pi]], [B_slot[si]], B_slot[si])
            return slots[si], B_slot[si]

        tiles = [list(range(i, min(i + 4, NP))) for i in range(0, NP, 4)] + [[NP]]
        E3 = ("act", "dve", "pool")

        def x_loads(tile):
            for b, gb in enumerate(tile):
                if gb == NP:
                    V(lambda e: e.memset(xin[b][64:128, :], 0.0), [], [B_xin[b]])
                    tk.dma("sp", xin[b][0:64, :], xs_d[:, :], [], [B_xin[b]], B_xin[b])
                else:
                    tk.dma("sp", xin[b][:], x_d[gb * 128:(gb + 1) * 128, :], [], [B_xin[b]], B_xin[b])

        def transposes_to(src, B_src, b, B_dst, act_only=False):
            for half in range(2):
                pt, Bp = ps()
                PE([tp(pt[:, k * 128:(k + 1) * 128], src[:, (half * 4 + k) * 128:(half * 4 + k + 1) * 128], ident_f)
                    for k in range(4)], [B_src] + Cr, [Bp])
                o = xT[:, half * 4:(half + 1) * 4, b * 128:(b + 1) * 128]
                i_ = pt[:].rearrange("p (k t) -> p k t", k=4)
                if half == 0 or act_only:
                    A(lambda e: e.copy(out=o, in_=i_), [Bp], [B_dst])
                else:
                    V(lambda e: e.tensor_copy(out=o, in_=i_), [Bp], [B_dst])

        def ln_gen(items):
            for (src, B_src, gcol, bcol, dst, B_dst, k, pool_b, post) in items:
                sm, B_sm = LNSETS[k]
                st, mv, ve, lnv = sm["st"], sm["mv"], sm["ve"], sm["lnv"]
                for hh in range(2):
                    V(lambda e: e.bn_stats(out=st[:, hh * 6:(hh + 1) * 6], in_=src[:, hh * 512:(hh + 1) * 512]),
                      [B_src], [B_sm["st"]])
                V(lambda e: e.bn_aggr(out=mv[:], in_=st[:]), [B_sm["st"]], [B_sm["mv"]])
                V(lambda e: e.tensor_scalar(out=ve[:], in0=mv[:, 1:2], scalar1=LN_EPS, scalar2=None, op0=ALU.add),
                  [B_sm["mv"]], [B_sm["ve"]])
                V(lambda e: e.tensor_scalar(out=lnv[:], in0=mv[:, 0:1], scalar1=-1.0, scalar2=None, op0=ALU.mult),
                  [B_sm["mv"]], [B_sm["lnv"]])
            yield
            for (src, B_src, gcol, bcol, dst, B_dst, k, pool_b, post) in items:
                sm, B_sm = LNSETS[k]
                ve, lnv, rstd, nmr = sm["ve"], sm["lnv"], sm["rstd"], sm["nmr"]
                A(lambda e: e.activation(out=ve[:], in_=ve[:], func=AF.Ln), [B_sm["ve"]], [B_sm["ve"]])
                A(lambda e: e.activation(out=rstd[:], in_=ve[:], func=AF.Exp, scale=-0.5), [B_sm["ve"]], [B_sm["rstd"]])
                A(lambda e: e.activation(out=nmr[:], in_=lnv[:], func=AF.Identity, scale=rstd[:]),
                  [B_sm["lnv"], B_sm["rstd"]], [B_sm["nmr"]])
                A(lambda e: e.activation(out=dst[:], in_=src[:], func=AF.Identity, bias=nmr[:], scale=rstd[:]),
                  [B_src, B_sm["nmr"], B_sm["rstd"]], [B_dst])
            yield
            for (src, B_src, gcol, bcol, dst, B_dst, k, pool_b, post) in items:
                V(lambda e: e.tensor_tensor(out=dst[:], in0=dst[:], in1=cfc(gcol, 1024), op=ALU.mult), [B_dst] + Cr, [B_dst])
                (G if pool_b else V)(lambda e: e.tensor_tensor(out=dst[:], in0=dst[:], in1=cfc(bcol, 1024), op=ALU.add),
                                     [B_dst] + Cr, [B_dst])
                if post is not None:
                    post()
            yield

        ckpt(1)
        first = True
        first_flag = [True]
        E3 = ("act", "dve", "pool")
        pending = []

        deferred = []

        def run_deferred(n):
            for _ in range(n):
                if deferred:
                    try:
                        next(deferred[0])
                    except StopIteration:
                        deferred.pop(0)

        def drain(n):
            for _ in range(n):
                if pending:
                    convert_piece(pending.pop(0), E3)

        def drain_until(pi_end):
            while pending and pending[0] < pi_end:
                convert_piece(pending.pop(0), E3)

        for ti, tile in enumerate(tiles):
          try:
            nb = len(tile)
            T = nb * 128
            is_s = tile[0] == NP
            if first:
                x_loads(tile)
                for b in range(nb):
                    transposes_to(xin[b], B_xin[b], b, B_xT[b])
            Bx_all = [B_xT[b] for b in range(nb)]

            if is_s:
                pt, Bp = ps()
                tk.dma("sp", ck_sb, ck_d[:, :], [], [B_ck], B_ck)
                tk.dma("sp", cv_sb, cv_d[:, :], [], [B_cv], B_cv, chain=True)
                PE([tp(pt[:, 0:128], ck_sb, ident_f)], [B_ck] + Cr, [Bp])
                V(lambda e: e.tensor_copy(out=kT[:, 0:128], in_=pt[:, 0:128]), [Bp], [B_kT])
                V(lambda e: e.tensor_copy(out=Vt[:, 0, :], in_=cv_sb), [B_cv], [B_Vt])
            elif ti > 0:
                pnb = len(tiles[ti - 1])
                V(lambda e: e.tensor_copy(out=kT[:, 0:128], in_=kT[:, pnb * 128:(pnb + 1) * 128]), [B_kT], [B_kT])
                V(lambda e: e.tensor_copy(out=Vt[:, 0, :], in_=Vt[:, pnb, :]), [B_Vt], [B_Vt])

            ev_ctr = [0]

            def evac(out, in_, r, w):
                ev_ctr[0] += 1
                if ev_ctr[0] % 2:
                    A(lambda e: e.copy(out=out, in_=in_), r, w)
                else:
                    V(lambda e: e.tensor_copy(out=out, in_=in_), r, w)

            cs_blk = None
            for b, gb in enumerate(tile):
                if gb == NP - 1 or gb == NP:
                    cs_blk = b
            need_kv = cs_blk is not None

            def fm_chunk(slot, Bs, cc, out_ap, B_out):
                pt, Bp = ps()
                PE([mm(pt[:, 0:T], slot[:, kc, cc * 128:(cc + 1) * 128], xT[:, kc, 0:T], kc == 0, kc == 7)
                    for kc in range(8)], [Bs] + Bx_all, [Bp])
                evac(out_ap, pt[:, 0:T], [Bp], B_out)

            def tm_block(slot, Bs, b, ncols):
                pt, Bp = ps()
                PE([mm(pt[:, 0:ncols], xT[:, kc, b * 128:(b + 1) * 128], slot[:, kc, 0:ncols], kc == 0, kc == 7)
                    for kc in range(8)], [Bs, B_xT[b]], [Bp])
                return pt, Bp

            for zi in range(2):
                slot, Bs = get_piece(P_IN[f"z{zi}"])
                for b in range(nb):
                    pt, Bp = tm_block(slot, Bs, b, 256)
                    A(lambda e: e.activation(out=U[:, 20 + b, zi * 256:(zi + 1) * 256], in_=pt[:, 0:256], func=AF.Silu),
                      [Bp], BU(20 + b, b))
                if zi == 0:
                    run_deferred(1)
            ckpt(3.1)
            if first:
                convert_wout()
            rows07 = sum([BU(c) for c in range(8)], [])
            if is_s:
                V(lambda e: e.tensor_copy(out=U[:, 0:8, 0:3], in_=halo_s[:]), [B_halos], rows07)
            else:
                V(lambda e: e.tensor_copy(out=U[:, 0:8, 0:3], in_=halo[:]), [B_halo], rows07)

            def conv_chunk(c):
                acc, Ba = Fb[4 + c % 2], B_F[4 + c % 2]
                A(lambda e: e.activation(out=acc[:, 0:T], in_=U[:, c, 0:T], func=AF.Identity,
                                         bias=cf[:, CBI + c:CBI + c + 1], scale=cf[:, CW + c * 4:CW + c * 4 + 1]),
                  BU(c) + Cr, [Ba])
                for i in range(1, 4):
                    V(lambda e: e.scalar_tensor_tensor(out=acc[:, 0:T], in0=U[:, c, i:i + T],
                                                       scalar=cf[:, CW + c * 4 + i:CW + c * 4 + i + 1], in1=acc[:, 0:T],
                                                       op0=ALU.mult, op1=ALU.add), BU(c) + [Ba] + Cr, [Ba])
                conv_tail.append((c, acc, Ba))

            conv_tail = []

            def conv_silu():
                if conv_tail:
                    c, acc, Ba = conv_tail.pop(0)
                    A(lambda e: e.activation(out=U[:, 8 + c, 0:T], in_=acc[:, 0:T], func=AF.Silu), [Ba], BU(8 + c))

            for xi, nm in enumerate(("x0", "x1", "B", "C")):
                slot, Bs = get_piece(P_IN[nm])
                for cc in range(2):
                    c = xi * 2 + cc
                    fm_chunk(slot, Bs, cc, U[:, c, 3:3 + T], BU(c))
                    conv_silu()
                    conv_chunk(c)
                if xi < 2:
                    run_deferred(1)
                if cs_blk is not None:
                    pt, Bp = tm_block(slot, Bs, cs_blk, 256)
                    evac(H[:, xi * 256:(xi + 1) * 256], pt[:, 0:256], [Bp], [B_H])
            conv_silu()
            if not is_s:
                V(lambda e: e.tensor_copy(out=halo[:], in_=U[:, 0:8, T:T + 3]), rows07, [B_halo])
            ckpt(3.2)
            if cs_blk is not None:
                r0 = 61 if is_s else 125
                tk.dma("pool", (convs_d if is_s else convp_d)[:, :], H[r0:r0 + 3, :], [B_H], [], B_H)
            ckpt(3.3)
            for qi in range(2):
                slot, Bs = get_piece(P_IN[f"q{qi}"])
                for jj in range(2):
                    j = qi * 2 + jj
                    fm_chunk(slot, Bs, jj, U[:, 16 + j, 0:T], BU(16 + j))
            slot, Bs = get_piece(P_IN["k"])
            fm_chunk(slot, Bs, 0, kT[:, 128:128 + T], [B_kT])
            ckpt(3.4)
            slot, Bs = get_piece(P_IN["kv"])
            for b in range(nb):
                pt, Bp = tm_block(slot, Bs, b, 256)
                if "noV" not in _os.environ.get("KVAR", ""):
                    V(lambda e: e.tensor_copy(out=Vt[:, b + 1, :], in_=pt[:, 128:256]), [Bp], [B_Vt])
                if need_kv and b == cs_blk and "noA" not in _os.environ.get("KVAR", ""):
                    A(lambda e: e.copy(out=ktok[:], in_=pt[:, 0:128]), [Bp], [B_ktok])
                    A(lambda e: e.copy(out=vtok[:], in_=pt[:, 128:256]), [Bp], [B_vtok])
                    if is_s:
                        tk.dma("pool", ks_d[0:64, :], ck_d[64:128, :], [], [], Buf("c2ck"))
                        tk.dma("pool", vs_d[0:64, :], cv_d[64:128, :], [], [], Buf("c2cv"))
                        tk.dma("pool", ks_d[64:128, :], ktok[0:64, :], [B_ktok], [], B_ktok)
                        tk.dma("pool", vs_d[64:128, :], vtok[0:64, :], [B_vtok], [], B_vtok)
                    elif _os.environ.get("KVAR", "") != "nodma":
                        tk.dma("pool", kp_d[:, :], ktok[:], [B_ktok], [], B_ktok)
                        tk.dma("pool", vp_d[:, :], vtok[:], [B_vtok], [], B_vtok)
            ckpt(3.5)
            slot, Bs = get_piece(P_IN["dt"])
            for b in range(nb):
                pt, Bp = tm_block(slot, Bs, b, 8)
                V(lambda e: e.tensor_tensor(out=dtpre[:, b, :], in0=pt[:, 0:8], in1=cfc(DTB, 8), op=ALU.add),
                  [Bp] + Cr, [B_dtpre])
            nb8 = nb * 8
            dp = dtpre[:, 0:nb, :]
            V(lambda e: e.scalar_tensor_tensor(out=dtw[:, 0:nb, :], in0=dp, scalar=-1.0, in1=dp, op0=ALU.mult, op1=ALU.max),
              [B_dtpre], [B_dtw])
            A(lambda e: e.activation(out=dtw[:, 0:nb, :], in_=dtw[:, 0:nb, :], func=AF.Exp, scale=-1.0), [B_dtw], [B_dtw])
            V(lambda e: e.tensor_scalar(out=dtw[:, 0:nb, :], in0=dtw[:, 0:nb, :], scalar1=1.0, scalar2=None, op0=ALU.add),
              [B_dtw], [B_dtw])
            A(lambda e: e.activation(out=dtw[:, 0:nb, :], in_=dtw[:, 0:nb, :], func=AF.Ln), [B_dtw], [B_dtw])
            V(lambda e: e.scalar_tensor_tensor(out=dtall[:, 0:nb, :], in0=dp, scalar=0.0, in1=dtw[:, 0:nb, :],
                                               op0=ALU.max, op1=ALU.add), [B_dtpre, B_dtw], [B_dtall])
            if is_s:
                V(lambda e: e.tensor_scalar(out=dtall[:, 0:nb, :], in0=dtall[:, 0:nb, :], scalar1=cf[:, TOKM:TOKM + 1],
                                            scalar2=None, op0=ALU.mult), [B_dtall] + Cr, [B_dtall])
            V(lambda e: e.tensor_tensor(out=dtAall[:, 0:nb, :], in0=dtall[:, 0:nb, :],
                                        in1=Ab[:].unsqueeze(1).broadcast_to([128, nb, 8]), op=ALU.mult),
              [B_dtall, B_Ab], [B_dtall])
            pc, Bpc = ps()
            dA2 = dtAall[:, 0:nb, :].rearrange("p b h -> p (b h)")
            PE([mm(pc[:, 0:nb8], tri_f, dA2, True, True), mm(pc[:, 32:32 + nb8], ones_f, dA2, True, True)],
               [B_dtall] + Cr, [Bpc])
            fl = lambda t_: t_[:, 0:nb, :].rearrange("p b h -> p (b h)")
            V(lambda e: e.tensor_copy(out=fl(dtw), in_=pc[:, 0:nb8]), [Bpc], [B_dtw])
            A(lambda e: e.activation(out=fl(ecumall), in_=pc[:, 0:nb8], func=AF.Exp), [Bpc], [B_dtall])
            A(lambda e: e.activation(out=fl(bdecall), in_=pc[:, 32:32 + nb8], func=AF.Exp), [Bpc], [B_dtall])
            V(lambda e: e.tensor_tensor(out=fl(dtw), in0=pc[:, 32:32 + nb8], in1=fl(dtw), op=ALU.subtract),
              [Bpc, B_dtw], [B_dtw])
            A(lambda e: e.activation(out=fl(toendall), in_=fl(dtw), func=AF.Exp), [B_dtw], [B_dtall])

            run_deferred(99)
            ckpt(4)
            if is_s:
                pt, Bp = ps()
                tk.dma("sp", sinit[:], sinit_d[:, :], [], [B_sinit], B_sinit)
                PE([tp(pt[:, c * 128:(c + 1) * 128], sinit[:, c * 128:(c + 1) * 128], ident_f) for c in range(4)],
                   [B_sinit] + Cr, [Bp])
                V(lambda e: e.tensor_copy(out=S[:], in_=pt[:]), [Bp], [B_S])
                A(lambda e: e.copy(out=Sb_[:], in_=pt[:]), [Bp], [B_Sb])

            emitted = set()

            def ssd_front(b, gb):
                t0 = b * 128
                ts = slice(t0, t0 + 128)
                par = b % 2
                (L, B_L, decay, B_decay, MT, B_MT, xdt, B_xdt, xdte, B_xdte, xsD, B_xsD, Btok, B_Btok,
                 cbm, B_cbm, gn, B_gn, FT, B_FT, FY, B_FY) = SSDSET[par]
                sm, B_sm = SMSETS[par]
                dt_ = dtall[:, b, :]
                dtA = dtAall[:, b, :]
                ecum = ecumall[:, b, :]
                toend = toendall[:, b, :]
                bdec = bdecall[:, b, :]
                B_dtT = [B_dtall]
                while b >= 2 and ('S', b - 2) not in emitted:
                    yield
                pxs, Bpxs = ps()
                pxs_b = pxs[:].bitcast(BF16)
                PE([tp(pxs_b[:, c * 128:(c + 1) * 128], U[:, 8 + c, ts], ident_b) for c in range(6)],
                   sum([BU(8 + c, b) for c in range(6)], []) + Cbr, [Bpxs])
                xs3 = pxs_b[:, 0:512].rearrange("p (h d) -> p h d", h=8)
                V(lambda e: e.tensor_tensor(out=xdt[:].rearrange("p (h d) -> p h d", h=8), in0=xs3,
                                            in1=dt_.unsqueeze(2).broadcast_to([128, 8, 64]), op=ALU.mult),
                  [Bpxs] + B_dtT, [B_xdt])
                A(lambda e: e.copy(out=Btok[:], in_=pxs_b[:, 512:768]), [Bpxs], [B_Btok])
                yield
                V(lambda e: e.tensor_tensor(out=xdte[:].rearrange("p (h d) -> p h d", h=8),
                                            in0=xdt[:].rearrange("p (h d) -> p h d", h=8),
                                            in1=toend.unsqueeze(2).broadcast_to([128, 8, 64]), op=ALU.mult),
                  [B_xdt] + B_dtT, [B_xdte])
                use_r = True
                V(lambda e: e.tensor_tensor(out=L[:, 0:4, :].bitcast(F32R),
                                            in0=mst_f.unsqueeze(1).broadcast_to([128, 4, 128]),
                                            in1=dtA[:, 0:4].unsqueeze(2).broadcast_to([128, 4, 128]), op=ALU.mult),
                  B_dtT + Cr, [B_L])
                B_La = LA.setdefault(id(B_L), Buf("La"))
                for h_ in range(4, 8):
                    A(lambda e: e.activation(out=L[:, h_, :].bitcast(F32R), in_=mst_f, func=AF.Identity,
                                             scale=dtA[:, h_:h_ + 1]), B_dtT + Cr, [B_La])
                yield
                for hh in range(2):
                    pg, Bpg = ps()
                    PE([mm(pg[:, k * 128:(k + 1) * 128],
                           (L[:, hh * 4 + k, :].bitcast(F32R) if use_r else L[:, hh * 4 + k, :]),
                           (tri_r.bitcast(F32R) if use_r else tri_f), True, True)
                        for k in range(4)], [B_L if hh == 0 else B_La, B_trir] + Cr, [Bpg])
                    A(lambda e: e.activation(out=decay[:, hh * 4:(hh + 1) * 4, :].rearrange("p a b -> p (a b)"),
                                             in_=pg[:], func=AF.Exp), [Bpg], [B_decay])
                    yield
                pcb, Bpcb = ps()
                PE([mm(pcb[:, g * 128:(g + 1) * 128], U[:, 12 + g, ts], U[:, 14 + g, ts], True, True) for g in range(2)],
                   BU(12, b) + BU(13, b) + BU(14, b) + BU(15, b), [Bpcb])
                V(lambda e: e.tensor_tensor(out=cbm[:], in0=pcb[:, 0:256].rearrange("p (g l) -> p g l", g=2),
                                            in1=cau_f.unsqueeze(1).broadcast_to([128, 2, 128]), op=ALU.mult),
                  [Bpcb] + Cr, [B_cbm])
                yield
                for g in range(2):
                    V(lambda e: e.tensor_tensor(out=MT[:, g * 4:(g + 1) * 4, :], in0=decay[:, g * 4:(g + 1) * 4, :],
                                                in1=cbm[:, g:g + 1, :].broadcast_to([128, 4, 128]), op=ALU.mult),
                      [B_decay, B_cbm], [B_MT])
                yield
                emitted.add(('front', b))
                yield

            def ssd_back(b, gb):
                t0 = b * 128
                ts = slice(t0, t0 + 128)
                par = b % 2
                (L, B_L, decay, B_decay, MT, B_MT, xdt, B_xdt, xdte, B_xdte, xsD, B_xsD, Btok, B_Btok,
                 cbm, B_cbm, gn, B_gn, FT, B_FT, FY, B_FY) = SSDSET[par]
                sm, B_sm = SMSETS[par]
                dt_ = dtall[:, b, :]
                dtA = dtAall[:, b, :]
                ecum = ecumall[:, b, :]
                toend = toendall[:, b, :]
                bdec = bdecall[:, b, :]
                B_dtT = [B_dtall]
                while ('front', b) not in emitted:
                    yield
                while b >= 1 and ('S', b - 1) not in emitted:
                    yield
                py, Bpy = ps()
                fns = [mm(py[:, c_ * 128:(c_ + 1) * 128], U[:, 8 + c_, ts], diagD[:, c_, :], c_ == 0, False) for c_ in range(4)]
                for h in range(8):
                    fns.append(mm(py[:, h * 64:(h + 1) * 64], MT[:, h, :], xdt[:, h * 64:(h + 1) * 64], False, h == 7))
                PE(fns, [B_diagD, B_MT, B_xdt] + sum([BU(8 + c_, b) for c_ in range(4)], []), [Bpy])
                pz, Bpz = ps()
                PE([mm(pz[:, g * 256:(g + 1) * 256], U[:, 14 + g, ts], Sb_[:, g * 256:(g + 1) * 256], True, True)
                    for g in range(2)], BU(14, b) + BU(15, b) + [B_Sb], [Bpz])
                t1, yv, gvv = FT, FY, FT
                V(lambda e: e.tensor_tensor(out=t1[:].rearrange("p (h d) -> p h d", h=8),
                                            in0=pz[:].rearrange("p (h d) -> p h d", h=8),
                                            in1=ecum.unsqueeze(2).broadcast_to([128, 8, 64]), op=ALU.mult),
                  [Bpz] + B_dtT, [B_FT])
                V(lambda e: e.tensor_tensor(out=yv[:], in0=py[:], in1=t1[:], op=ALU.add), [Bpy, B_FT], [B_FY])
                yield
                pn, Bpn = ps()
                PE([mm(pn[:, g * 256:(g + 1) * 256], Btok[:, g * 128:(g + 1) * 128], xdte[:, g * 256:(g + 1) * 256],
                       True, True) for g in range(2)], [B_Btok, B_xdte], [Bpn])
                V(lambda e: e.tensor_tensor(out=S[:].rearrange("p (h d) -> p h d", h=8),
                                            in0=S[:].rearrange("p (h d) -> p h d", h=8),
                                            in1=bdec.unsqueeze(2).broadcast_to([128, 8, 64]), op=ALU.mult),
                  [B_S] + B_dtT, [B_S])
                V(lambda e: e.tensor_tensor(out=S[:], in0=pn[:], in1=S[:], op=ALU.add), [Bpn, B_S], [B_S])
                A(lambda e: e.copy(out=Sb_[:], in_=S[:]), [B_S], [B_Sb])
                emitted.add(('S', b))
                if gb == NP - 1 or gb == NP:
                    pt, Bp = ps()
                    PE([tp(pt[:, c * 128:(c + 1) * 128], S[:, c * 128:(c + 1) * 128], ident_f) for c in range(4)],
                       [B_S] + Cr, [Bp])
                    V(lambda e: e.tensor_copy(out=sout[:], in_=pt[:]), [Bp], [B_sout])
                    tk.dma("pool", (ssms_d if is_s else ssmp_d)[:, :], sout[:], [B_sout], [], B_sout)
                yield
                V(lambda e: e.tensor_tensor(out=gvv[:], in0=yv[:], in1=U[:, 20 + b, 0:512], op=ALU.mult),
                  [B_FY] + BU(20 + b, b), [B_FT])
                st2, mv2, ms, lms, rs2 = sm["st2"], sm["mv2"], sm["ms"], sm["lms"], sm["rs2"]
                for g in range(2):
                    V(lambda e: e.bn_stats(out=st2[:, g * 6:(g + 1) * 6], in_=gvv[:, g * 256:(g + 1) * 256]),
                      [B_FT], [B_sm["st2"]])
                    V(lambda e: e.bn_aggr(out=mv2[:, g * 2:(g + 1) * 2], in_=st2[:, g * 6:(g + 1) * 6]),
                      [B_sm["st2"]], [B_sm["mv2"]])
                    V(lambda e: e.scalar_tensor_tensor(out=ms[:, g:g + 1], in0=mv2[:, g * 2:g * 2 + 1],
                                                       scalar=mv2[:, g * 2:g * 2 + 1], in1=mv2[:, g * 2 + 1:g * 2 + 2],
                                                       op0=ALU.mult, op1=ALU.add), [B_sm["mv2"]], [B_sm["ms"]])
                V(lambda e: e.tensor_scalar(out=ms[:], in0=ms[:], scalar1=RMS_EPS, scalar2=None, op0=ALU.add),
                  [B_sm["ms"]], [B_sm["ms"]])
                A(lambda e: e.activation(out=lms[:], in_=ms[:], func=AF.Ln), [B_sm["ms"]], [B_sm["lms"]])
                A(lambda e: e.activation(out=rs2[:], in_=lms[:], func=AF.Exp, scale=-0.5), [B_sm["lms"]], [B_sm["rs2"]])
                yield
                for g in range(2):
                    V(lambda e: e.scalar_tensor_tensor(out=gn[:, g * 256:(g + 1) * 256], in0=gvv[:, g * 256:(g + 1) * 256],
                                                       scalar=rs2[:, g:g + 1], in1=cf[:, WN + g * 256:WN + (g + 1) * 256],
                                                       op0=ALU.mult, op1=ALU.mult), [B_FT, B_sm["rs2"]] + Cr, [B_gn])
                yield
                while b >= 2 and ('outmm', b - 2) not in emitted:
                    yield
                pgt, Bpgt = ps()
                pgt_b = pgt[:].bitcast(BF16)
                PE([tp(pgt_b[:, c * 128:(c + 1) * 128], gn[:, c * 128:(c + 1) * 128], ident_b) for c in range(4)],
                   [B_gn] + Cbr, [Bpgt])
                A(lambda e: e.copy(out=mixS[b % 2][:], in_=pgt_b[:, 0:512].rearrange("p (c t) -> p c t", c=4)),
                  [Bpgt], [B_mixS[b % 2]])

                emitted.add(('ssd', b))
                yield

            def att_chain(b, gb):
                t0 = b * 128
                ts = slice(t0, t0 + 128)
                kbs = [1] if gb == 0 else [0, 1]
                for g in range(2):
                    gs = slice(g * 64, (g + 1) * 64)
                    for kb in kbs:
                        gkb = g * 2 + kb
                        kc0 = t0 + kb * 128
                        psc, Bpsc = ps()
                        PE([mm(psc[:], kT[gs, kc0:kc0 + 128], U[gs, 16:20, ts], True, False),
                            mm(psc[:], ident_b, cb[:, BIAS + (gkb * 2) * 512:BIAS + (gkb * 2 + 1) * 512], False, False),
                            mm(psc[:], ident_b, cb[:, BIAS + (gkb * 2 + 1) * 512:BIAS + (gkb * 2 + 2) * 512], False, True)],
                           [B_kT] + BU(16, b) + BU(17, b) + BU(18, b) + BU(19, b) + Cbr, [Bpsc])
                        A(lambda e: e.activation(out=PT[gkb][:], in_=psc[:], func=AF.Exp, scale=0.125),
                          [Bpsc], [B_PT[gkb]])
                        yield
                po, Bpo = ps()
                pd, Bpd = ps()
                fo, fd = [], []
                for g in range(2):
                    gs = slice(g * 64, (g + 1) * 64)
                    for ki, kb in enumerate(kbs):
                        fo.append(mm(po[gs, :], Vt[:, b + kb, gs], PT[g * 2 + kb][:], ki == 0, ki == len(kbs) - 1))
                        fd.append(mm(pd[gs, :], ones_b[:, 0:64], PT[g * 2 + kb][:], ki == 0, False))
                    fd.append(mm(pd[gs, :], ones_b[:, 0:64], sinkrow[:, g * 512:(g + 1) * 512], False, True))
                PE(fo, [B_Vt] + B_PT, [Bpo])
                PE(fd, B_PT + [B_sink] + Cbr, [Bpd])
                lnd, rcp = Fb[0], Fb[1]
                A(lambda e: e.activation(out=lnd[:], in_=pd[:], func=AF.Ln), [Bpd], [B_F[0]])
                A(lambda e: e.activation(out=rcp[:], in_=lnd[:], func=AF.Exp, scale=-1.0), [B_F[0]], [B_F[1]])
                V(lambda e: e.tensor_tensor(out=attT[:, :, ts], in0=po[:].rearrange("p (r q) -> p r q", r=4),
                                            in1=rcp[:].rearrange("p (r q) -> p r q", r=4), op=ALU.mult),
                  [Bpo, B_F[1]], [B_attT[b]])

                emitted.add(('att', b))
                yield

            def out_chain(b, gb):
                while ('ssd', b) not in emitted or ('att', b) not in emitted:
                    yield
                t0 = b * 128
                ts = slice(t0, t0 + 128)
                Hx, B_Hx = HS[b % 2]
                for dh in range(2):
                    pm, Bpm = ps()
                    PE([mm(pm[:], (mixS[b % 2][:, kc, :] if kc < 4 else attT[:, kc - 4, ts]),
                           wout[:, kc, dh * 512:(dh + 1) * 512], kc == 0, False)
                        for kc in range(8)] +
                       [mm(pm[:], aident[:], xin[b][:, dh * 512:(dh + 1) * 512], False, True)],
                       [B_mixS[b % 2], B_attT[b], B_wout, B_xin[b], B_aident], [Bpm])
                    A(lambda e: e.copy(out=Hx[:, dh * 512:(dh + 1) * 512], in_=pm[:]), [Bpm], [B_Hx])
                    yield
                emitted.add(('outmm', b))
                yield from ln_gen([(Hx, B_Hx, L1G, L1B, hb[b], B_hb[b], b, False, None)])
                transposes_to(hb[b], B_hb[b], b, B_xT[b], act_only=True)
                emitted.add(('out', b))
                yield

            def p4_gen(c0, c1, Bxs, bsel, use_load=False):
                n_ = c1 - c0
                for i in range(11):
                    gslot, Bg = get_piece(P_G0 + 2 * i)
                    uslot, Bu = get_piece(P_G0 + 2 * i + 1)
                    for jj in range(2):
                        j = i * 2 + jj
                        pgg, Bpgg = ps()
                        puu, Bpuu = ps()
                        PE([mm(pgg[:, 0:n_], gslot[:, kc, jj * 128:(jj + 1) * 128], xT[:, kc, c0:c1], kc == 0, kc == 7)
                            for kc in range(8)], [Bg] + Bxs, [Bpgg])
                        PE([mm(puu[:, 0:n_], uslot[:, kc, jj * 128:(jj + 1) * 128], xT[:, kc, c0:c1], kc == 0, kc == 7)
                            for kc in range(8)], [Bu] + Bxs, [Bpuu])
                        sg, Bsg = Fb[j % 2], B_F[j % 2]
                        A(lambda e: e.activation(out=sg[:, 0:n_], in_=pgg[:, 0:n_], func=AF.Silu), [Bpgg], [Bsg])
                        V(lambda e: e.tensor_tensor(out=U[:, j, c0:c1], in0=sg[:, 0:n_], in1=puu[:, 0:n_], op=ALU.mult),
                          [Bsg, Bpuu], BU(j, bsel))
                        yield

            def p4h0_chain():
                while ('out', 0) not in emitted or ('out', 1) not in emitted:
                    yield
                yield from p4_gen(0, 256, [B_xT[0], B_xT[1]], 0)

            interleave_p4 = False

            def seq(gens):
                for g_ in gens:
                    yield from g_

            chains = [seq([ssd_front(b, gb) for b, gb in enumerate(tile) if b % 2 == 0]),
                      seq([ssd_front(b, gb) for b, gb in enumerate(tile) if b % 2 == 1]),
                      seq([ssd_back(b, gb) for b, gb in enumerate(tile) if b % 2 == 0]),
                      seq([ssd_back(b, gb) for b, gb in enumerate(tile) if b % 2 == 1]),
                      seq([att_chain(b, gb) for b, gb in enumerate(tile)]),
                      seq([out_chain(b, gb) for b, gb in enumerate(tile) if b % 2 == 0]),
                      seq([out_chain(b, gb) for b, gb in enumerate(tile) if b % 2 == 1])]
            if interleave_p4:
                chains.append(p4h0_chain())
            steps_per_round = [1] * 8
            while chains:
                for ci_, c_ in enumerate(list(chains)):
                    try:
                        for _ in range(steps_per_round[ci_] if len(chains) >= 3 else 1):
                            next(c_)
                    except StopIteration:
                        chains.remove(c_)
            ckpt(6)

            drain_until(P_D0)

            if not interleave_p4:
                for _ in p4_gen(0, T, Bx_all, None):
                    pass
            else:
                for _ in p4_gen(256, 512, [B_xT[2], B_xT[3]], 2, use_load=True):
                    pass

            ckpt(7)
            drain_until(NPIECE)

            if ti + 1 < len(tiles):
                nxt = tiles[ti + 1]
                x_loads(nxt)
                for b in range(len(nxt)):
                    transposes_to(xin[b], B_xin[b], b, B_xT[b])

            for dq in range(4):
                pbs = [ps() for _ in range(nb)]
                for fp in range(3):
                    dslot, Bd = get_piece(P_D0 + dq * 3 + fp)
                    nf = 8 if fp < 2 else 6
                    for b in range(nb):
                        pb, Bpb = pbs[b]
                        PE([mm(pb[:, 0:256], U[:, fp * 8 + k, b * 128:(b + 1) * 128], dslot[:, k, :],
                               fp == 0 and k == 0, fp == 2 and k == nf - 1) for k in range(nf)],
                           [Bd] + sum([BU(fp * 8 + k, b) for k in range(nf)], []), [Bpb])
                for b in range(nb):
                    pb, Bpb = pbs[b]
                    V(lambda e: e.scalar_tensor_tensor(out=hb[b][:, dq * 256:(dq + 1) * 256],
                                                       in0=hb[b][:, dq * 256:(dq + 1) * 256], scalar=ALPHA,
                                                       in1=pb[:, 0:256], op0=ALU.mult, op1=ALU.add),
                      [B_hb[b], Bpb], [B_hb[b]])
            def mk_store(b, gb, is_s_):
                def post():
                    if is_s_:
                        tk.dma("pool", ys_d[:, :], hb[b][0:64, :], [B_hb[b]], [], B_hb[b])
                    else:
                        tk.dma("pool", y_d[gb * 128:(gb + 1) * 128, :], hb[b][:], [B_hb[b]], [], B_hb[b])
                return post

            deferred.append(ln_gen([(hb[b], B_hb[b], L2G, L2B, hb[b], B_hb[b], b, False, mk_store(b, gb, is_s))
                                    for b, gb in enumerate(tile)]))
            run_deferred(3)
            first = False
            first_flag[0] = False
          except _Stop:
            break

        run_deferred(99)
        tk.finish()
    return nc


def _consts():
    cf = np.zeros((128, NCF), np.float32)
    i = np.arange(128)
    cf[:, IDF:IDF + 128] = np.eye(128)
    cf[:, TRI:TRI + 128] = (i[:, None] <= i[None, :])
    cf[:, MST:MST + 128] = (i[None, :] < i[:, None])
    cf[:, CAU:CAU + 128] = (i[:, None] <= i[None, :])
    cf[:, ONE:ONE + 128] = 1.0
    cf[:, TOKM] = (i < 64)
    cb = np.zeros((128, NCB), np.float32)
    cb[:, IDB:IDB + 128] = np.eye(128)
    cb[:, ONB:ONB + 128] = 1.0
    lo_part = np.zeros((128, NCB), np.float32)
    s = i[:, None].astype(np.float64)
    q = i[None, :].astype(np.float64)
    for g in range(2):
        for kb in range(2):
            gkb = g * 2 + kb
            tile_ = np.zeros((128, 512), np.float64)
            for r in range(4):
                m = 2.0 ** -(g * 4 + r + 1)
                if kb == 0:
                    dist = 128.0 + q - s
                    bad = (q >= 64) & (s < 64)
                else:
                    dist = np.abs(q - s)
                    bad = (q < 64) & (s >= 64)
                v = -8.0 * m * dist
                v = np.where(bad, -32768.0, v)
                tile_[:, r * 128:(r + 1) * 128] = v
            hi = tile_.astype(np.float32).astype(ml_dtypes.bfloat16)
            lo = (tile_ - hi.astype(np.float64)).astype(np.float32).astype(ml_dtypes.bfloat16)
            o = BIAS + (gkb * 2) * 512
            cb[:, o:o + 512] = hi.astype(np.float32)
            cb[:, o + 512:o + 1024] = lo.astype(np.float32)
    return cf, cb.astype(ml_dtypes.bfloat16)


_NC_CACHE = {}
_SIM_HOOK = None


def _run(NP, x_prompt, x_sample, state_conv, state_ssm, cache_k, cache_v, w_in, conv_w, conv_b, dt_bias,
         a_log, d_skip, ssm_norm_w, attn_sinks, w_out, ln1_g, ln1_b, w_gate, w_up, w_down, ln2_g, ln2_b):
    f = np.float32
    ncores = x_prompt.shape[0]
    cf, cb = _consts()
    cf[:, CW:CW + 32] = np.asarray(conv_w[0], f).reshape(4, 8, 128).transpose(2, 1, 0).reshape(128, 32)
    cf[:, CBI:CBI + 8] = np.asarray(conv_b[0], f).reshape(8, 128).T
    cf[:, DTB:DTB + 8] = np.broadcast_to(np.asarray(dt_bias[0], f), (128, 8))
    cf[:, ALOG:ALOG + 8] = np.broadcast_to(np.asarray(a_log[0], f), (128, 8))
    cf[:, DFULL:DFULL + 512] = np.broadcast_to(np.repeat(np.asarray(d_skip[0], f), 64), (128, 512))
    cf[:, DCOL:DCOL + 4] = np.repeat(np.asarray(d_skip[0], f), 64).reshape(4, 128).T
    cf[:, WN:WN + 512] = np.broadcast_to(np.asarray(ssm_norm_w[0], f), (128, 512))
    cf[:, L1G:L1G + 1024] = np.broadcast_to(np.asarray(ln1_g[0], f), (128, 1024))
    cf[:, L1B:L1B + 1024] = np.broadcast_to(np.asarray(ln1_b[0], f), (128, 1024))
    cf[:, L2G:L2G + 1024] = np.broadcast_to(np.asarray(ln2_g[0], f), (128, 1024))
    cf[:, L2B:L2B + 1024] = np.broadcast_to(np.asarray(ln2_b[0], f), (128, 1024))
    sink = np.ascontiguousarray(np.broadcast_to(np.repeat(np.asarray(attn_sinks[0], f), 128), (128, 1024)))
    if NP not in _NC_CACHE:
        _NC_CACHE[NP] = build(NP)
    nc = _NC_CACHE[NP]
    shared = {"w_in": np.ascontiguousarray(w_in[0], f), "w_out": np.ascontiguousarray(w_out[0], f),
              "w_gate": np.ascontiguousarray(w_gate[0], f), "w_up": np.ascontiguousarray(w_up[0], f),
              "w_down": np.ascontiguousarray(w_down[0], f), "cf": cf, "cb": cb, "sink": sink}
    in_maps = []
    for c in range(ncores):
        m = dict(shared)
        m["x"] = np.ascontiguousarray(x_prompt[c], f)
        m["xs"] = np.ascontiguousarray(x_sample[c], f)
        m["halo_s"] = np.ascontiguousarray(np.asarray(state_conv[0, c], f).reshape(3, 8, 128).transpose(2, 1, 0)).reshape(128, 24)
        m["sinit"] = np.ascontiguousarray(np.asarray(state_ssm[0, c], f).reshape(4, 128, 128).transpose(1, 0, 2)).reshape(128, 512)
        m["ck"] = np.ascontiguousarray(np.asarray(cache_k[0, c], f).reshape(128, 128))
        m["cv"] = np.ascontiguousarray(np.asarray(cache_v[0, c], f).reshape(128, 128))
        in_maps.append(m)
    if _SIM_HOOK is not None:
        R = _SIM_HOOK(nc, in_maps)
    else:
        res = run_bass_kernel_spmd(nc, in_maps, core_ids=list(range(ncores)))
        R = res.results

    def ssm(a):
        return np.asarray(a, f).reshape(128, 4, 128).transpose(1, 0, 2).reshape(8, 64, 128)

    y_p = np.stack([np.asarray(r["y"], f) for r in R])
    y_s = np.stack([np.asarray(r["ys"], f) for r in R])
    conv_p = np.stack([np.asarray(r["conv_p"], f) for r in R])[None]
    ssm_p = np.stack([ssm(r["ssm_p"]) for r in R])[None]
    k_p = np.stack([np.asarray(r["k_p"], f).reshape(128, 2, 64) for r in R])[None]
    v_p = np.stack([np.asarray(r["v_p"], f).reshape(128, 2, 64) for r in R])[None]
    conv_s = np.stack([np.asarray(r["conv_s"], f) for r in R])[None]
    ssm_s = np.stack([ssm(r["ssm_s"]) for r in R])[None]
    k_s = np.stack([np.asarray(r["k_s"], f).reshape(128, 2, 64) for r in R])[None]
    v_s = np.stack([np.asarray(r["v_s"], f).reshape(128, 2, 64) for r in R])[None]
    return (y_p, y_s, conv_p, ssm_p, k_p, v_p, conv_s, ssm_s, k_s, v_s)


def kernel(**inputs):
    inputs = {k: np.asarray(v) for k, v in inputs.items()}
    NP = inputs["x_prompt"].shape[1] // 128
    return _run(NP, **inputs)
```

```python
import numpy as np
import ml_dtypes
from contextlib import ExitStack
import concourse.bass as bass
import concourse.mybir as mybir
from concourse.bass_utils import run_bass_kernel_spmd

F32 = mybir.dt.float32
BF16 = mybir.dt.bfloat16
F32R = mybir.dt.float32r
AF = mybir.ActivationFunctionType
ALU = mybir.AluOpType

ALPHA = 2.0 ** 0.25
LN_EPS = 1e-5
RMS_EPS = 1e-5

IDF, TRI, MST, CAU, ONE, CW, CBI, DTB, ALOG, TOKM, DCOL = 0, 128, 256, 384, 512, 640, 672, 680, 688, 696, 697
DFULL, WN, L1G, L1B, L2G, L2B, NCF = 704, 1216, 1728, 2752, 3776, 4800, 5824
IDB, ONB, BIAS, NCB = 0, 128, 256, 4352


class Ev:
    __slots__ = ("sem", "val")

    def __init__(self, sem, val):
        self.sem = sem
        self.val = val


class Buf:
    def __init__(self, name, excl=False):
        self.name = name
        self.excl = excl
        self.w = None
        self.r = {}
        self.dsem = {}
        self.dcnt = {}


class TK:
    def __init__(self, nc, es):
        self.nc = nc
        self.es = es
        self.E = {"pe": nc.tensor, "act": nc.scalar, "dve": nc.vector, "pool": nc.gpsimd, "sp": nc.sync}
        self.sem = {k: es.enter_context(nc.semaphore("sem_" + k)) for k in ("pe", "act", "dve", "pool")}
        self.cnt = {k: 0 for k in self.sem}
        self.seen = {k: {} for k in self.E}
        self.dbufs = []

    def _wait(self, e, ev):
        if ev is None:
            return
        if e == "pe" and ev.sem is self.sem["pe"]:
            return
        k = id(ev.sem)
        if self.seen[e].get(k, 0) >= ev.val:
            return
        self.E[e].wait_ge(ev.sem, ev.val)
        self.seen[e][k] = ev.val

    def _deps(self, e, reads, writes):
        for b in reads:
            self._wait(e, b.w)
            if b.excl:
                for r in b.r.values():
                    self._wait(e, r)
        for b in writes:
            self._wait(e, b.w)
            for r in b.r.values():
                self._wait(e, r)

    def _rec(self, ev, reads, writes):
        for b in reads:
            b.r[id(ev.sem)] = ev
        for b in writes:
            b.w = ev
            b.r = {}

    def op(self, e, fn, reads, writes):
        self._deps(e, reads, writes)
        inst = fn(self.E[e])
        self.cnt[e] += 1
        inst.then_inc(self.sem[e], 1)
        self._rec(Ev(self.sem[e], self.cnt[e]), reads, writes)

    def group(self, e, fns, reads, writes):
        self._deps(e, reads, writes)
        inst = None
        for fn in fns:
            inst = fn(self.E[e])
        self.cnt[e] += 1
        inst.then_inc(self.sem[e], 1)
        self._rec(Ev(self.sem[e], self.cnt[e]), reads, writes)

    def dma(self, q, out, in_, reads, writes, sb, chain=False):
        if chain and sb.dsem.get(q) is not None:
            saved = [(b, b.w) for b in writes if b.w is not None and b.w.sem is sb.dsem[q]]
            for b, _ in saved:
                b.w = None
            self._deps(q, reads, writes)
            for b, w in saved:
                b.w = w
        else:
            self._deps(q, reads, writes)
        if q not in sb.dsem:
            sb.dsem[q] = self.es.enter_context(self.nc.semaphore("d_" + q + "_" + sb.name))
            sb.dcnt[q] = 0
            self.dbufs.append((sb, q))
        inst = self.E[q].dma_start(out=out, in_=in_)
        sb.dcnt[q] += 16
        inst.then_inc(sb.dsem[q], 16)
        self._rec(Ev(sb.dsem[q], sb.dcnt[q]), reads, writes)

    def finish(self):
        for e in ("pool", "sp"):
            for sb, q in self.dbufs:
                self._wait(e, Ev(sb.dsem[q], sb.dcnt[q]))


class _Stop(Exception):
    pass


def build(NP=32):
    import os
    KSTOP = float(os.environ.get("KSTOP", "99"))

    def ckpt(n):
        if n >= KSTOP:
            raise _Stop()

    nc = bass.Bass("TRN2", target_bir_lowering=False)
    TALL = NP * 128

    def din(name, shape, dt=F32):
        return nc.dram_tensor(name, shape, dt, kind="ExternalInput").ap()

    def dout(name, shape, dt=F32):
        return nc.dram_tensor(name, shape, dt, kind="ExternalOutput").ap()

    x_d = din("x", [TALL, 1024])
    xs_d = din("xs", [64, 1024])
    w_in_d = din("w_in", [1024, 2312])
    w_out_d = din("w_out", [1024, 1024])
    w_gate_d = din("w_gate", [1024, 2816])
    w_up_d = din("w_up", [1024, 2816])
    w_down_d = din("w_down", [2816, 1024])
    cf_d = din("cf", [128, NCF])
    cb_d = din("cb", [128, NCB], BF16)
    sink_d = din("sink", [128, 1024])
    halo_d = din("halo_s", [128, 24])
    sinit_d = din("sinit", [128, 512])
    ck_d = din("ck", [128, 128])
    cv_d = din("cv", [128, 128])
    y_d = dout("y", [TALL, 1024])
    ys_d = dout("ys", [64, 1024])
    convp_d = dout("conv_p", [3, 1024])
    ssmp_d = dout("ssm_p", [128, 512])
    kp_d = dout("k_p", [128, 128])
    vp_d = dout("v_p", [128, 128])
    convs_d = dout("conv_s", [3, 1024])
    ssms_d = dout("ssm_s", [128, 512])
    ks_d = dout("k_s", [128, 128])
    vs_d = dout("v_s", [128, 128])

    NPIECE = 11 + 22 + 12
    wscr = nc.dram_tensor("wscr", [NPIECE, 128, 2048], BF16, kind="Internal").ap()

    es = ExitStack()
    with es:
        import os as _os
        for _i in range(int(_os.environ.get("KDUMMY", "0"))):
            es.enter_context(nc.semaphore(f"dummy{_i}"))
        tk = TK(nc, es)

        def sb(name, shape, dt=F32):
            return es.enter_context(nc.sbuf_tensor("sb_" + name, shape, dt))

        cf = sb("cf", [128, NCF]); B_cf = Buf("cf")
        cb = sb("cb", [128, NCB], BF16); B_cb = Buf("cb")
        sinkrow = sb("sinkrow", [128, 1024], BF16); B_sink = Buf("sinkrow")
        wout = sb("wout", [128, 8, 1024], BF16); B_wout = Buf("wout")
        slots = [sb(f"slot{i}", [128, 8, 256], BF16) for i in range(3)]
        B_slot = [Buf(f"slot{i}") for i in range(3)]
        NSTG = 3
        stages = [sb(f"stage{i}", [128, 8, 128]) for i in range(NSTG)]
        B_stage = [Buf(f"stage{i}") for i in range(NSTG)]
        xin = [sb(f"xin{i}", [128, 1024]) for i in range(4)]
        B_xin = [Buf(f"xin{i}") for i in range(4)]
        hb = [sb(f"hb{i}", [128, 1024]) for i in range(4)]
        B_hb = [Buf(f"hb{i}") for i in range(4)]
        xT = sb("xT", [128, 8, 512], BF16)
        B_xT = [Buf(f"xT{i}") for i in range(4)]
        U = sb("U", [128, 24, 515], BF16)
        B_UH = [[Buf(f"U{i}a"), Buf(f"U{i}b")] for i in range(24)]

        def BU(r, b=None):
            return list(B_UH[r]) if b is None else [B_UH[r][b // 2]]
        kT = sb("kT", [128, 640], BF16); B_kT = Buf("kT")
        Vt = sb("Vt", [128, 5, 128], BF16); B_Vt = Buf("Vt")
        ktok = sb("ktok", [128, 128]); B_ktok = Buf("ktok")
        vtok = sb("vtok", [128, 128]); B_vtok = Buf("vtok")
        dtpre = sb("dtpre", [128, 4, 8]); B_dtpre = Buf("dtpre")
        halo = sb("halo", [128, 8, 3], BF16); B_halo = Buf("halo")
        halo_s = sb("halo_s", [128, 8, 3]); B_halos = Buf("halos")
        S = sb("S", [128, 512]); B_S = Buf("S")
        Sb_ = sb("Sb", [128, 512], BF16); B_Sb = Buf("Sb")
        Fb = [sb(f"F{i}", [128, 512]) for i in range(6)]
        B_F = [Buf(f"F{i}") for i in range(6)]
        H = sb("H", [128, 1024]); B_H = Buf("H")
        H2 = sb("H2", [128, 1024]); B_H2 = Buf("H2")
        HS = [(H, B_H), (H2, B_H2)]
        L = sb("L", [128, 8, 128]); B_L = Buf("L")
        decay = sb("decay", [128, 8, 128]); B_decay = Buf("decay")
        MT = sb("MT", [128, 8, 128], BF16); B_MT = Buf("MT")
        xdt = sb("xdt", [128, 512], BF16); B_xdt = Buf("xdt")
        xdte = sb("xdte", [128, 512], BF16); B_xdte = Buf("xdte")
        xsD = sb("xsD", [128, 512], BF16); B_xsD = Buf("xsD")
        Btok = sb("Btok", [128, 256], BF16); B_Btok = Buf("Btok")
        cbm = sb("cbm", [128, 2, 128]); B_cbm = Buf("cbm")
        gn = sb("gn", [128, 512], BF16); B_gn = Buf("gn")
        PT = [sb(f"PT{i}", [128, 512], BF16) for i in range(4)]
        B_PT = [Buf(f"PT{i}") for i in range(4)]
        mixS = [sb(f"mixS{i}", [128, 4, 128], BF16) for i in range(2)]
        B_mixS = [Buf(f"mixS{i}") for i in range(2)]
        attT = sb("attT", [128, 4, 512], BF16)
        B_attT = [Buf(f"attT{i}") for i in range(4)]
        Ab = sb("Ab", [128, 8]); B_Ab = Buf("Ab")
        sm = {}
        B_sm = {}
        SMW = {}
        for nm, w in (("dt", 8), ("ab", 8), ("e1", 8), ("l1", 8), ("dtA", 8), ("cum", 8), ("ecum", 8), ("dd", 8),
                      ("toend", 8), ("bdec", 8), ("st", 12), ("mv", 2), ("ve", 1), ("lnv", 1), ("rstd", 1),
                      ("nmr", 1), ("st2", 12), ("mv2", 4), ("ms", 2), ("lms", 2), ("rs2", 2)):
            sm[nm] = sb("sm_" + nm, [128, w])
            SMW[nm] = w
            B_sm[nm] = Buf("sm_" + nm)

        sout = Fb[0]; B_sout = B_F[0]
        sinit = Fb[0]; B_sinit = B_F[0]
        ck_sb = Fb[1][:, 0:128]; cv_sb = Fb[1][:, 128:256]; B_ck = B_F[1]; B_cv = B_F[1]
        tri_rt = sb("tri_r", [128, 128]); B_trir = Buf("tri_r")
        diagD = sb("diagD", [128, 4, 128], BF16); B_diagD = Buf("diagD")
        aident = sb("aident", [128, 128]); B_aident = Buf("aident")
        dtw = sb("dtw", [128, 4, 8]); B_dtw = Buf("dtw")
        dtall = sb("dtall", [128, 4, 8]); B_dtall = Buf("dtall")
        dtAall = sb("dtAall", [128, 4, 8])
        ecumall = sb("ecumall", [128, 4, 8])
        toendall = sb("toendall", [128, 4, 8])
        bdecall = sb("bdecall", [128, 4, 8])
        L1 = sb("L1", [128, 8, 128]); B_L1 = Buf("L1")
        MT1 = sb("MT1", [128, 8, 128], BF16); B_MT1 = Buf("MT1")
        xdt1 = sb("xdt1", [128, 512], BF16); B_xdt1 = Buf("xdt1")
        xdte1 = sb("xdte1", [128, 512], BF16); B_xdte1 = Buf("xdte1")
        xsD1 = sb("xsD1", [128, 512], BF16); B_xsD1 = Buf("xsD1")
        Btok1 = sb("Btok1", [128, 256], BF16); B_Btok1 = Buf("Btok1")
        cbm1 = sb("cbm1", [128, 2, 128]); B_cbm1 = Buf("cbm1")
        gn1 = sb("gn1", [128, 512], BF16); B_gn1 = Buf("gn1")
        sm1 = {}
        B_sm1 = {}
        for nm in list(sm.keys()):
            sm1[nm] = sb("sm1_" + nm, [128, SMW[nm]])
            B_sm1[nm] = Buf("sm1_" + nm)
        SSDSET = [
            (L, B_L, decay, B_decay, MT, B_MT, xdt, B_xdt, xdte, B_xdte, xsD, B_xsD, Btok, B_Btok,
             cbm, B_cbm, gn, B_gn, Fb[2], B_F[2], Fb[3], B_F[3]),
            (L1, B_L1, stages[1], B_stage[1], MT1, B_MT1, xdt1, B_xdt1, xdte1, B_xdte1, xsD1, B_xsD1,
             Btok1, B_Btok1, cbm1, B_cbm1, gn1, B_gn1, Fb[4], B_F[4], Fb[5], B_F[5]),
        ]
        SMSETS = [(sm, B_sm), (sm1, B_sm1)]
        LA = {}
        LNSETS = []
        for k_ in range(4):
            d_, B_d = {}, {}
            for nm in ("st", "mv", "ve", "lnv", "rstd", "nmr"):
                d_[nm] = sb(f"ln{k_}_{nm}", [128, SMW[nm]])
                B_d[nm] = Buf(f"ln{k_}_{nm}")
            LNSETS.append((d_, B_d))

        banks = [es.enter_context(nc.psum_tensor(f"bank{i}", [128, 512], F32)) for i in range(8)]
        B_bank = [Buf(f"bank{i}", excl=True) for i in range(8)]
        bank_ctr = [0]

        def ps():
            i = bank_ctr[0] % 8
            bank_ctr[0] += 1
            return banks[i], B_bank[i]

        def V(fn, r, w):
            tk.op("dve", fn, r, w)

        def A(fn, r, w):
            tk.op("act", fn, r, w)

        def G(fn, r, w):
            tk.op("pool", fn, r, w)

        def PE(fns, r, w):
            tk.group("pe", fns, r, w)

        def mm(out, lhsT, rhs, start, stop):
            return lambda e: e.matmul(out, lhsT, rhs, start=start, stop=stop)

        def tp(out, in_, ident):
            return lambda e: e.transpose(out, in_, ident)

        def cfc(off, n):
            return cf[:, off:off + n]

        ident_f = cfc(IDF, 128)
        tri_f = cfc(TRI, 128)
        mst_f = cfc(MST, 128)
        cau_f = tri_f
        tri_r = tri_rt[:]
        ones_f = cfc(ONE, 128)
        ident_b = cb[:, IDB:IDB + 128]
        ones_b = cb[:, ONB:ONB + 128]
        Cr = [B_cf]
        Cbr = [B_cb]

        tk.dma("sp", cf[:], cf_d[:, :], [], [B_cf], B_cf)
        tk.dma("sp", cb[:], cb_d[:, :], [], [B_cb], B_cb)
        tk.dma("sp", H[:], sink_d[:, :], [], [B_H], B_H)
        tk.dma("sp", halo_s[:].rearrange("p a b -> p (a b)"), halo_d[:, :], [], [B_halos], B_halos)
        A(lambda e: e.activation(out=Ab[:], in_=cfc(ALOG, 8), func=AF.Exp), Cr, [B_Ab])
        V(lambda e: e.tensor_scalar(out=Ab[:], in0=Ab[:], scalar1=-1.0, scalar2=None, op0=ALU.mult), [B_Ab], [B_Ab])
        MTf = MT[:].rearrange("p a b -> p (a b)")
        A(lambda e: e.activation(out=H[:], in_=H[:], func=AF.Exp), [B_H], [B_H])
        V(lambda e: e.tensor_copy(out=MTf, in_=H[:]), [B_H], [B_MT])
        V(lambda e: e.tensor_tensor(out=H[:], in0=H[:], in1=MTf, op=ALU.subtract), [B_H, B_MT], [B_H])
        V(lambda e: e.memset(sinkrow[:], 0.0), [], [B_sink])
        V(lambda e: e.tensor_copy(out=sinkrow[0:1, :], in_=MTf[0:1, :]), [B_MT], [B_sink])
        V(lambda e: e.tensor_copy(out=sinkrow[32:33, :], in_=H[32:33, :]), [B_H], [B_sink])
        V(lambda e: e.tensor_scalar(out=aident[:], in0=ident_f, scalar1=ALPHA, scalar2=None, op0=ALU.mult), Cr, [B_aident])
        V(lambda e: e.memset(S[:], 0.0), [], [B_S])
        V(lambda e: e.memset(Sb_[:], 0.0), [], [B_Sb])
        V(lambda e: e.memset(halo[:], 0.0), [], [B_halo])
        for c_ in range(4):
            A(lambda e: e.activation(out=diagD[:, c_, :], in_=ident_b, func=AF.Identity, scale=cf[:, DCOL + c_:DCOL + c_ + 1]),
              Cr + Cbr, [B_diagD])
        V(lambda e: e.tensor_scalar(out=tri_r.bitcast(F32R), in0=tri_f, scalar1=1.0, scalar2=None, op0=ALU.mult),
          Cr, [B_trir])

        wv = w_in_d.rearrange("(kc p) c -> p kc c", p=128)
        qv = w_in_d[:, 1544:2056].rearrange("(kc p) (g j d) -> p kc j g d", p=128, g=2, j=4, d=64)
        gv_ = w_gate_d.rearrange("(kc p) c -> p kc c", p=128)
        uv_ = w_up_d.rearrange("(kc p) c -> p kc c", p=128)
        dv_ = w_down_d.rearrange("(fc p) c -> p fc c", p=128)

        pieces = []

        def simple(name, view, c0, ncols):
            sts = []
            for o in range(0, ncols, 128):
                n = min(128, ncols - o)
                sts.append((o, n, 8, [(view[:, :, c0 + o:c0 + o + n], 0, n)]))
            pieces.append((name, sts))

        simple("z0", wv, 0, 256)
        simple("z1", wv, 256, 256)
        simple("x0", wv, 512, 256)
        simple("x1", wv, 768, 256)
        simple("B", wv, 1024, 256)
        simple("C", wv, 1280, 256)
        for qi in range(2):
            sts = []
            for jj in range(2):
                j = qi * 2 + jj
                sts.append((jj * 128, 128, 8, [(qv[:, :, j, 0, :], 0, 64), (qv[:, :, j, 1, :], 64, 128)]))
            pieces.append((f"q{qi}", sts))
        simple("k", wv, 2056, 128)
        simple("kv", wv, 2056, 256)
        simple("dt", wv, 1536, 8)
        P_IN = {n: i for i, (n, _) in enumerate(pieces)}
        for i in range(11):
            simple(f"g{i}", gv_, i * 256, 256)
            simple(f"u{i}", uv_, i * 256, 256)
        P_G0 = 11
        for dq in range(4):
            for fp in range(3):
                f0 = fp * 8
                nf = 8 if fp < 2 else 6
                sts = []
                for o in (0, 128):
                    sts.append((o, 128, nf, [(dv_[:, f0:f0 + nf, dq * 256 + o:dq * 256 + o + 128], 0, 128)]))
                pieces.append((f"d{dq}_{fp}", sts))
        P_D0 = 33
        assert len(pieces) == NPIECE
        B_scr = [Buf(f"scr{i}") for i in range(NPIECE)]
        wscr_v = [wscr[i].rearrange("p (k c) -> p k c", c=256) for i in range(NPIECE)]

        st_ctr = [0]
        slot_ctr = [0]

        def stage_load(srcs, nk):
            i = st_ctr[0] % NSTG
            st_ctr[0] += 1
            for n_, (src, lo, hi) in enumerate(srcs):
                tk.dma("sp", stages[i][:, 0:nk, lo:hi], src, [], [B_stage[i]], B_stage[i], chain=n_ > 0)
            return i

        def cast(eng, out, in_, r, w):
            if eng == "act":
                A(lambda e: e.copy(out=out, in_=in_), r, w)
            elif eng == "dve":
                V(lambda e: e.tensor_copy(out=out, in_=in_), r, w)
            else:
                G(lambda e: e.tensor_copy(out=out, in_=in_), r, w)

        def convert_piece(pi, engs):
            name, sts = pieces[pi]
            si = slot_ctr[0] % 3
            slot_ctr[0] += 1
            for n_, (o, n, nk, srcs) in enumerate(sts):
                i = stage_load(srcs, nk)
                cast(engs[n_ % len(engs)], slots[si][:, 0:nk, o:o + n], stages[i][:, 0:nk, 0:n],
                     [B_stage[i]], [B_slot[si]])
            tk.dma("pool", wscr_v[pi], slots[si][:], [B_slot[si]], [B_scr[pi]], B_slot[si])
            return slots[si], B_slot[si]

        def convert_wout():
            for cbk in range(8):
                cs = slice(cbk * 128, (cbk + 1) * 128)
                i = st_ctr[0] % NSTG
                st_ctr[0] += 1
                v0 = w_out_d[0:512, :].rearrange("(kc p) c -> p kc c", p=128)
                tk.dma("sp", stages[i][:, 0:4, :], v0[:, :, cs], [], [B_stage[i]], B_stage[i])
                v1 = w_out_d[512:1024, :].rearrange("(g r d) c -> d g r c", g=2, r=4, d=64)
                for g in range(2):
                    tk.dma("sp", stages[i][g * 64:(g + 1) * 64, 4:8, :], v1[:, g, :, cs], [], [B_stage[i]], B_stage[i],
                           chain=True)
                cast(("act", "dve", "pool")[cbk % 3], wout[:, :, cs], stages[i][:], [B_stage[i]], [B_wout])

        def get_piece(pi):
            if first_flag[0]:
                return convert_piece(pi, E3)
            return load_piece(pi)

        def load_piece(pi):
            si = slot_ctr[0] % 3
            slot_ctr[0] += 1
            tk.dma("sp", slots[si][:], wscr_v[pi], [B_scr[pi]], [B_slot[si]], B_slot[si])
            return slots[si], B_slot[si]

        tiles = [list(range(i, min(i + 4, NP))) for i in range(0, NP, 4)] + [[NP]]
        E3 = ("act", "dve", "pool")

        def x_loads(tile):
            for b, gb in enumerate(tile):
                if gb == NP:
                    V(lambda e: e.memset(xin[b][64:128, :], 0.0), [], [B_xin[b]])
                    tk.dma("sp", xin[b][0:64, :], xs_d[:, :], [], [B_xin[b]], B_xin[b])
                else:
                    tk.dma("sp", xin[b][:], x_d[gb * 128:(gb + 1) * 128, :], [], [B_xin[b]], B_xin[b])

        def transposes_to(src, B_src, b, B_dst, act_only=False):
            for half in range(2):
                pt, Bp = ps()
                PE([tp(pt[:, k * 128:(k + 1) * 128], src[:, (half * 4 + k) * 128:(half * 4 + k + 1) * 128], ident_f)
                    for k in range(4)], [B_src] + Cr, [Bp])
                o = xT[:, half * 4:(half + 1) * 4, b * 128:(b + 1) * 128]
                i_ = pt[:].rearrange("p (k t) -> p k t", k=4)
                if half == 0 or act_only:
                    A(lambda e: e.copy(out=o, in_=i_), [Bp], [B_dst])
                else:
                    V(lambda e: e.tensor_copy(out=o, in_=i_), [Bp], [B_dst])

        def ln_gen(items):
            for (src, B_src, gcol, bcol, dst, B_dst, k, pool_b, post) in items:
                sm, B_sm = LNSETS[k]
                st, mv, ve, lnv = sm["st"], sm["mv"], sm["ve"], sm["lnv"]
                for hh in range(2):
                    V(lambda e: e.bn_stats(out=st[:, hh * 6:(hh + 1) * 6], in_=src[:, hh * 512:(hh + 1) * 512]),
                      [B_src], [B_sm["st"]])
                V(lambda e: e.bn_aggr(out=mv[:], in_=st[:]), [B_sm["st"]], [B_sm["mv"]])
                V(lambda e: e.tensor_scalar(out=ve[:], in0=mv[:, 1:2], scalar1=LN_EPS, scalar2=None, op0=ALU.add),
                  [B_sm["mv"]], [B_sm["ve"]])
                V(lambda e: e.tensor_scalar(out=lnv[:], in0=mv[:, 0:1], scalar1=-1.0, scalar2=None, op0=ALU.mult),
                  [B_sm["mv"]], [B_sm["lnv"]])
            yield
            for (src, B_src, gcol, bcol, dst, B_dst, k, pool_b, post) in items:
                sm, B_sm = LNSETS[k]
                ve, lnv, rstd, nmr = sm["ve"], sm["lnv"], sm["rstd"], sm["nmr"]
                A(lambda e: e.activation(out=ve[:], in_=ve[:], func=AF.Ln), [B_sm["ve"]], [B_sm["ve"]])
                A(lambda e: e.activation(out=rstd[:], in_=ve[:], func=AF.Exp, scale=-0.5), [B_sm["ve"]], [B_sm["rstd"]])
                A(lambda e: e.activation(out=nmr[:], in_=lnv[:], func=AF.Identity, scale=rstd[:]),
                  [B_sm["lnv"], B_sm["rstd"]], [B_sm["nmr"]])
                A(lambda e: e.activation(out=dst[:], in_=src[:], func=AF.Identity, bias=nmr[:], scale=rstd[:]),
                  [B_src, B_sm["nmr"], B_sm["rstd"]], [B_dst])
            yield
            for (src, B_src, gcol, bcol, dst, B_dst, k, pool_b, post) in items:
                V(lambda e: e.tensor_tensor(out=dst[:], in0=dst[:], in1=cfc(gcol, 1024), op=ALU.mult), [B_dst] + Cr, [B_dst])
                (G if pool_b else V)(lambda e: e.tensor_tensor(out=dst[:], in0=dst[:], in1=cfc(bcol, 1024), op=ALU.add),
                                     [B_dst] + Cr, [B_dst])
                if post is not None:
                    post()
            yield

        ckpt(1)
        first = True
        first_flag = [True]
        E3 = ("act", "dve", "pool")
        pending = []

        deferred = []

        def run_deferred(n):
            for _ in range(n):
                if deferred:
                    try:
                        next(deferred[0])
                    except StopIteration:
                        deferred.pop(0)

        def drain(n):
            for _ in range(n):
                if pending:
                    convert_piece(pending.pop(0), E3)

        def drain_until(pi_end):
            while pending and pending[0] < pi_end:
                convert_piece(pending.pop(0), E3)

        for ti, tile in enumerate(tiles):
          try:
            nb = len(tile)
            T = nb * 128
            is_s = tile[0] == NP
            if first:
                x_loads(tile)
                for b in range(nb):
                    transposes_to(xin[b], B_xin[b], b, B_xT[b])
            Bx_all = [B_xT[b] for b in range(nb)]

            if is_s:
                pt, Bp = ps()
                tk.dma("sp", ck_sb, ck_d[:, :], [], [B_ck], B_ck)
                tk.dma("sp", cv_sb, cv_d[:, :], [], [B_cv], B_cv, chain=True)
                PE([tp(pt[:, 0:128], ck_sb, ident_f)], [B_ck] + Cr, [Bp])
                V(lambda e: e.tensor_copy(out=kT[:, 0:128], in_=pt[:, 0:128]), [Bp], [B_kT])
                V(lambda e: e.tensor_copy(out=Vt[:, 0, :], in_=cv_sb), [B_cv], [B_Vt])
            elif ti > 0:
                pnb = len(tiles[ti - 1])
                V(lambda e: e.tensor_copy(out=kT[:, 0:128], in_=kT[:, pnb * 128:(pnb + 1) * 128]), [B_kT], [B_kT])
                V(lambda e: e.tensor_copy(out=Vt[:, 0, :], in_=Vt[:, pnb, :]), [B_Vt], [B_Vt])

            ev_ctr = [0]

            def evac(out, in_, r, w):
                ev_ctr[0] += 1
                if ev_ctr[0] % 2:
                    A(lambda e: e.copy(out=out, in_=in_), r, w)
                else:
                    V(lambda e: e.tensor_copy(out=out, in_=in_), r, w)

            cs_blk = None
            for b, gb in enumerate(tile):
                if gb == NP - 1 or gb == NP:
                    cs_blk = b
            need_kv = cs_blk is not None

            def fm_chunk(slot, Bs, cc, out_ap, B_out, on_act=False):
                pt, Bp = ps()
                PE([mm(pt[:, 0:T], slot[:, kc, cc * 128:(cc + 1) * 128], xT[:, kc, 0:T], kc == 0, kc == 7)
                    for kc in range(8)], [Bs] + Bx_all, [Bp])
                if on_act:
                    A(lambda e: e.copy(out=out_ap, in_=pt[:, 0:T]), [Bp], B_out)
                else:
                    evac(out_ap, pt[:, 0:T], [Bp], B_out)

            def tm_block(slot, Bs, b, ncols):
                pt, Bp = ps()
                PE([mm(pt[:, 0:ncols], xT[:, kc, b * 128:(b + 1) * 128], slot[:, kc, 0:ncols], kc == 0, kc == 7)
                    for kc in range(8)], [Bs, B_xT[b]], [Bp])
                return pt, Bp

            for zi in range(2):
                slot, Bs = get_piece(P_IN[f"z{zi}"])
                for b in range(nb):
                    pt, Bp = tm_block(slot, Bs, b, 256)
                    A(lambda e: e.activation(out=U[:, 20 + b, zi * 256:(zi + 1) * 256], in_=pt[:, 0:256], func=AF.Silu),
                      [Bp], BU(20 + b, b))
                if zi == 0:
                    run_deferred(1)
            ckpt(3.1)
            if first:
                convert_wout()
            rows07 = sum([BU(c) for c in range(8)], [])
            if is_s:
                V(lambda e: e.tensor_copy(out=U[:, 0:8, 0:3], in_=halo_s[:]), [B_halos], rows07)
            else:
                V(lambda e: e.tensor_copy(out=U[:, 0:8, 0:3], in_=halo[:]), [B_halo], rows07)

            def conv_chunk(c):
                acc, Ba = Fb[4 + c % 2], B_F[4 + c % 2]
                A(lambda e: e.activation(out=acc[:, 0:T], in_=U[:, c, 0:T], func=AF.Identity,
                                         bias=cf[:, CBI + c:CBI + c + 1], scale=cf[:, CW + c * 4:CW + c * 4 + 1]),
                  BU(c) + Cr, [Ba])
                for i in range(1, 4):
                    V(lambda e: e.scalar_tensor_tensor(out=acc[:, 0:T], in0=U[:, c, i:i + T],
                                                       scalar=cf[:, CW + c * 4 + i:CW + c * 4 + i + 1], in1=acc[:, 0:T],
                                                       op0=ALU.mult, op1=ALU.add), BU(c) + [Ba] + Cr, [Ba])
                conv_tail.append((c, acc, Ba))

            conv_tail = []

            def conv_silu():
                if conv_tail:
                    c, acc, Ba = conv_tail.pop(0)
                    A(lambda e: e.activation(out=U[:, 8 + c, 0:T], in_=acc[:, 0:T], func=AF.Silu), [Ba], BU(8 + c))

            for xi, nm in enumerate(("x0", "x1", "B", "C")):
                slot, Bs = get_piece(P_IN[nm])
                for cc in range(2):
                    c = xi * 2 + cc
                    fm_chunk(slot, Bs, cc, U[:, c, 3:3 + T], BU(c), on_act=True)
                    conv_silu()
                    conv_chunk(c)
                if xi < 2:
                    run_deferred(1)
                if cs_blk is not None:
                    pt, Bp = tm_block(slot, Bs, cs_blk, 256)
                    evac(H[:, xi * 256:(xi + 1) * 256], pt[:, 0:256], [Bp], [B_H])
            conv_silu()
            if not is_s:
                V(lambda e: e.tensor_copy(out=halo[:], in_=U[:, 0:8, T:T + 3]), rows07, [B_halo])
            ckpt(3.2)
            if cs_blk is not None:
                r0 = 61 if is_s else 125
                tk.dma("pool", (convs_d if is_s else convp_d)[:, :], H[r0:r0 + 3, :], [B_H], [], B_H)
            ckpt(3.3)
            for qi in range(2):
                slot, Bs = get_piece(P_IN[f"q{qi}"])
                for jj in range(2):
                    j = qi * 2 + jj
                    fm_chunk(slot, Bs, jj, U[:, 16 + j, 0:T], BU(16 + j))
            slot, Bs = get_piece(P_IN["k"])
            fm_chunk(slot, Bs, 0, kT[:, 128:128 + T], [B_kT])
            ckpt(3.4)
            slot, Bs = get_piece(P_IN["kv"])
            for b in range(nb):
                pt, Bp = tm_block(slot, Bs, b, 256)
                if "noV" not in _os.environ.get("KVAR", ""):
                    V(lambda e: e.tensor_copy(out=Vt[:, b + 1, :], in_=pt[:, 128:256]), [Bp], [B_Vt])
                if need_kv and b == cs_blk and "noA" not in _os.environ.get("KVAR", ""):
                    A(lambda e: e.copy(out=ktok[:], in_=pt[:, 0:128]), [Bp], [B_ktok])
                    A(lambda e: e.copy(out=vtok[:], in_=pt[:, 128:256]), [Bp], [B_vtok])
                    if is_s:
                        tk.dma("pool", ks_d[0:64, :], ck_d[64:128, :], [], [], Buf("c2ck"))
                        tk.dma("pool", vs_d[0:64, :], cv_d[64:128, :], [], [], Buf("c2cv"))
                        tk.dma("pool", ks_d[64:128, :], ktok[0:64, :], [B_ktok], [], B_ktok)
                        tk.dma("pool", vs_d[64:128, :], vtok[0:64, :], [B_vtok], [], B_vtok)
                    elif _os.environ.get("KVAR", "") != "nodma":
                        tk.dma("pool", kp_d[:, :], ktok[:], [B_ktok], [], B_ktok)
                        tk.dma("pool", vp_d[:, :], vtok[:], [B_vtok], [], B_vtok)
            ckpt(3.5)
            slot, Bs = get_piece(P_IN["dt"])
            for b in range(nb):
                pt, Bp = tm_block(slot, Bs, b, 8)
                V(lambda e: e.tensor_tensor(out=dtpre[:, b, :], in0=pt[:, 0:8], in1=cfc(DTB, 8), op=ALU.add),
                  [Bp] + Cr, [B_dtpre])
            nb8 = nb * 8
            dp = dtpre[:, 0:nb, :]
            V(lambda e: e.scalar_tensor_tensor(out=dtw[:, 0:nb, :], in0=dp, scalar=-1.0, in1=dp, op0=ALU.mult, op1=ALU.max),
              [B_dtpre], [B_dtw])
            A(lambda e: e.activation(out=dtw[:, 0:nb, :], in_=dtw[:, 0:nb, :], func=AF.Exp, scale=-1.0), [B_dtw], [B_dtw])
            V(lambda e: e.tensor_scalar(out=dtw[:, 0:nb, :], in0=dtw[:, 0:nb, :], scalar1=1.0, scalar2=None, op0=ALU.add),
              [B_dtw], [B_dtw])
            A(lambda e: e.activation(out=dtw[:, 0:nb, :], in_=dtw[:, 0:nb, :], func=AF.Ln), [B_dtw], [B_dtw])
            V(lambda e: e.scalar_tensor_tensor(out=dtall[:, 0:nb, :], in0=dp, scalar=0.0, in1=dtw[:, 0:nb, :],
                                               op0=ALU.max, op1=ALU.add), [B_dtpre, B_dtw], [B_dtall])
            if is_s:
                V(lambda e: e.tensor_scalar(out=dtall[:, 0:nb, :], in0=dtall[:, 0:nb, :], scalar1=cf[:, TOKM:TOKM + 1],
                                            scalar2=None, op0=ALU.mult), [B_dtall] + Cr, [B_dtall])
            V(lambda e: e.tensor_tensor(out=dtAall[:, 0:nb, :], in0=dtall[:, 0:nb, :],
                                        in1=Ab[:].unsqueeze(1).broadcast_to([128, nb, 8]), op=ALU.mult),
              [B_dtall, B_Ab], [B_dtall])
            pc, Bpc = ps()
            dA2 = dtAall[:, 0:nb, :].rearrange("p b h -> p (b h)")
            PE([mm(pc[:, 0:nb8], tri_f, dA2, True, True), mm(pc[:, 32:32 + nb8], ones_f, dA2, True, True)],
               [B_dtall] + Cr, [Bpc])
            fl = lambda t_: t_[:, 0:nb, :].rearrange("p b h -> p (b h)")
            V(lambda e: e.tensor_copy(out=fl(dtw), in_=pc[:, 0:nb8]), [Bpc], [B_dtw])
            A(lambda e: e.activation(out=fl(ecumall), in_=pc[:, 0:nb8], func=AF.Exp), [Bpc], [B_dtall])
            A(lambda e: e.activation(out=fl(bdecall), in_=pc[:, 32:32 + nb8], func=AF.Exp), [Bpc], [B_dtall])
            V(lambda e: e.tensor_tensor(out=fl(dtw), in0=pc[:, 32:32 + nb8], in1=fl(dtw), op=ALU.subtract),
              [Bpc, B_dtw], [B_dtw])
            A(lambda e: e.activation(out=fl(toendall), in_=fl(dtw), func=AF.Exp), [B_dtw], [B_dtall])

            run_deferred(99)
            ckpt(4)
            if is_s:
                pt, Bp = ps()
                tk.dma("sp", sinit[:], sinit_d[:, :], [], [B_sinit], B_sinit)
                PE([tp(pt[:, c * 128:(c + 1) * 128], sinit[:, c * 128:(c + 1) * 128], ident_f) for c in range(4)],
                   [B_sinit] + Cr, [Bp])
                V(lambda e: e.tensor_copy(out=S[:], in_=pt[:]), [Bp], [B_S])
                A(lambda e: e.copy(out=Sb_[:], in_=pt[:]), [Bp], [B_Sb])

            emitted = set()

            def ssd_front(b, gb):
                t0 = b * 128
                ts = slice(t0, t0 + 128)
                par = b % 2
                (L, B_L, decay, B_decay, MT, B_MT, xdt, B_xdt, xdte, B_xdte, xsD, B_xsD, Btok, B_Btok,
                 cbm, B_cbm, gn, B_gn, FT, B_FT, FY, B_FY) = SSDSET[par]
                sm, B_sm = SMSETS[par]
                dt_ = dtall[:, b, :]
                dtA = dtAall[:, b, :]
                ecum = ecumall[:, b, :]
                toend = toendall[:, b, :]
                bdec = bdecall[:, b, :]
                B_dtT = [B_dtall]
                while b >= 2 and ('S', b - 2) not in emitted:
                    yield
                pxs, Bpxs = ps()
                pxs_b = pxs[:].bitcast(BF16)
                PE([tp(pxs_b[:, c * 128:(c + 1) * 128], U[:, 8 + c, ts], ident_b) for c in range(6)],
                   sum([BU(8 + c, b) for c in range(6)], []) + Cbr, [Bpxs])
                xs3 = pxs_b[:, 0:512].rearrange("p (h d) -> p h d", h=8)
                V(lambda e: e.tensor_tensor(out=xdt[:].rearrange("p (h d) -> p h d", h=8), in0=xs3,
                                            in1=dt_.unsqueeze(2).broadcast_to([128, 8, 64]), op=ALU.mult),
                  [Bpxs] + B_dtT, [B_xdt])
                A(lambda e: e.copy(out=Btok[:], in_=pxs_b[:, 512:768]), [Bpxs], [B_Btok])
                yield
                V(lambda e: e.tensor_tensor(out=xdte[:].rearrange("p (h d) -> p h d", h=8),
                                            in0=xdt[:].rearrange("p (h d) -> p h d", h=8),
                                            in1=toend.unsqueeze(2).broadcast_to([128, 8, 64]), op=ALU.mult),
                  [B_xdt] + B_dtT, [B_xdte])
                use_r = True
                V(lambda e: e.tensor_tensor(out=L[:, 0:4, :].bitcast(F32R),
                                            in0=mst_f.unsqueeze(1).broadcast_to([128, 4, 128]),
                                            in1=dtA[:, 0:4].unsqueeze(2).broadcast_to([128, 4, 128]), op=ALU.mult),
                  B_dtT + Cr, [B_L])
                B_La = LA.setdefault(id(B_L), Buf("La"))
                for h_ in range(4, 8):
                    A(lambda e: e.activation(out=L[:, h_, :].bitcast(F32R), in_=mst_f, func=AF.Identity,
                                             scale=dtA[:, h_:h_ + 1]), B_dtT + Cr, [B_La])
                yield
                for hh in range(2):
                    pg, Bpg = ps()
                    PE([mm(pg[:, k * 128:(k + 1) * 128],
                           (L[:, hh * 4 + k, :].bitcast(F32R) if use_r else L[:, hh * 4 + k, :]),
                           (tri_r.bitcast(F32R) if use_r else tri_f), True, True)
                        for k in range(4)], [B_L if hh == 0 else B_La, B_trir] + Cr, [Bpg])
                    A(lambda e: e.activation(out=decay[:, hh * 4:(hh + 1) * 4, :].rearrange("p a b -> p (a b)"),
                                             in_=pg[:], func=AF.Exp), [Bpg], [B_decay])
                    yield
                pcb, Bpcb = ps()
                PE([mm(pcb[:, g * 128:(g + 1) * 128], U[:, 12 + g, ts], U[:, 14 + g, ts], True, True) for g in range(2)],
                   BU(12, b) + BU(13, b) + BU(14, b) + BU(15, b), [Bpcb])
                V(lambda e: e.tensor_tensor(out=cbm[:], in0=pcb[:, 0:256].rearrange("p (g l) -> p g l", g=2),
                                            in1=cau_f.unsqueeze(1).broadcast_to([128, 2, 128]), op=ALU.mult),
                  [Bpcb] + Cr, [B_cbm])
                yield
                for g in range(2):
                    V(lambda e: e.tensor_tensor(out=MT[:, g * 4:(g + 1) * 4, :], in0=decay[:, g * 4:(g + 1) * 4, :],
                                                in1=cbm[:, g:g + 1, :].broadcast_to([128, 4, 128]), op=ALU.mult),
                      [B_decay, B_cbm], [B_MT])
                yield
                emitted.add(('front', b))
                yield

            def ssd_back(b, gb):
                t0 = b * 128
                ts = slice(t0, t0 + 128)
                par = b % 2
                (L, B_L, decay, B_decay, MT, B_MT, xdt, B_xdt, xdte, B_xdte, xsD, B_xsD, Btok, B_Btok,
                 cbm, B_cbm, gn, B_gn, FT, B_FT, FY, B_FY) = SSDSET[par]
                sm, B_sm = SMSETS[par]
                dt_ = dtall[:, b, :]
                dtA = dtAall[:, b, :]
                ecum = ecumall[:, b, :]
                toend = toendall[:, b, :]
                bdec = bdecall[:, b, :]
                B_dtT = [B_dtall]
                while ('front', b) not in emitted:
                    yield
                while b >= 1 and ('S', b - 1) not in emitted:
                    yield
                py, Bpy = ps()
                fns = [mm(py[:, c_ * 128:(c_ + 1) * 128], U[:, 8 + c_, ts], diagD[:, c_, :], c_ == 0, False) for c_ in range(4)]
                for h in range(8):
                    fns.append(mm(py[:, h * 64:(h + 1) * 64], MT[:, h, :], xdt[:, h * 64:(h + 1) * 64], False, h == 7))
                PE(fns, [B_diagD, B_MT, B_xdt] + sum([BU(8 + c_, b) for c_ in range(4)], []), [Bpy])
                pz, Bpz = ps()
                PE([mm(pz[:, g * 256:(g + 1) * 256], U[:, 14 + g, ts], Sb_[:, g * 256:(g + 1) * 256], True, True)
                    for g in range(2)], BU(14, b) + BU(15, b) + [B_Sb], [Bpz])
                t1, yv, gvv = FT, FY, FT
                V(lambda e: e.tensor_tensor(out=t1[:].rearrange("p (h d) -> p h d", h=8),
                                            in0=pz[:].rearrange("p (h d) -> p h d", h=8),
                                            in1=ecum.unsqueeze(2).broadcast_to([128, 8, 64]), op=ALU.mult),
                  [Bpz] + B_dtT, [B_FT])
                V(lambda e: e.tensor_tensor(out=yv[:], in0=py[:], in1=t1[:], op=ALU.add), [Bpy, B_FT], [B_FY])
                yield
                pn, Bpn = ps()
                PE([mm(pn[:, g * 256:(g + 1) * 256], Btok[:, g * 128:(g + 1) * 128], xdte[:, g * 256:(g + 1) * 256],
                       True, True) for g in range(2)], [B_Btok, B_xdte], [Bpn])
                V(lambda e: e.tensor_tensor(out=S[:].rearrange("p (h d) -> p h d", h=8),
                                            in0=S[:].rearrange("p (h d) -> p h d", h=8),
                                            in1=bdec.unsqueeze(2).broadcast_to([128, 8, 64]), op=ALU.mult),
                  [B_S] + B_dtT, [B_S])
                V(lambda e: e.tensor_tensor(out=S[:], in0=pn[:], in1=S[:], op=ALU.add), [Bpn, B_S], [B_S])
                A(lambda e: e.copy(out=Sb_[:], in_=S[:]), [B_S], [B_Sb])
                emitted.add(('S', b))
                if gb == NP - 1 or gb == NP:
                    pt, Bp = ps()
                    PE([tp(pt[:, c * 128:(c + 1) * 128], S[:, c * 128:(c + 1) * 128], ident_f) for c in range(4)],
                       [B_S] + Cr, [Bp])
                    V(lambda e: e.tensor_copy(out=sout[:], in_=pt[:]), [Bp], [B_sout])
                    tk.dma("pool", (ssms_d if is_s else ssmp_d)[:, :], sout[:], [B_sout], [], B_sout)
                yield
                V(lambda e: e.tensor_tensor(out=gvv[:], in0=yv[:], in1=U[:, 20 + b, 0:512], op=ALU.mult),
                  [B_FY] + BU(20 + b, b), [B_FT])
                st2, mv2, ms, lms, rs2 = sm["st2"], sm["mv2"], sm["ms"], sm["lms"], sm["rs2"]
                for g in range(2):
                    V(lambda e: e.bn_stats(out=st2[:, g * 6:(g + 1) * 6], in_=gvv[:, g * 256:(g + 1) * 256]),
                      [B_FT], [B_sm["st2"]])
                    V(lambda e: e.bn_aggr(out=mv2[:, g * 2:(g + 1) * 2], in_=st2[:, g * 6:(g + 1) * 6]),
                      [B_sm["st2"]], [B_sm["mv2"]])
                    V(lambda e: e.scalar_tensor_tensor(out=ms[:, g:g + 1], in0=mv2[:, g * 2:g * 2 + 1],
                                                       scalar=mv2[:, g * 2:g * 2 + 1], in1=mv2[:, g * 2 + 1:g * 2 + 2],
                                                       op0=ALU.mult, op1=ALU.add), [B_sm["mv2"]], [B_sm["ms"]])
                V(lambda e: e.tensor_scalar(out=ms[:], in0=ms[:], scalar1=RMS_EPS, scalar2=None, op0=ALU.add),
                  [B_sm["ms"]], [B_sm["ms"]])
                A(lambda e: e.activation(out=lms[:], in_=ms[:], func=AF.Ln), [B_sm["ms"]], [B_sm["lms"]])
                A(lambda e: e.activation(out=rs2[:], in_=lms[:], func=AF.Exp, scale=-0.5), [B_sm["lms"]], [B_sm["rs2"]])
                yield
                for g in range(2):
                    V(lambda e: e.scalar_tensor_tensor(out=gn[:, g * 256:(g + 1) * 256], in0=gvv[:, g * 256:(g + 1) * 256],
                                                       scalar=rs2[:, g:g + 1], in1=cf[:, WN + g * 256:WN + (g + 1) * 256],
                                                       op0=ALU.mult, op1=ALU.mult), [B_FT, B_sm["rs2"]] + Cr, [B_gn])
                yield
                while b >= 2 and ('outmm', b - 2) not in emitted:
                    yield
                pgt, Bpgt = ps()
                pgt_b = pgt[:].bitcast(BF16)
                PE([tp(pgt_b[:, c * 128:(c + 1) * 128], gn[:, c * 128:(c + 1) * 128], ident_b) for c in range(4)],
                   [B_gn] + Cbr, [Bpgt])
                A(lambda e: e.copy(out=mixS[b % 2][:], in_=pgt_b[:, 0:512].rearrange("p (c t) -> p c t", c=4)),
                  [Bpgt], [B_mixS[b % 2]])

                emitted.add(('ssd', b))
                yield

            def att_chain(b, gb):
                t0 = b * 128
                ts = slice(t0, t0 + 128)
                kbs = [1] if gb == 0 else [0, 1]
                for g in range(2):
                    gs = slice(g * 64, (g + 1) * 64)
                    for kb in kbs:
                        gkb = g * 2 + kb
                        kc0 = t0 + kb * 128
                        psc, Bpsc = ps()
                        PE([mm(psc[:], kT[gs, kc0:kc0 + 128], U[gs, 16:20, ts], True, False),
                            mm(psc[:], ident_b, cb[:, BIAS + (gkb * 2) * 512:BIAS + (gkb * 2 + 1) * 512], False, False),
                            mm(psc[:], ident_b, cb[:, BIAS + (gkb * 2 + 1) * 512:BIAS + (gkb * 2 + 2) * 512], False, True)],
                           [B_kT] + BU(16, b) + BU(17, b) + BU(18, b) + BU(19, b) + Cbr, [Bpsc])
                        A(lambda e: e.activation(out=PT[gkb][:], in_=psc[:], func=AF.Exp, scale=0.125),
                          [Bpsc], [B_PT[gkb]])
                        yield
                po, Bpo = ps()
                pd, Bpd = ps()
                fo, fd = [], []
                for g in range(2):
                    gs = slice(g * 64, (g + 1) * 64)
                    for ki, kb in enumerate(kbs):
                        fo.append(mm(po[gs, :], Vt[:, b + kb, gs], PT[g * 2 + kb][:], ki == 0, ki == len(kbs) - 1))
                        fd.append(mm(pd[gs, :], ones_b[:, 0:64], PT[g * 2 + kb][:], ki == 0, False))
                    fd.append(mm(pd[gs, :], ones_b[:, 0:64], sinkrow[:, g * 512:(g + 1) * 512], False, True))
                PE(fo, [B_Vt] + B_PT, [Bpo])
                PE(fd, B_PT + [B_sink] + Cbr, [Bpd])
                lnd, rcp = Fb[0], Fb[1]
                A(lambda e: e.activation(out=lnd[:], in_=pd[:], func=AF.Ln), [Bpd], [B_F[0]])
                A(lambda e: e.activation(out=rcp[:], in_=lnd[:], func=AF.Exp, scale=-1.0), [B_F[0]], [B_F[1]])
                V(lambda e: e.tensor_tensor(out=attT[:, :, ts], in0=po[:].rearrange("p (r q) -> p r q", r=4),
                                            in1=rcp[:].rearrange("p (r q) -> p r q", r=4), op=ALU.mult),
                  [Bpo, B_F[1]], [B_attT[b]])

                emitted.add(('att', b))
                yield

            def out_chain(b, gb):
                while ('ssd', b) not in emitted or ('att', b) not in emitted:
                    yield
                t0 = b * 128
                ts = slice(t0, t0 + 128)
                Hx, B_Hx = HS[b % 2]
                for dh in range(2):
                    pm, Bpm = ps()
                    PE([mm(pm[:], (mixS[b % 2][:, kc, :] if kc < 4 else attT[:, kc - 4, ts]),
                           wout[:, kc, dh * 512:(dh + 1) * 512], kc == 0, False)
                        for kc in range(8)] +
                       [mm(pm[:], aident[:], xin[b][:, dh * 512:(dh + 1) * 512], False, True)],
                       [B_mixS[b % 2], B_attT[b], B_wout, B_xin[b], B_aident], [Bpm])
                    A(lambda e: e.copy(out=Hx[:, dh * 512:(dh + 1) * 512], in_=pm[:]), [Bpm], [B_Hx])
                    yield
                emitted.add(('outmm', b))
                yield from ln_gen([(Hx, B_Hx, L1G, L1B, hb[b], B_hb[b], b, False, None)])
                transposes_to(hb[b], B_hb[b], b, B_xT[b], act_only=True)
                emitted.add(('out', b))
                yield

            def p4_gen(c0, c1, Bxs, bsel, use_load=False):
                n_ = c1 - c0
                for i in range(11):
                    gslot, Bg = get_piece(P_G0 + 2 * i)
                    uslot, Bu = get_piece(P_G0 + 2 * i + 1)
                    for jj in range(2):
                        j = i * 2 + jj
                        pgg, Bpgg = ps()
                        puu, Bpuu = ps()
                        PE([mm(pgg[:, 0:n_], gslot[:, kc, jj * 128:(jj + 1) * 128], xT[:, kc, c0:c1], kc == 0, kc == 7)
                            for kc in range(8)], [Bg] + Bxs, [Bpgg])
                        PE([mm(puu[:, 0:n_], uslot[:, kc, jj * 128:(jj + 1) * 128], xT[:, kc, c0:c1], kc == 0, kc == 7)
                            for kc in range(8)], [Bu] + Bxs, [Bpuu])
                        sg, Bsg = Fb[j % 2], B_F[j % 2]
                        A(lambda e: e.activation(out=sg[:, 0:n_], in_=pgg[:, 0:n_], func=AF.Silu), [Bpgg], [Bsg])
                        V(lambda e: e.tensor_tensor(out=U[:, j, c0:c1], in0=sg[:, 0:n_], in1=puu[:, 0:n_], op=ALU.mult),
                          [Bsg, Bpuu], BU(j, bsel))
                        yield

            def p4h0_chain():
                while ('out', 0) not in emitted or ('out', 1) not in emitted:
                    yield
                yield from p4_gen(0, 256, [B_xT[0], B_xT[1]], 0)

            interleave_p4 = False

            def seq(gens):
                for g_ in gens:
                    yield from g_

            chains = [seq([ssd_front(b, gb) for b, gb in enumerate(tile) if b % 2 == 0]),
                      seq([ssd_front(b, gb) for b, gb in enumerate(tile) if b % 2 == 1]),
                      seq([ssd_back(b, gb) for b, gb in enumerate(tile) if b % 2 == 0]),
                      seq([ssd_back(b, gb) for b, gb in enumerate(tile) if b % 2 == 1]),
                      seq([att_chain(b, gb) for b, gb in enumerate(tile)]),
                      seq([out_chain(b, gb) for b, gb in enumerate(tile) if b % 2 == 0]),
                      seq([out_chain(b, gb) for b, gb in enumerate(tile) if b % 2 == 1])]
            if interleave_p4:
                chains.append(p4h0_chain())
            steps_per_round = [1] * 8
            while chains:
                for ci_, c_ in enumerate(list(chains)):
                    try:
                        for _ in range(steps_per_round[ci_] if len(chains) >= 3 else 1):
                            next(c_)
                    except StopIteration:
                        chains.remove(c_)
            ckpt(6)

            drain_until(P_D0)

            if not interleave_p4:
                for _ in p4_gen(0, T, Bx_all, None):
                    pass
            else:
                for _ in p4_gen(256, 512, [B_xT[2], B_xT[3]], 2, use_load=True):
                    pass

            ckpt(7)
            drain_until(NPIECE)

            if ti + 1 < len(tiles):
                nxt = tiles[ti + 1]
                x_loads(nxt)
                for b in range(len(nxt)):
                    transposes_to(xin[b], B_xin[b], b, B_xT[b])

            for dq in range(4):
                pbs = [ps() for _ in range(nb)]
                for fp in range(3):
                    dslot, Bd = get_piece(P_D0 + dq * 3 + fp)
                    nf = 8 if fp < 2 else 6
                    for b in range(nb):
                        pb, Bpb = pbs[b]
                        PE([mm(pb[:, 0:256], U[:, fp * 8 + k, b * 128:(b + 1) * 128], dslot[:, k, :],
                               fp == 0 and k == 0, fp == 2 and k == nf - 1) for k in range(nf)],
                           [Bd] + sum([BU(fp * 8 + k, b) for k in range(nf)], []), [Bpb])
                for b in range(nb):
                    pb, Bpb = pbs[b]
                    V(lambda e: e.scalar_tensor_tensor(out=hb[b][:, dq * 256:(dq + 1) * 256],
                                                       in0=hb[b][:, dq * 256:(dq + 1) * 256], scalar=ALPHA,
                                                       in1=pb[:, 0:256], op0=ALU.mult, op1=ALU.add),
                      [B_hb[b], Bpb], [B_hb[b]])
            def mk_store(b, gb, is_s_):
                def post():
                    if is_s_:
                        tk.dma("pool", ys_d[:, :], hb[b][0:64, :], [B_hb[b]], [], B_hb[b])
                    else:
                        tk.dma("pool", y_d[gb * 128:(gb + 1) * 128, :], hb[b][:], [B_hb[b]], [], B_hb[b])
                return post

            deferred.append(ln_gen([(hb[b], B_hb[b], L2G, L2B, hb[b], B_hb[b], b, False, mk_store(b, gb, is_s))
                                    for b, gb in enumerate(tile)]))
            run_deferred(3)
            first = False
            first_flag[0] = False
          except _Stop:
            break

        run_deferred(99)
        tk.finish()
    return nc


def _consts():
    cf = np.zeros((128, NCF), np.float32)
    i = np.arange(128)
    cf[:, IDF:IDF + 128] = np.eye(128)
    cf[:, TRI:TRI + 128] = (i[:, None] <= i[None, :])
    cf[:, MST:MST + 128] = (i[None, :] < i[:, None])
    cf[:, CAU:CAU + 128] = (i[:, None] <= i[None, :])
    cf[:, ONE:ONE + 128] = 1.0
    cf[:, TOKM] = (i < 64)
    cb = np.zeros((128, NCB), np.float32)
    cb[:, IDB:IDB + 128] = np.eye(128)
    cb[:, ONB:ONB + 128] = 1.0
    lo_part = np.zeros((128, NCB), np.float32)
    s = i[:, None].astype(np.float64)
    q = i[None, :].astype(np.float64)
    for g in range(2):
        for kb in range(2):
            gkb = g * 2 + kb
            tile_ = np.zeros((128, 512), np.float64)
            for r in range(4):
                m = 2.0 ** -(g * 4 + r + 1)
                if kb == 0:
                    dist = 128.0 + q - s
                    bad = (q >= 64) & (s < 64)
                else:
                    dist = np.abs(q - s)
                    bad = (q < 64) & (s >= 64)
                v = -8.0 * m * dist
                v = np.where(bad, -32768.0, v)
                tile_[:, r * 128:(r + 1) * 128] = v
            hi = tile_.astype(np.float32).astype(ml_dtypes.bfloat16)
            lo = (tile_ - hi.astype(np.float64)).astype(np.float32).astype(ml_dtypes.bfloat16)
            o = BIAS + (gkb * 2) * 512
            cb[:, o:o + 512] = hi.astype(np.float32)
            cb[:, o + 512:o + 1024] = lo.astype(np.float32)
    return cf, cb.astype(ml_dtypes.bfloat16)


_NC_CACHE = {}
_SIM_HOOK = None


def _run(NP, x_prompt, x_sample, state_conv, state_ssm, cache_k, cache_v, w_in, conv_w, conv_b, dt_bias,
         a_log, d_skip, ssm_norm_w, attn_sinks, w_out, ln1_g, ln1_b, w_gate, w_up, w_down, ln2_g, ln2_b):
    f = np.float32
    ncores = x_prompt.shape[0]
    cf, cb = _consts()
    cf[:, CW:CW + 32] = np.asarray(conv_w[0], f).reshape(4, 8, 128).transpose(2, 1, 0).reshape(128, 32)
    cf[:, CBI:CBI + 8] = np.asarray(conv_b[0], f).reshape(8, 128).T
    cf[:, DTB:DTB + 8] = np.broadcast_to(np.asarray(dt_bias[0], f), (128, 8))
    cf[:, ALOG:ALOG + 8] = np.broadcast_to(np.asarray(a_log[0], f), (128, 8))
    cf[:, DFULL:DFULL + 512] = np.broadcast_to(np.repeat(np.asarray(d_skip[0], f), 64), (128, 512))
    cf[:, DCOL:DCOL + 4] = np.repeat(np.asarray(d_skip[0], f), 64).reshape(4, 128).T
    cf[:, WN:WN + 512] = np.broadcast_to(np.asarray(ssm_norm_w[0], f), (128, 512))
    cf[:, L1G:L1G + 1024] = np.broadcast_to(np.asarray(ln1_g[0], f), (128, 1024))
    cf[:, L1B:L1B + 1024] = np.broadcast_to(np.asarray(ln1_b[0], f), (128, 1024))
    cf[:, L2G:L2G + 1024] = np.broadcast_to(np.asarray(ln2_g[0], f), (128, 1024))
    cf[:, L2B:L2B + 1024] = np.broadcast_to(np.asarray(ln2_b[0], f), (128, 1024))
    sink = np.ascontiguousarray(np.broadcast_to(np.repeat(np.asarray(attn_sinks[0], f), 128), (128, 1024)))
    if NP not in _NC_CACHE:
        _NC_CACHE[NP] = build(NP)
    nc = _NC_CACHE[NP]
    shared = {"w_in": np.ascontiguousarray(w_in[0], f), "w_out": np.ascontiguousarray(w_out[0], f),
              "w_gate": np.ascontiguousarray(w_gate[0], f), "w_up": np.ascontiguousarray(w_up[0], f),
              "w_down": np.ascontiguousarray(w_down[0], f), "cf": cf, "cb": cb, "sink": sink}
    in_maps = []
    for c in range(ncores):
        m = dict(shared)
        m["x"] = np.ascontiguousarray(x_prompt[c], f)
        m["xs"] = np.ascontiguousarray(x_sample[c], f)
        m["halo_s"] = np.ascontiguousarray(np.asarray(state_conv[0, c], f).reshape(3, 8, 128).transpose(2, 1, 0)).reshape(128, 24)
        m["sinit"] = np.ascontiguousarray(np.asarray(state_ssm[0, c], f).reshape(4, 128, 128).transpose(1, 0, 2)).reshape(128, 512)
        m["ck"] = np.ascontiguousarray(np.asarray(cache_k[0, c], f).reshape(128, 128))
        m["cv"] = np.ascontiguousarray(np.asarray(cache_v[0, c], f).reshape(128, 128))
        in_maps.append(m)
    if _SIM_HOOK is not None:
        R = _SIM_HOOK(nc, in_maps)
    else:
        res = run_bass_kernel_spmd(nc, in_maps, core_ids=list(range(ncores)))
        R = res.results

    def ssm(a):
        return np.asarray(a, f).reshape(128, 4, 128).transpose(1, 0, 2).reshape(8, 64, 128)

    y_p = np.stack([np.asarray(r["y"], f) for r in R])
    y_s = np.stack([np.asarray(r["ys"], f) for r in R])
    conv_p = np.stack([np.asarray(r["conv_p"], f) for r in R])[None]
    ssm_p = np.stack([ssm(r["ssm_p"]) for r in R])[None]
    k_p = np.stack([np.asarray(r["k_p"], f).reshape(128, 2, 64) for r in R])[None]
    v_p = np.stack([np.asarray(r["v_p"], f).reshape(128, 2, 64) for r in R])[None]
    conv_s = np.stack([np.asarray(r["conv_s"], f) for r in R])[None]
    ssm_s = np.stack([ssm(r["ssm_s"]) for r in R])[None]
    k_s = np.stack([np.asarray(r["k_s"], f).reshape(128, 2, 64) for r in R])[None]
    v_s = np.stack([np.asarray(r["v_s"], f).reshape(128, 2, 64) for r in R])[None]
    return (y_p, y_s, conv_p, ssm_p, k_p, v_p, conv_s, ssm_s, k_s, v_s)


def kernel(**inputs):
    inputs = {k: np.asarray(v) for k, v in inputs.items()}
    NP = inputs["x_prompt"].shape[1] // 128
    return _run(NP, **inputs)
```

```python
import numpy as np
import ml_dtypes
from contextlib import ExitStack
import concourse.bass as bass
import concourse.mybir as mybir
from concourse.bass_utils import run_bass_kernel_spmd

F32 = mybir.dt.float32
BF16 = mybir.dt.bfloat16
F32R = mybir.dt.float32r
AF = mybir.ActivationFunctionType
ALU = mybir.AluOpType

ALPHA = 2.0 ** 0.25
LN_EPS = 1e-5
RMS_EPS = 1e-5

IDF, TRI, MST, CAU, ONE, CW, CBI, DTB, ALOG, TOKM, DCOL = 0, 128, 256, 384, 512, 640, 672, 680, 688, 696, 697
DFULL, WN, L1G, L1B, L2G, L2B, NCF = 704, 1216, 1728, 2752, 3776, 4800, 5824
IDB, ONB, BIAS, NCB = 0, 128, 256, 4352


class Ev:
    __slots__ = ("sem", "val")

    def __init__(self, sem, val):
        self.sem = sem
        self.val = val


class Buf:
    def __init__(self, name, excl=False):
        self.name = name
        self.excl = excl
        self.w = None
        self.r = {}
        self.dsem = {}
        self.dcnt = {}


class TK:
    def __init__(self, nc, es):
        self.nc = nc
        self.es = es
        self.E = {"pe": nc.tensor, "act": nc.scalar, "dve": nc.vector, "pool": nc.gpsimd, "sp": nc.sync}
        self.sem = {k: es.enter_context(nc.semaphore("sem_" + k)) for k in ("pe", "act", "dve", "pool")}
        self.cnt = {k: 0 for k in self.sem}
        self.seen = {k: {} for k in self.E}
        self.dbufs = []

    def _wait(self, e, ev):
        if ev is None:
            return
        if e == "pe" and ev.sem is self.sem["pe"]:
            return
        k = id(ev.sem)
        if self.seen[e].get(k, 0) >= ev.val:
            return
        self.E[e].wait_ge(ev.sem, ev.val)
        self.seen[e][k] = ev.val

    def _deps(self, e, reads, writes):
        for b in reads:
            self._wait(e, b.w)
            if b.excl:
                for r in b.r.values():
                    self._wait(e, r)
        for b in writes:
            self._wait(e, b.w)
            for r in b.r.values():
                self._wait(e, r)

    def _rec(self, ev, reads, writes):
        for b in reads:
            b.r[id(ev.sem)] = ev
        for b in writes:
            b.w = ev
            b.r = {}

    def op(self, e, fn, reads, writes):
        self._deps(e, reads, writes)
        inst = fn(self.E[e])
        self.cnt[e] += 1
        inst.then_inc(self.sem[e], 1)
        self._rec(Ev(self.sem[e], self.cnt[e]), reads, writes)

    def group(self, e, fns, reads, writes):
        self._deps(e, reads, writes)
        inst = None
        for fn in fns:
            inst = fn(self.E[e])
        self.cnt[e] += 1
        inst.then_inc(self.sem[e], 1)
        self._rec(Ev(self.sem[e], self.cnt[e]), reads, writes)

    def dma(self, q, out, in_, reads, writes, sb, chain=False):
        if chain and sb.dsem.get(q) is not None:
            saved = [(b, b.w) for b in writes if b.w is not None and b.w.sem is sb.dsem[q]]
            for b, _ in saved:
                b.w = None
            self._deps(q, reads, writes)
            for b, w in saved:
                b.w = w
        else:
            self._deps(q, reads, writes)
        if q not in sb.dsem:
            sb.dsem[q] = self.es.enter_context(self.nc.semaphore("d_" + q + "_" + sb.name))
            sb.dcnt[q] = 0
            self.dbufs.append((sb, q))
        inst = self.E[q].dma_start(out=out, in_=in_)
        sb.dcnt[q] += 16
        inst.then_inc(sb.dsem[q], 16)
        self._rec(Ev(sb.dsem[q], sb.dcnt[q]), reads, writes)

    def finish(self):
        for e in ("pool", "sp"):
            for sb, q in self.dbufs:
                self._wait(e, Ev(sb.dsem[q], sb.dcnt[q]))


class _Stop(Exception):
    pass


def build(NP=32):
    import os
    KSTOP = float(os.environ.get("KSTOP", "99"))

    def ckpt(n):
        if n >= KSTOP:
            raise _Stop()

    nc = bass.Bass("TRN2", target_bir_lowering=False)
    TALL = NP * 128

    def din(name, shape, dt=F32):
        return nc.dram_tensor(name, shape, dt, kind="ExternalInput").ap()

    def dout(name, shape, dt=F32):
        return nc.dram_tensor(name, shape, dt, kind="ExternalOutput").ap()

    x_d = din("x", [TALL, 1024])
    xs_d = din("xs", [64, 1024])
    w_in_d = din("w_in", [1024, 2312])
    w_out_d = din("w_out", [1024, 1024])
    w_gate_d = din("w_gate", [1024, 2816])
    w_up_d = din("w_up", [1024, 2816])
    w_down_d = din("w_down", [2816, 1024])
    cf_d = din("cf", [128, NCF])
    cb_d = din("cb", [128, NCB], BF16)
    sink_d = din("sink", [128, 1024])
    halo_d = din("halo_s", [128, 24])
    sinit_d = din("sinit", [128, 512])
    ck_d = din("ck", [128, 128])
    cv_d = din("cv", [128, 128])
    y_d = dout("y", [TALL, 1024])
    ys_d = dout("ys", [64, 1024])
    convp_d = dout("conv_p", [3, 1024])
    ssmp_d = dout("ssm_p", [128, 512])
    kp_d = dout("k_p", [128, 128])
    vp_d = dout("v_p", [128, 128])
    convs_d = dout("conv_s", [3, 1024])
    ssms_d = dout("ssm_s", [128, 512])
    ks_d = dout("k_s", [128, 128])
    vs_d = dout("v_s", [128, 128])

    NPIECE = 11 + 22 + 12
    wscr = nc.dram_tensor("wscr", [NPIECE, 128, 2048], BF16, kind="Internal").ap()

    es = ExitStack()
    with es:
        import os as _os
        for _i in range(int(_os.environ.get("KDUMMY", "0"))):
            es.enter_context(nc.semaphore(f"dummy{_i}"))
        tk = TK(nc, es)

        def sb(name, shape, dt=F32):
            return es.enter_context(nc.sbuf_tensor("sb_" + name, shape, dt))

        cf = sb("cf", [128, NCF]); B_cf = Buf("cf")
        cb = sb("cb", [128, NCB], BF16); B_cb = Buf("cb")
        sinkrow = sb("sinkrow", [128, 1024], BF16); B_sink = Buf("sinkrow")
        wout = sb("wout", [128, 8, 1024], BF16); B_wout = Buf("wout")
        slots = [sb(f"slot{i}", [128, 8, 256], BF16) for i in range(3)]
        B_slot = [Buf(f"slot{i}") for i in range(3)]
        NSTG = 3
        stages = [sb(f"stage{i}", [128, 8, 128]) for i in range(NSTG)]
        B_stage = [Buf(f"stage{i}") for i in range(NSTG)]
        xin = [sb(f"xin{i}", [128, 1024]) for i in range(4)]
        B_xin = [Buf(f"xin{i}") for i in range(4)]
        hb = [sb(f"hb{i}", [128, 1024]) for i in range(4)]
        B_hb = [Buf(f"hb{i}") for i in range(4)]
        xT = sb("xT", [128, 8, 512], BF16)
        B_xT = [Buf(f"xT{i}") for i in range(4)]
        U = sb("U", [128, 24, 515], BF16)
        B_UH = [[Buf(f"U{i}a"), Buf(f"U{i}b")] for i in range(24)]

        def BU(r, b=None):
            return list(B_UH[r]) if b is None else [B_UH[r][b // 2]]
        kT = sb("kT", [128, 640], BF16); B_kT = Buf("kT")
        Vt = sb("Vt", [128, 5, 128], BF16); B_Vt = Buf("Vt")
        ktok = sb("ktok", [128, 128]); B_ktok = Buf("ktok")
        vtok = sb("vtok", [128, 128]); B_vtok = Buf("vtok")
        dtpre = sb("dtpre", [128, 4, 8]); B_dtpre = Buf("dtpre")
        halo = sb("halo", [128, 8, 3], BF16); B_halo = Buf("halo")
        halo_s = sb("halo_s", [128, 8, 3]); B_halos = Buf("halos")
        S = sb("S", [128, 512]); B_S = Buf("S")
        Sb_ = sb("Sb", [128, 512], BF16); B_Sb = Buf("Sb")
        Fb = [sb(f"F{i}", [128, 512]) for i in range(6)]
        B_F = [Buf(f"F{i}") for i in range(6)]
        H = sb("H", [128, 1024]); B_H = Buf("H")
        H2 = sb("H2", [128, 1024]); B_H2 = Buf("H2")
        HS = [(H, B_H), (H2, B_H2)]
        L = sb("L", [128, 8, 128]); B_L = Buf("L")
        decay = sb("decay", [128, 8, 128]); B_decay = Buf("decay")
        MT = sb("MT", [128, 8, 128], BF16); B_MT = Buf("MT")
        xdt = sb("xdt", [128, 512], BF16); B_xdt = Buf("xdt")
        xdte = sb("xdte", [128, 512], BF16); B_xdte = Buf("xdte")
        xsD = sb("xsD", [128, 512], BF16); B_xsD = Buf("xsD")
        Btok = sb("Btok", [128, 256], BF16); B_Btok = Buf("Btok")
        cbm = sb("cbm", [128, 2, 128]); B_cbm = Buf("cbm")
        gn = sb("gn", [128, 512], BF16); B_gn = Buf("gn")
        PT = [sb(f"PT{i}", [128, 512], BF16) for i in range(4)]
        B_PT = [Buf(f"PT{i}") for i in range(4)]
        mixS = [sb(f"mixS{i}", [128, 4, 128], BF16) for i in range(2)]
        B_mixS = [Buf(f"mixS{i}") for i in range(2)]
        attT = sb("attT", [128, 4, 512], BF16)
        B_attT = [Buf(f"attT{i}") for i in range(4)]
        Ab = sb("Ab", [128, 8]); B_Ab = Buf("Ab")
        sm = {}
        B_sm = {}
        SMW = {}
        for nm, w in (("dt", 8), ("ab", 8), ("e1", 8), ("l1", 8), ("dtA", 8), ("cum", 8), ("ecum", 8), ("dd", 8),
                      ("toend", 8), ("bdec", 8), ("st", 12), ("mv", 2), ("ve", 1), ("lnv", 1), ("rstd", 1),
                      ("nmr", 1), ("st2", 12), ("mv2", 4), ("ms", 2), ("lms", 2), ("rs2", 2)):
            sm[nm] = sb("sm_" + nm, [128, w])
            SMW[nm] = w
            B_sm[nm] = Buf("sm_" + nm)

        sout = Fb[0]; B_sout = B_F[0]
        sinit = Fb[0]; B_sinit = B_F[0]
        ck_sb = Fb[1][:, 0:128]; cv_sb = Fb[1][:, 128:256]; B_ck = B_F[1]; B_cv = B_F[1]
        tri_rt = sb("tri_r", [128, 128]); B_trir = Buf("tri_r")
        diagD = sb("diagD", [128, 4, 128], BF16); B_diagD = Buf("diagD")
        aident = sb("aident", [128, 128]); B_aident = Buf("aident")
        dtw = sb("dtw", [128, 4, 8]); B_dtw = Buf("dtw")
        dtall = sb("dtall", [128, 4, 8]); B_dtall = Buf("dtall")
        dtAall = sb("dtAall", [128, 4, 8])
        ecumall = sb("ecumall", [128, 4, 8])
        toendall = sb("toendall", [128, 4, 8])
        bdecall = sb("bdecall", [128, 4, 8])
        L1 = sb("L1", [128, 8, 128]); B_L1 = Buf("L1")
        MT1 = sb("MT1", [128, 8, 128], BF16); B_MT1 = Buf("MT1")
        xdt1 = sb("xdt1", [128, 512], BF16); B_xdt1 = Buf("xdt1")
        xdte1 = sb("xdte1", [128, 512], BF16); B_xdte1 = Buf("xdte1")
        xsD1 = sb("xsD1", [128, 512], BF16); B_xsD1 = Buf("xsD1")
        Btok1 = sb("Btok1", [128, 256], BF16); B_Btok1 = Buf("Btok1")
        cbm1 = sb("cbm1", [128, 2, 128]); B_cbm1 = Buf("cbm1")
        gn1 = sb("gn1", [128, 512], BF16); B_gn1 = Buf("gn1")
        sm1 = {}
        B_sm1 = {}
        for nm in list(sm.keys()):
            sm1[nm] = sb("sm1_" + nm, [128, SMW[nm]])
            B_sm1[nm] = Buf("sm1_" + nm)
        SSDSET = [
            (L, B_L, decay, B_decay, MT, B_MT, xdt, B_xdt, xdte, B_xdte, xsD, B_xsD, Btok, B_Btok,
             cbm, B_cbm, gn, B_gn, Fb[2], B_F[2], Fb[3], B_F[3]),
            (L1, B_L1, stages[1], B_stage[1], MT1, B_MT1, xdt1, B_xdt1, xdte1, B_xdte1, xsD1, B_xsD1,
             Btok1, B_Btok1, cbm1, B_cbm1, gn1, B_gn1, Fb[4], B_F[4], Fb[5], B_F[5]),
        ]
        SMSETS = [(sm, B_sm), (sm1, B_sm1)]
        LA = {}
        LNSETS = []
        for k_ in range(4):
            d_, B_d = {}, {}
            for nm in ("st", "mv", "ve", "lnv", "rstd", "nmr"):
                d_[nm] = sb(f"ln{k_}_{nm}", [128, SMW[nm]])
                B_d[nm] = Buf(f"ln{k_}_{nm}")
            LNSETS.append((d_, B_d))

        banks = [es.enter_context(nc.psum_tensor(f"bank{i}", [128, 512], F32)) for i in range(8)]
        B_bank = [Buf(f"bank{i}", excl=True) for i in range(8)]
        bank_ctr = [0]

        def ps():
            i = bank_ctr[0] % 8
            bank_ctr[0] += 1
            return banks[i], B_bank[i]

        def V(fn, r, w):
            tk.op("dve", fn, r, w)

        def A(fn, r, w):
            tk.op("act", fn, r, w)

        def G(fn, r, w):
            tk.op("pool", fn, r, w)

        def PE(fns, r, w):
            tk.group("pe", fns, r, w)

        def mm(out, lhsT, rhs, start, stop):
            return lambda e: e.matmul(out, lhsT, rhs, start=start, stop=stop)

        def tp(out, in_, ident):
            return lambda e: e.transpose(out, in_, ident)

        def cfc(off, n):
            return cf[:, off:off + n]

        ident_f = cfc(IDF, 128)
        tri_f = cfc(TRI, 128)
        mst_f = cfc(MST, 128)
        cau_f = tri_f
        tri_r = tri_rt[:]
        ones_f = cfc(ONE, 128)
        ident_b = cb[:, IDB:IDB + 128]
        ones_b = cb[:, ONB:ONB + 128]
        Cr = [B_cf]
        Cbr = [B_cb]

        tk.dma("sp", cf[:], cf_d[:, :], [], [B_cf], B_cf)
        tk.dma("sp", cb[:], cb_d[:, :], [], [B_cb], B_cb)
        tk.dma("sp", H[:], sink_d[:, :], [], [B_H], B_H)
        tk.dma("sp", halo_s[:].rearrange("p a b -> p (a b)"), halo_d[:, :], [], [B_halos], B_halos)
        A(lambda e: e.activation(out=Ab[:], in_=cfc(ALOG, 8), func=AF.Exp), Cr, [B_Ab])
        V(lambda e: e.tensor_scalar(out=Ab[:], in0=Ab[:], scalar1=-1.0, scalar2=None, op0=ALU.mult), [B_Ab], [B_Ab])
        MTf = MT[:].rearrange("p a b -> p (a b)")
        A(lambda e: e.activation(out=H[:], in_=H[:], func=AF.Exp), [B_H], [B_H])
        V(lambda e: e.tensor_copy(out=MTf, in_=H[:]), [B_H], [B_MT])
        V(lambda e: e.tensor_tensor(out=H[:], in0=H[:], in1=MTf, op=ALU.subtract), [B_H, B_MT], [B_H])
        V(lambda e: e.memset(sinkrow[:], 0.0), [], [B_sink])
        V(lambda e: e.tensor_copy(out=sinkrow[0:1, :], in_=MTf[0:1, :]), [B_MT], [B_sink])
        V(lambda e: e.tensor_copy(out=sinkrow[32:33, :], in_=H[32:33, :]), [B_H], [B_sink])
        V(lambda e: e.tensor_scalar(out=aident[:], in0=ident_f, scalar1=ALPHA, scalar2=None, op0=ALU.mult), Cr, [B_aident])
        V(lambda e: e.memset(S[:], 0.0), [], [B_S])
        V(lambda e: e.memset(Sb_[:], 0.0), [], [B_Sb])
        V(lambda e: e.memset(halo[:], 0.0), [], [B_halo])
        for c_ in range(4):
            A(lambda e: e.activation(out=diagD[:, c_, :], in_=ident_b, func=AF.Identity, scale=cf[:, DCOL + c_:DCOL + c_ + 1]),
              Cr + Cbr, [B_diagD])
        V(lambda e: e.tensor_scalar(out=tri_r.bitcast(F32R), in0=tri_f, scalar1=1.0, scalar2=None, op0=ALU.mult),
          Cr, [B_trir])

        wv = w_in_d.rearrange("(kc p) c -> p kc c", p=128)
        qv = w_in_d[:, 1544:2056].rearrange("(kc p) (g j d) -> p kc j g d", p=128, g=2, j=4, d=64)
        gv_ = w_gate_d.rearrange("(kc p) c -> p kc c", p=128)
        uv_ = w_up_d.rearrange("(kc p) c -> p kc c", p=128)
        dv_ = w_down_d.rearrange("(fc p) c -> p fc c", p=128)

        pieces = []

        def simple(name, view, c0, ncols):
            sts = []
            for o in range(0, ncols, 128):
                n = min(128, ncols - o)
                sts.append((o, n, 8, [(view[:, :, c0 + o:c0 + o + n], 0, n)]))
            pieces.append((name, sts))

        simple("z0", wv, 0, 256)
        simple("z1", wv, 256, 256)
        simple("x0", wv, 512, 256)
        simple("x1", wv, 768, 256)
        simple("B", wv, 1024, 256)
        simple("C", wv, 1280, 256)
        for qi in range(2):
            sts = []
            for jj in range(2):
                j = qi * 2 + jj
                sts.append((jj * 128, 128, 8, [(qv[:, :, j, 0, :], 0, 64), (qv[:, :, j, 1, :], 64, 128)]))
            pieces.append((f"q{qi}", sts))
        simple("k", wv, 2056, 128)
        simple("kv", wv, 2056, 256)
        simple("dt", wv, 1536, 8)
        P_IN = {n: i for i, (n, _) in enumerate(pieces)}
        for i in range(11):
            simple(f"g{i}", gv_, i * 256, 256)
            simple(f"u{i}", uv_, i * 256, 256)
        P_G0 = 11
        for dq in range(4):
            for fp in range(3):
                f0 = fp * 8
                nf = 8 if fp < 2 else 6
                sts = []
                for o in (0, 128):
                    sts.append((o, 128, nf, [(dv_[:, f0:f0 + nf, dq * 256 + o:dq * 256 + o + 128], 0, 128)]))
                pieces.append((f"d{dq}_{fp}", sts))
        P_D0 = 33
        assert len(pieces) == NPIECE
        B_scr = [Buf(f"scr{i}") for i in range(NPIECE)]
        wscr_v = [wscr[i].rearrange("p (k c) -> p k c", c=256) for i in range(NPIECE)]

        st_ctr = [0]
        slot_ctr = [0]

        def stage_load(srcs, nk):
            i = st_ctr[0] % NSTG
            st_ctr[0] += 1
            for n_, (src, lo, hi) in enumerate(srcs):
                tk.dma("sp", stages[i][:, 0:nk, lo:hi], src, [], [B_stage[i]], B_stage[i], chain=n_ > 0)
            return i

        def cast(eng, out, in_, r, w):
            if eng == "act":
                A(lambda e: e.copy(out=out, in_=in_), r, w)
            elif eng == "dve":
                V(lambda e: e.tensor_copy(out=out, in_=in_), r, w)
            else:
                G(lambda e: e.tensor_copy(out=out, in_=in_), r, w)

        def convert_piece(pi, engs):
            name, sts = pieces[pi]
            si = slot_ctr[0] % 3
            slot_ctr[0] += 1
            for n_, (o, n, nk, srcs) in enumerate(sts):
                i = stage_load(srcs, nk)
                cast(engs[n_ % len(engs)], slots[si][:, 0:nk, o:o + n], stages[i][:, 0:nk, 0:n],
                     [B_stage[i]], [B_slot[si]])
            tk.dma("pool", wscr_v[pi], slots[si][:], [B_slot[si]], [B_scr[pi]], B_slot[si])
            return slots[si], B_slot[si]

        def convert_wout():
            for cbk in range(8):
                cs = slice(cbk * 128, (cbk + 1) * 128)
                i = st_ctr[0] % NSTG
                st_ctr[0] += 1
                v0 = w_out_d[0:512, :].rearrange("(kc p) c -> p kc c", p=128)
                tk.dma("sp", stages[i][:, 0:4, :], v0[:, :, cs], [], [B_stage[i]], B_stage[i])
                v1 = w_out_d[512:1024, :].rearrange("(g r d) c -> d g r c", g=2, r=4, d=64)
                for g in range(2):
                    tk.dma("sp", stages[i][g * 64:(g + 1) * 64, 4:8, :], v1[:, g, :, cs], [], [B_stage[i]], B_stage[i],
                           chain=True)
                cast(("act", "dve")[cbk % 2], wout[:, :, cs], stages[i][:], [B_stage[i]], [B_wout])

        def get_piece(pi):
            if first_flag[0]:
                return convert_piece(pi, ("act", "dve") if pi < P_G0 else E3)
            return load_piece(pi)

        def load_piece(pi):
            si = slot_ctr[0] % 3
            slot_ctr[0] += 1
            tk.dma("sp", slots[si][:], wscr_v[pi], [B_scr[pi]], [B_slot[si]], B_slot[si])
            return slots[si], B_slot[si]

        tiles = [list(range(i, min(i + 4, NP))) for i in range(0, NP, 4)] + [[NP]]
        E3 = ("act", "dve", "pool")

        def x_loads(tile):
            for b, gb in enumerate(tile):
                if gb == NP:
                    V(lambda e: e.memset(xin[b][64:128, :], 0.0), [], [B_xin[b]])
                    tk.dma("sp", xin[b][0:64, :], xs_d[:, :], [], [B_xin[b]], B_xin[b])
                else:
                    tk.dma("sp", xin[b][:], x_d[gb * 128:(gb + 1) * 128, :], [], [B_xin[b]], B_xin[b])

        def transposes_to(src, B_src, b, B_dst, act_only=False):
            for half in range(2):
                pt, Bp = ps()
                PE([tp(pt[:, k * 128:(k + 1) * 128], src[:, (half * 4 + k) * 128:(half * 4 + k + 1) * 128], ident_f)
                    for k in range(4)], [B_src] + Cr, [Bp])
                o = xT[:, half * 4:(half + 1) * 4, b * 128:(b + 1) * 128]
                i_ = pt[:].rearrange("p (k t) -> p k t", k=4)
                if half == 0 or act_only:
                    A(lambda e: e.copy(out=o, in_=i_), [Bp], [B_dst])
                else:
                    V(lambda e: e.tensor_copy(out=o, in_=i_), [Bp], [B_dst])

        def ln_gen(items):
            for (src, B_src, gcol, bcol, dst, B_dst, k, pool_b, post) in items:
                sm, B_sm = LNSETS[k]
                st, mv, ve, lnv = sm["st"], sm["mv"], sm["ve"], sm["lnv"]
                for hh in range(2):
                    V(lambda e: e.bn_stats(out=st[:, hh * 6:(hh + 1) * 6], in_=src[:, hh * 512:(hh + 1) * 512]),
                      [B_src], [B_sm["st"]])
                V(lambda e: e.bn_aggr(out=mv[:], in_=st[:]), [B_sm["st"]], [B_sm["mv"]])
                V(lambda e: e.tensor_scalar(out=ve[:], in0=mv[:, 1:2], scalar1=LN_EPS, scalar2=None, op0=ALU.add),
                  [B_sm["mv"]], [B_sm["ve"]])
                V(lambda e: e.tensor_scalar(out=lnv[:], in0=mv[:, 0:1], scalar1=-1.0, scalar2=None, op0=ALU.mult),
                  [B_sm["mv"]], [B_sm["lnv"]])
            yield
            for (src, B_src, gcol, bcol, dst, B_dst, k, pool_b, post) in items:
                sm, B_sm = LNSETS[k]
                ve, lnv, rstd, nmr = sm["ve"], sm["lnv"], sm["rstd"], sm["nmr"]
                A(lambda e: e.activation(out=ve[:], in_=ve[:], func=AF.Ln), [B_sm["ve"]], [B_sm["ve"]])
                A(lambda e: e.activation(out=rstd[:], in_=ve[:], func=AF.Exp, scale=-0.5), [B_sm["ve"]], [B_sm["rstd"]])
                A(lambda e: e.activation(out=nmr[:], in_=lnv[:], func=AF.Identity, scale=rstd[:]),
                  [B_sm["lnv"], B_sm["rstd"]], [B_sm["nmr"]])
                A(lambda e: e.activation(out=dst[:], in_=src[:], func=AF.Identity, bias=nmr[:], scale=rstd[:]),
                  [B_src, B_sm["nmr"], B_sm["rstd"]], [B_dst])
            yield
            for (src, B_src, gcol, bcol, dst, B_dst, k, pool_b, post) in items:
                V(lambda e: e.tensor_tensor(out=dst[:], in0=dst[:], in1=cfc(gcol, 1024), op=ALU.mult), [B_dst] + Cr, [B_dst])
                (G if pool_b else V)(lambda e: e.tensor_tensor(out=dst[:], in0=dst[:], in1=cfc(bcol, 1024), op=ALU.add),
                                     [B_dst] + Cr, [B_dst])
                if post is not None:
                    post()
            yield

        ckpt(1)
        first = True
        first_flag = [True]
        E3 = ("act", "dve", "pool")
        pending = []

        deferred = []

        def run_deferred(n):
            for _ in range(n):
                if deferred:
                    try:
                        next(deferred[0])
                    except StopIteration:
                        deferred.pop(0)

        def drain(n):
            for _ in range(n):
                if pending:
                    convert_piece(pending.pop(0), E3)

        def drain_until(pi_end):
            while pending and pending[0] < pi_end:
                convert_piece(pending.pop(0), E3)

        for ti, tile in enumerate(tiles):
          try:
            nb = len(tile)
            T = nb * 128
            is_s = tile[0] == NP
            if first:
                x_loads(tile)
                for b in range(nb):
                    transposes_to(xin[b], B_xin[b], b, B_xT[b])
            Bx_all = [B_xT[b] for b in range(nb)]

            if is_s:
                pt, Bp = ps()
                tk.dma("sp", ck_sb, ck_d[:, :], [], [B_ck], B_ck)
                tk.dma("sp", cv_sb, cv_d[:, :], [], [B_cv], B_cv, chain=True)
                PE([tp(pt[:, 0:128], ck_sb, ident_f)], [B_ck] + Cr, [Bp])
                V(lambda e: e.tensor_copy(out=kT[:, 0:128], in_=pt[:, 0:128]), [Bp], [B_kT])
                V(lambda e: e.tensor_copy(out=Vt[:, 0, :], in_=cv_sb), [B_cv], [B_Vt])
            elif ti > 0:
                pnb = len(tiles[ti - 1])
                V(lambda e: e.tensor_copy(out=kT[:, 0:128], in_=kT[:, pnb * 128:(pnb + 1) * 128]), [B_kT], [B_kT])
                V(lambda e: e.tensor_copy(out=Vt[:, 0, :], in_=Vt[:, pnb, :]), [B_Vt], [B_Vt])

            ev_ctr = [0]

            def evac(out, in_, r, w):
                ev_ctr[0] += 1
                if ev_ctr[0] % 2:
                    A(lambda e: e.copy(out=out, in_=in_), r, w)
                else:
                    V(lambda e: e.tensor_copy(out=out, in_=in_), r, w)

            cs_blk = None
            for b, gb in enumerate(tile):
                if gb == NP - 1 or gb == NP:
                    cs_blk = b
            need_kv = cs_blk is not None

            def fm_chunk(slot, Bs, cc, out_ap, B_out):
                pt, Bp = ps()
                PE([mm(pt[:, 0:T], slot[:, kc, cc * 128:(cc + 1) * 128], xT[:, kc, 0:T], kc == 0, kc == 7)
                    for kc in range(8)], [Bs] + Bx_all, [Bp])
                evac(out_ap, pt[:, 0:T], [Bp], B_out)

            def tm_block(slot, Bs, b, ncols):
                pt, Bp = ps()
                PE([mm(pt[:, 0:ncols], xT[:, kc, b * 128:(b + 1) * 128], slot[:, kc, 0:ncols], kc == 0, kc == 7)
                    for kc in range(8)], [Bs, B_xT[b]], [Bp])
                return pt, Bp

            for zi in range(2):
                slot, Bs = get_piece(P_IN[f"z{zi}"])
                for b in range(nb):
                    pt, Bp = tm_block(slot, Bs, b, 256)
                    A(lambda e: e.activation(out=U[:, 20 + b, zi * 256:(zi + 1) * 256], in_=pt[:, 0:256], func=AF.Silu),
                      [Bp], BU(20 + b, b))
                if zi == 0:
                    run_deferred(1)
            ckpt(3.1)
            if first:
                convert_wout()
            rows07 = sum([BU(c) for c in range(8)], [])
            if is_s:
                V(lambda e: e.tensor_copy(out=U[:, 0:8, 0:3], in_=halo_s[:]), [B_halos], rows07)
            else:
                V(lambda e: e.tensor_copy(out=U[:, 0:8, 0:3], in_=halo[:]), [B_halo], rows07)

            def conv_chunk(c):
                acc, Ba = Fb[4 + c % 2], B_F[4 + c % 2]
                A(lambda e: e.activation(out=acc[:, 0:T], in_=U[:, c, 0:T], func=AF.Identity,
                                         bias=cf[:, CBI + c:CBI + c + 1], scale=cf[:, CW + c * 4:CW + c * 4 + 1]),
                  BU(c) + Cr, [Ba])
                for i in range(1, 4):
                    V(lambda e: e.scalar_tensor_tensor(out=acc[:, 0:T], in0=U[:, c, i:i + T],
                                                       scalar=cf[:, CW + c * 4 + i:CW + c * 4 + i + 1], in1=acc[:, 0:T],
                                                       op0=ALU.mult, op1=ALU.add), BU(c) + [Ba] + Cr, [Ba])
                conv_tail.append((c, acc, Ba))

            conv_tail = []

            def conv_silu():
                if conv_tail:
                    c, acc, Ba = conv_tail.pop(0)
                    A(lambda e: e.activation(out=U[:, 8 + c, 0:T], in_=acc[:, 0:T], func=AF.Silu), [Ba], BU(8 + c))

            for xi, nm in enumerate(("x0", "x1", "B", "C")):
                slot, Bs = get_piece(P_IN[nm])
                for cc in range(2):
                    c = xi * 2 + cc
                    fm_chunk(slot, Bs, cc, U[:, c, 3:3 + T], BU(c))
                    conv_silu()
                    conv_chunk(c)
                if xi < 2:
                    run_deferred(1)
                if cs_blk is not None:
                    pt, Bp = tm_block(slot, Bs, cs_blk, 256)
                    evac(H[:, xi * 256:(xi + 1) * 256], pt[:, 0:256], [Bp], [B_H])
            conv_silu()
            if not is_s:
                V(lambda e: e.tensor_copy(out=halo[:], in_=U[:, 0:8, T:T + 3]), rows07, [B_halo])
            ckpt(3.2)
            if cs_blk is not None:
                r0 = 61 if is_s else 125
                tk.dma("pool", (convs_d if is_s else convp_d)[:, :], H[r0:r0 + 3, :], [B_H], [], B_H)
            ckpt(3.3)
            for qi in range(2):
                slot, Bs = get_piece(P_IN[f"q{qi}"])
                for jj in range(2):
                    j = qi * 2 + jj
                    fm_chunk(slot, Bs, jj, U[:, 16 + j, 0:T], BU(16 + j))
            slot, Bs = get_piece(P_IN["k"])
            fm_chunk(slot, Bs, 0, kT[:, 128:128 + T], [B_kT])
            ckpt(3.4)
            slot, Bs = get_piece(P_IN["kv"])
            for b in range(nb):
                pt, Bp = tm_block(slot, Bs, b, 256)
                if "noV" not in _os.environ.get("KVAR", ""):
                    V(lambda e: e.tensor_copy(out=Vt[:, b + 1, :], in_=pt[:, 128:256]), [Bp], [B_Vt])
                if need_kv and b == cs_blk and "noA" not in _os.environ.get("KVAR", ""):
                    A(lambda e: e.copy(out=ktok[:], in_=pt[:, 0:128]), [Bp], [B_ktok])
                    A(lambda e: e.copy(out=vtok[:], in_=pt[:, 128:256]), [Bp], [B_vtok])
                    if is_s:
                        tk.dma("pool", ks_d[0:64, :], ck_d[64:128, :], [], [], Buf("c2ck"))
                        tk.dma("pool", vs_d[0:64, :], cv_d[64:128, :], [], [], Buf("c2cv"))
                        tk.dma("pool", ks_d[64:128, :], ktok[0:64, :], [B_ktok], [], B_ktok)
                        tk.dma("pool", vs_d[64:128, :], vtok[0:64, :], [B_vtok], [], B_vtok)
                    elif _os.environ.get("KVAR", "") != "nodma":
                        tk.dma("pool", kp_d[:, :], ktok[:], [B_ktok], [], B_ktok)
                        tk.dma("pool", vp_d[:, :], vtok[:], [B_vtok], [], B_vtok)
            ckpt(3.5)
            slot, Bs = get_piece(P_IN["dt"])
            for b in range(nb):
                pt, Bp = tm_block(slot, Bs, b, 8)
                V(lambda e: e.tensor_tensor(out=dtpre[:, b, :], in0=pt[:, 0:8], in1=cfc(DTB, 8), op=ALU.add),
                  [Bp] + Cr, [B_dtpre])
            nb8 = nb * 8
            dp = dtpre[:, 0:nb, :]
            V(lambda e: e.scalar_tensor_tensor(out=dtw[:, 0:nb, :], in0=dp, scalar=-1.0, in1=dp, op0=ALU.mult, op1=ALU.max),
              [B_dtpre], [B_dtw])
            A(lambda e: e.activation(out=dtw[:, 0:nb, :], in_=dtw[:, 0:nb, :], func=AF.Exp, scale=-1.0), [B_dtw], [B_dtw])
            V(lambda e: e.tensor_scalar(out=dtw[:, 0:nb, :], in0=dtw[:, 0:nb, :], scalar1=1.0, scalar2=None, op0=ALU.add),
              [B_dtw], [B_dtw])
            A(lambda e: e.activation(out=dtw[:, 0:nb, :], in_=dtw[:, 0:nb, :], func=AF.Ln), [B_dtw], [B_dtw])
            V(lambda e: e.scalar_tensor_tensor(out=dtall[:, 0:nb, :], in0=dp, scalar=0.0, in1=dtw[:, 0:nb, :],
                                               op0=ALU.max, op1=ALU.add), [B_dtpre, B_dtw], [B_dtall])
            if is_s:
                V(lambda e: e.tensor_scalar(out=dtall[:, 0:nb, :], in0=dtall[:, 0:nb, :], scalar1=cf[:, TOKM:TOKM + 1],
                                            scalar2=None, op0=ALU.mult), [B_dtall] + Cr, [B_dtall])
            V(lambda e: e.tensor_tensor(out=dtAall[:, 0:nb, :], in0=dtall[:, 0:nb, :],
                                        in1=Ab[:].unsqueeze(1).broadcast_to([128, nb, 8]), op=ALU.mult),
              [B_dtall, B_Ab], [B_dtall])
            pc, Bpc = ps()
            dA2 = dtAall[:, 0:nb, :].rearrange("p b h -> p (b h)")
            PE([mm(pc[:, 0:nb8], tri_f, dA2, True, True), mm(pc[:, 32:32 + nb8], ones_f, dA2, True, True)],
               [B_dtall] + Cr, [Bpc])
            fl = lambda t_: t_[:, 0:nb, :].rearrange("p b h -> p (b h)")
            V(lambda e: e.tensor_copy(out=fl(dtw), in_=pc[:, 0:nb8]), [Bpc], [B_dtw])
            A(lambda e: e.activation(out=fl(ecumall), in_=pc[:, 0:nb8], func=AF.Exp), [Bpc], [B_dtall])
            A(lambda e: e.activation(out=fl(bdecall), in_=pc[:, 32:32 + nb8], func=AF.Exp), [Bpc], [B_dtall])
            V(lambda e: e.tensor_tensor(out=fl(dtw), in0=pc[:, 32:32 + nb8], in1=fl(dtw), op=ALU.subtract),
              [Bpc, B_dtw], [B_dtw])
            A(lambda e: e.activation(out=fl(toendall), in_=fl(dtw), func=AF.Exp), [B_dtw], [B_dtall])

            run_deferred(99)
            ckpt(4)
            if is_s:
                pt, Bp = ps()
                tk.dma("sp", sinit[:], sinit_d[:, :], [], [B_sinit], B_sinit)
                PE([tp(pt[:, c * 128:(c + 1) * 128], sinit[:, c * 128:(c + 1) * 128], ident_f) for c in range(4)],
                   [B_sinit] + Cr, [Bp])
                V(lambda e: e.tensor_copy(out=S[:], in_=pt[:]), [Bp], [B_S])
                A(lambda e: e.copy(out=Sb_[:], in_=pt[:]), [Bp], [B_Sb])

            emitted = set()

            def ssd_front(b, gb):
                t0 = b * 128
                ts = slice(t0, t0 + 128)
                par = b % 2
                (L, B_L, decay, B_decay, MT, B_MT, xdt, B_xdt, xdte, B_xdte, xsD, B_xsD, Btok, B_Btok,
                 cbm, B_cbm, gn, B_gn, FT, B_FT, FY, B_FY) = SSDSET[par]
                sm, B_sm = SMSETS[par]
                dt_ = dtall[:, b, :]
                dtA = dtAall[:, b, :]
                ecum = ecumall[:, b, :]
                toend = toendall[:, b, :]
                bdec = bdecall[:, b, :]
                B_dtT = [B_dtall]
                while b >= 2 and ('S', b - 2) not in emitted:
                    yield
                pxs, Bpxs = ps()
                pxs_b = pxs[:].bitcast(BF16)
                PE([tp(pxs_b[:, c * 128:(c + 1) * 128], U[:, 8 + c, ts], ident_b) for c in range(6)],
                   sum([BU(8 + c, b) for c in range(6)], []) + Cbr, [Bpxs])
                xs3 = pxs_b[:, 0:512].rearrange("p (h d) -> p h d", h=8)
                V(lambda e: e.tensor_tensor(out=xdt[:].rearrange("p (h d) -> p h d", h=8), in0=xs3,
                                            in1=dt_.unsqueeze(2).broadcast_to([128, 8, 64]), op=ALU.mult),
                  [Bpxs] + B_dtT, [B_xdt])
                A(lambda e: e.copy(out=Btok[:], in_=pxs_b[:, 512:768]), [Bpxs], [B_Btok])
                yield
                V(lambda e: e.tensor_tensor(out=xdte[:].rearrange("p (h d) -> p h d", h=8),
                                            in0=xdt[:].rearrange("p (h d) -> p h d", h=8),
                                            in1=toend.unsqueeze(2).broadcast_to([128, 8, 64]), op=ALU.mult),
                  [B_xdt] + B_dtT, [B_xdte])
                use_r = True
                V(lambda e: e.tensor_tensor(out=L[:, 0:4, :].bitcast(F32R),
                                            in0=mst_f.unsqueeze(1).broadcast_to([128, 4, 128]),
                                            in1=dtA[:, 0:4].unsqueeze(2).broadcast_to([128, 4, 128]), op=ALU.mult),
                  B_dtT + Cr, [B_L])
                B_La = LA.setdefault(id(B_L), Buf("La"))
                for h_ in range(4, 8):
                    A(lambda e: e.activation(out=L[:, h_, :].bitcast(F32R), in_=mst_f, func=AF.Identity,
                                             scale=dtA[:, h_:h_ + 1]), B_dtT + Cr, [B_La])
                yield
                for hh in range(2):
                    pg, Bpg = ps()
                    PE([mm(pg[:, k * 128:(k + 1) * 128],
                           (L[:, hh * 4 + k, :].bitcast(F32R) if use_r else L[:, hh * 4 + k, :]),
                           (tri_r.bitcast(F32R) if use_r else tri_f), True, True)
                        for k in range(4)], [B_L if hh == 0 else B_La, B_trir] + Cr, [Bpg])
                    A(lambda e: e.activation(out=decay[:, hh * 4:(hh + 1) * 4, :].rearrange("p a b -> p (a b)"),
                                             in_=pg[:], func=AF.Exp), [Bpg], [B_decay])
                    yield
                pcb, Bpcb = ps()
                PE([mm(pcb[:, g * 128:(g + 1) * 128], U[:, 12 + g, ts], U[:, 14 + g, ts], True, True) for g in range(2)],
                   BU(12, b) + BU(13, b) + BU(14, b) + BU(15, b), [Bpcb])
                V(lambda e: e.tensor_tensor(out=cbm[:], in0=pcb[:, 0:256].rearrange("p (g l) -> p g l", g=2),
                                            in1=cau_f.unsqueeze(1).broadcast_to([128, 2, 128]), op=ALU.mult),
                  [Bpcb] + Cr, [B_cbm])
                yield
                for g in range(2):
                    V(lambda e: e.tensor_tensor(out=MT[:, g * 4:(g + 1) * 4, :], in0=decay[:, g * 4:(g + 1) * 4, :],
                                                in1=cbm[:, g:g + 1, :].broadcast_to([128, 4, 128]), op=ALU.mult),
                      [B_decay, B_cbm], [B_MT])
                yield
                emitted.add(('front', b))
                yield

            def ssd_back(b, gb):
                t0 = b * 128
                ts = slice(t0, t0 + 128)
                par = b % 2
                (L, B_L, decay, B_decay, MT, B_MT, xdt, B_xdt, xdte, B_xdte, xsD, B_xsD, Btok, B_Btok,
                 cbm, B_cbm, gn, B_gn, FT, B_FT, FY, B_FY) = SSDSET[par]
                sm, B_sm = SMSETS[par]
                dt_ = dtall[:, b, :]
                dtA = dtAall[:, b, :]
                ecum = ecumall[:, b, :]
                toend = toendall[:, b, :]
                bdec = bdecall[:, b, :]
                B_dtT = [B_dtall]
                while ('front', b) not in emitted:
                    yield
                while b >= 1 and ('S', b - 1) not in emitted:
                    yield
                py, Bpy = ps()
                fns = [mm(py[:, c_ * 128:(c_ + 1) * 128], U[:, 8 + c_, ts], diagD[:, c_, :], c_ == 0, False) for c_ in range(4)]
                for h in range(8):
                    fns.append(mm(py[:, h * 64:(h + 1) * 64], MT[:, h, :], xdt[:, h * 64:(h + 1) * 64], False, h == 7))
                PE(fns, [B_diagD, B_MT, B_xdt] + sum([BU(8 + c_, b) for c_ in range(4)], []), [Bpy])
                pz, Bpz = ps()
                PE([mm(pz[:, g * 256:(g + 1) * 256], U[:, 14 + g, ts], Sb_[:, g * 256:(g + 1) * 256], True, True)
                    for g in range(2)], BU(14, b) + BU(15, b) + [B_Sb], [Bpz])
                t1, yv, gvv = FT, FY, FT
                V(lambda e: e.tensor_tensor(out=t1[:].rearrange("p (h d) -> p h d", h=8),
                                            in0=pz[:].rearrange("p (h d) -> p h d", h=8),
                                            in1=ecum.unsqueeze(2).broadcast_to([128, 8, 64]), op=ALU.mult),
                  [Bpz] + B_dtT, [B_FT])
                V(lambda e: e.tensor_tensor(out=yv[:], in0=py[:], in1=t1[:], op=ALU.add), [Bpy, B_FT], [B_FY])
                yield
                pn, Bpn = ps()
                PE([mm(pn[:, g * 256:(g + 1) * 256], Btok[:, g * 128:(g + 1) * 128], xdte[:, g * 256:(g + 1) * 256],
                       True, True) for g in range(2)], [B_Btok, B_xdte], [Bpn])
                V(lambda e: e.tensor_tensor(out=S[:].rearrange("p (h d) -> p h d", h=8),
                                            in0=S[:].rearrange("p (h d) -> p h d", h=8),
                                            in1=bdec.unsqueeze(2).broadcast_to([128, 8, 64]), op=ALU.mult),
                  [B_S] + B_dtT, [B_S])
                V(lambda e: e.tensor_tensor(out=S[:], in0=pn[:], in1=S[:], op=ALU.add), [Bpn, B_S], [B_S])
                A(lambda e: e.copy(out=Sb_[:], in_=S[:]), [B_S], [B_Sb])
                emitted.add(('S', b))
                if gb == NP - 1 or gb == NP:
                    pt, Bp = ps()
                    PE([tp(pt[:, c * 128:(c + 1) * 128], S[:, c * 128:(c + 1) * 128], ident_f) for c in range(4)],
                       [B_S] + Cr, [Bp])
                    V(lambda e: e.tensor_copy(out=sout[:], in_=pt[:]), [Bp], [B_sout])
                    tk.dma("pool", (ssms_d if is_s else ssmp_d)[:, :], sout[:], [B_sout], [], B_sout)
                yield
                V(lambda e: e.tensor_tensor(out=gvv[:], in0=yv[:], in1=U[:, 20 + b, 0:512], op=ALU.mult),
                  [B_FY] + BU(20 + b, b), [B_FT])
                st2, mv2, ms, lms, rs2 = sm["st2"], sm["mv2"], sm["ms"], sm["lms"], sm["rs2"]
                for g in range(2):
                    V(lambda e: e.bn_stats(out=st2[:, g * 6:(g + 1) * 6], in_=gvv[:, g * 256:(g + 1) * 256]),
                      [B_FT], [B_sm["st2"]])
                    V(lambda e: e.bn_aggr(out=mv2[:, g * 2:(g + 1) * 2], in_=st2[:, g * 6:(g + 1) * 6]),
                      [B_sm["st2"]], [B_sm["mv2"]])
                    V(lambda e: e.scalar_tensor_tensor(out=ms[:, g:g + 1], in0=mv2[:, g * 2:g * 2 + 1],
                                                       scalar=mv2[:, g * 2:g * 2 + 1], in1=mv2[:, g * 2 + 1:g * 2 + 2],
                                                       op0=ALU.mult, op1=ALU.add), [B_sm["mv2"]], [B_sm["ms"]])
                V(lambda e: e.tensor_scalar(out=ms[:], in0=ms[:], scalar1=RMS_EPS, scalar2=None, op0=ALU.add),
                  [B_sm["ms"]], [B_sm["ms"]])
                A(lambda e: e.activation(out=lms[:], in_=ms[:], func=AF.Ln), [B_sm["ms"]], [B_sm["lms"]])
                A(lambda e: e.activation(out=rs2[:], in_=lms[:], func=AF.Exp, scale=-0.5), [B_sm["lms"]], [B_sm["rs2"]])
                yield
                for g in range(2):
                    V(lambda e: e.scalar_tensor_tensor(out=gn[:, g * 256:(g + 1) * 256], in0=gvv[:, g * 256:(g + 1) * 256],
                                                       scalar=rs2[:, g:g + 1], in1=cf[:, WN + g * 256:WN + (g + 1) * 256],
                                                       op0=ALU.mult, op1=ALU.mult), [B_FT, B_sm["rs2"]] + Cr, [B_gn])
                yield
                while b >= 2 and ('outmm', b - 2) not in emitted:
                    yield
                pgt, Bpgt = ps()
                pgt_b = pgt[:].bitcast(BF16)
                PE([tp(pgt_b[:, c * 128:(c + 1) * 128], gn[:, c * 128:(c + 1) * 128], ident_b) for c in range(4)],
                   [B_gn] + Cbr, [Bpgt])
                A(lambda e: e.copy(out=mixS[b % 2][:], in_=pgt_b[:, 0:512].rearrange("p (c t) -> p c t", c=4)),
                  [Bpgt], [B_mixS[b % 2]])

                emitted.add(('ssd', b))
                yield

            def att_chain(b, gb):
                t0 = b * 128
                ts = slice(t0, t0 + 128)
                kbs = [1] if gb == 0 else [0, 1]
                for g in range(2):
                    gs = slice(g * 64, (g + 1) * 64)
                    for kb in kbs:
                        gkb = g * 2 + kb
                        kc0 = t0 + kb * 128
                        psc, Bpsc = ps()
                        PE([mm(psc[:], kT[gs, kc0:kc0 + 128], U[gs, 16:20, ts], True, False),
                            mm(psc[:], ident_b, cb[:, BIAS + (gkb * 2) * 512:BIAS + (gkb * 2 + 1) * 512], False, False),
                            mm(psc[:], ident_b, cb[:, BIAS + (gkb * 2 + 1) * 512:BIAS + (gkb * 2 + 2) * 512], False, True)],
                           [B_kT] + BU(16, b) + BU(17, b) + BU(18, b) + BU(19, b) + Cbr, [Bpsc])
                        A(lambda e: e.activation(out=PT[gkb][:], in_=psc[:], func=AF.Exp, scale=0.125),
                          [Bpsc], [B_PT[gkb]])
                        yield
                po, Bpo = ps()
                pd, Bpd = ps()
                fo, fd = [], []
                for g in range(2):
                    gs = slice(g * 64, (g + 1) * 64)
                    for ki, kb in enumerate(kbs):
                        fo.append(mm(po[gs, :], Vt[:, b + kb, gs], PT[g * 2 + kb][:], ki == 0, ki == len(kbs) - 1))
                        fd.append(mm(pd[gs, :], ones_b[:, 0:64], PT[g * 2 + kb][:], ki == 0, False))
                    fd.append(mm(pd[gs, :], ones_b[:, 0:64], sinkrow[:, g * 512:(g + 1) * 512], False, True))
                PE(fo, [B_Vt] + B_PT, [Bpo])
                PE(fd, B_PT + [B_sink] + Cbr, [Bpd])
                lnd, rcp = Fb[0], Fb[1]
                A(lambda e: e.activation(out=lnd[:], in_=pd[:], func=AF.Ln), [Bpd], [B_F[0]])
                A(lambda e: e.activation(out=rcp[:], in_=lnd[:], func=AF.Exp, scale=-1.0), [B_F[0]], [B_F[1]])
                V(lambda e: e.tensor_tensor(out=attT[:, :, ts], in0=po[:].rearrange("p (r q) -> p r q", r=4),
                                            in1=rcp[:].rearrange("p (r q) -> p r q", r=4), op=ALU.mult),
                  [Bpo, B_F[1]], [B_attT[b]])

                emitted.add(('att', b))
                yield

            def out_chain(b, gb):
                while ('ssd', b) not in emitted or ('att', b) not in emitted:
                    yield
                t0 = b * 128
                ts = slice(t0, t0 + 128)
                Hx, B_Hx = HS[b % 2]
                for dh in range(2):
                    pm, Bpm = ps()
                    PE([mm(pm[:], (mixS[b % 2][:, kc, :] if kc < 4 else attT[:, kc - 4, ts]),
                           wout[:, kc, dh * 512:(dh + 1) * 512], kc == 0, False)
                        for kc in range(8)] +
                       [mm(pm[:], aident[:], xin[b][:, dh * 512:(dh + 1) * 512], False, True)],
                       [B_mixS[b % 2], B_attT[b], B_wout, B_xin[b], B_aident], [Bpm])
                    A(lambda e: e.copy(out=Hx[:, dh * 512:(dh + 1) * 512], in_=pm[:]), [Bpm], [B_Hx])
                    yield
                emitted.add(('outmm', b))
                yield from ln_gen([(Hx, B_Hx, L1G, L1B, hb[b], B_hb[b], b, False, None)])
                transposes_to(hb[b], B_hb[b], b, B_xT[b], act_only=True)
                emitted.add(('out', b))
                yield

            def p4_gen(c0, c1, Bxs, bsel, use_load=False):
                n_ = c1 - c0
                for i in range(11):
                    gslot, Bg = get_piece(P_G0 + 2 * i)
                    uslot, Bu = get_piece(P_G0 + 2 * i + 1)
                    for jj in range(2):
                        j = i * 2 + jj
                        pgg, Bpgg = ps()
                        puu, Bpuu = ps()
                        PE([mm(pgg[:, 0:n_], gslot[:, kc, jj * 128:(jj + 1) * 128], xT[:, kc, c0:c1], kc == 0, kc == 7)
                            for kc in range(8)], [Bg] + Bxs, [Bpgg])
                        PE([mm(puu[:, 0:n_], uslot[:, kc, jj * 128:(jj + 1) * 128], xT[:, kc, c0:c1], kc == 0, kc == 7)
                            for kc in range(8)], [Bu] + Bxs, [Bpuu])
                        sg, Bsg = Fb[j % 2], B_F[j % 2]
                        A(lambda e: e.activation(out=sg[:, 0:n_], in_=pgg[:, 0:n_], func=AF.Silu), [Bpgg], [Bsg])
                        V(lambda e: e.tensor_tensor(out=U[:, j, c0:c1], in0=sg[:, 0:n_], in1=puu[:, 0:n_], op=ALU.mult),
                          [Bsg, Bpuu], BU(j, bsel))
                        yield

            def p4h0_chain():
                while ('out', 0) not in emitted or ('out', 1) not in emitted:
                    yield
                yield from p4_gen(0, 256, [B_xT[0], B_xT[1]], 0)

            interleave_p4 = False

            def seq(gens):
                for g_ in gens:
                    yield from g_

            chains = [seq([ssd_front(b, gb) for b, gb in enumerate(tile) if b % 2 == 0]),
                      seq([ssd_front(b, gb) for b, gb in enumerate(tile) if b % 2 == 1]),
                      seq([ssd_back(b, gb) for b, gb in enumerate(tile) if b % 2 == 0]),
                      seq([ssd_back(b, gb) for b, gb in enumerate(tile) if b % 2 == 1]),
                      seq([att_chain(b, gb) for b, gb in enumerate(tile)]),
                      seq([out_chain(b, gb) for b, gb in enumerate(tile) if b % 2 == 0]),
                      seq([out_chain(b, gb) for b, gb in enumerate(tile) if b % 2 == 1])]
            if interleave_p4:
                chains.append(p4h0_chain())
            steps_per_round = [1] * 8
            while chains:
                for ci_, c_ in enumerate(list(chains)):
                    try:
                        for _ in range(steps_per_round[ci_] if len(chains) >= 3 else 1):
                            next(c_)
                    except StopIteration:
                        chains.remove(c_)
            ckpt(6)

            drain_until(P_D0)

            if not interleave_p4:
                for _ in p4_gen(0, T, Bx_all, None):
                    pass
            else:
                for _ in p4_gen(256, 512, [B_xT[2], B_xT[3]], 2, use_load=True):
                    pass

            ckpt(7)
            drain_until(NPIECE)

            if ti + 1 < len(tiles):
                nxt = tiles[ti + 1]
                x_loads(nxt)
                for b in range(len(nxt)):
                    transposes_to(xin[b], B_xin[b], b, B_xT[b])

            for dq in range(4):
                pbs = [ps() for _ in range(nb)]
                for fp in range(3):
                    dslot, Bd = get_piece(P_D0 + dq * 3 + fp)
                    nf = 8 if fp < 2 else 6
                    for b in range(nb):
                        pb, Bpb = pbs[b]
                        PE([mm(pb[:, 0:256], U[:, fp * 8 + k, b * 128:(b + 1) * 128], dslot[:, k, :],
                               fp == 0 and k == 0, fp == 2 and k == nf - 1) for k in range(nf)],
                           [Bd] + sum([BU(fp * 8 + k, b) for k in range(nf)], []), [Bpb])
                for b in range(nb):
                    pb, Bpb = pbs[b]
                    V(lambda e: e.scalar_tensor_tensor(out=hb[b][:, dq * 256:(dq + 1) * 256],
                                                       in0=hb[b][:, dq * 256:(dq + 1) * 256], scalar=ALPHA,
                                                       in1=pb[:, 0:256], op0=ALU.mult, op1=ALU.add),
                      [B_hb[b], Bpb], [B_hb[b]])
            def mk_store(b, gb, is_s_):
                def post():
                    if is_s_:
                        tk.dma("pool", ys_d[:, :], hb[b][0:64, :], [B_hb[b]], [], B_hb[b])
                    else:
                        tk.dma("pool", y_d[gb * 128:(gb + 1) * 128, :], hb[b][:], [B_hb[b]], [], B_hb[b])
                return post

            deferred.append(ln_gen([(hb[b], B_hb[b], L2G, L2B, hb[b], B_hb[b], b, False, mk_store(b, gb, is_s))
                                    for b, gb in enumerate(tile)]))
            run_deferred(3)
            first = False
            first_flag[0] = False
          except _Stop:
            break

        run_deferred(99)
        tk.finish()
    return nc


def _consts():
    cf = np.zeros((128, NCF), np.float32)
    i = np.arange(128)
    cf[:, IDF:IDF + 128] = np.eye(128)
    cf[:, TRI:TRI + 128] = (i[:, None] <= i[None, :])
    cf[:, MST:MST + 128] = (i[None, :] < i[:, None])
    cf[:, CAU:CAU + 128] = (i[:, None] <= i[None, :])
    cf[:, ONE:ONE + 128] = 1.0
    cf[:, TOKM] = (i < 64)
    cb = np.zeros((128, NCB), np.float32)
    cb[:, IDB:IDB + 128] = np.eye(128)
    cb[:, ONB:ONB + 128] = 1.0
    lo_part = np.zeros((128, NCB), np.float32)
    s = i[:, None].astype(np.float64)
    q = i[None, :].astype(np.float64)
    for g in range(2):
        for kb in range(2):
            gkb = g * 2 + kb
            tile_ = np.zeros((128, 512), np.float64)
            for r in range(4):
                m = 2.0 ** -(g * 4 + r + 1)
                if kb == 0:
                    dist = 128.0 + q - s
                    bad = (q >= 64) & (s < 64)
                else:
                    dist = np.abs(q - s)
                    bad = (q < 64) & (s >= 64)
                v = -8.0 * m * dist
                v = np.where(bad, -32768.0, v)
                tile_[:, r * 128:(r + 1) * 128] = v
            hi = tile_.astype(np.float32).astype(ml_dtypes.bfloat16)
            lo = (tile_ - hi.astype(np.float64)).astype(np.float32).astype(ml_dtypes.bfloat16)
            o = BIAS + (gkb * 2) * 512
            cb[:, o:o + 512] = hi.astype(np.float32)
            cb[:, o + 512:o + 1024] = lo.astype(np.float32)
    return cf, cb.astype(ml_dtypes.bfloat16)


_NC_CACHE = {}
_SIM_HOOK = None


def _run(NP, x_prompt, x_sample, state_conv, state_ssm, cache_k, cache_v, w_in, conv_w, conv_b, dt_bias,
         a_log, d_skip, ssm_norm_w, attn_sinks, w_out, ln1_g, ln1_b, w_gate, w_up, w_down, ln2_g, ln2_b):
    f = np.float32
    ncores = x_prompt.shape[0]
    cf, cb = _consts()
    cf[:, CW:CW + 32] = np.asarray(conv_w[0], f).reshape(4, 8, 128).transpose(2, 1, 0).reshape(128, 32)
    cf[:, CBI:CBI + 8] = np.asarray(conv_b[0], f).reshape(8, 128).T
    cf[:, DTB:DTB + 8] = np.broadcast_to(np.asarray(dt_bias[0], f), (128, 8))
    cf[:, ALOG:ALOG + 8] = np.broadcast_to(np.asarray(a_log[0], f), (128, 8))
    cf[:, DFULL:DFULL + 512] = np.broadcast_to(np.repeat(np.asarray(d_skip[0], f), 64), (128, 512))
    cf[:, DCOL:DCOL + 4] = np.repeat(np.asarray(d_skip[0], f), 64).reshape(4, 128).T
    cf[:, WN:WN + 512] = np.broadcast_to(np.asarray(ssm_norm_w[0], f), (128, 512))
    cf[:, L1G:L1G + 1024] = np.broadcast_to(np.asarray(ln1_g[0], f), (128, 1024))
    cf[:, L1B:L1B + 1024] = np.broadcast_to(np.asarray(ln1_b[0], f), (128, 1024))
    cf[:, L2G:L2G + 1024] = np.broadcast_to(np.asarray(ln2_g[0], f), (128, 1024))
    cf[:, L2B:L2B + 1024] = np.broadcast_to(np.asarray(ln2_b[0], f), (128, 1024))
    sink = np.ascontiguousarray(np.broadcast_to(np.repeat(np.asarray(attn_sinks[0], f), 128), (128, 1024)))
    if NP not in _NC_CACHE:
        _NC_CACHE[NP] = build(NP)
    nc = _NC_CACHE[NP]
    shared = {"w_in": np.ascontiguousarray(w_in[0], f), "w_out": np.ascontiguousarray(w_out[0], f),
              "w_gate": np.ascontiguousarray(w_gate[0], f), "w_up": np.ascontiguousarray(w_up[0], f),
              "w_down": np.ascontiguousarray(w_down[0], f), "cf": cf, "cb": cb, "sink": sink}
    in_maps = []
    for c in range(ncores):
        m = dict(shared)
        m["x"] = np.ascontiguousarray(x_prompt[c], f)
        m["xs"] = np.ascontiguousarray(x_sample[c], f)
        m["halo_s"] = np.ascontiguousarray(np.asarray(state_conv[0, c], f).reshape(3, 8, 128).transpose(2, 1, 0)).reshape(128, 24)
        m["sinit"] = np.ascontiguousarray(np.asarray(state_ssm[0, c], f).reshape(4, 128, 128).transpose(1, 0, 2)).reshape(128, 512)
        m["ck"] = np.ascontiguousarray(np.asarray(cache_k[0, c], f).reshape(128, 128))
        m["cv"] = np.ascontiguousarray(np.asarray(cache_v[0, c], f).reshape(128, 128))
        in_maps.append(m)
    if _SIM_HOOK is not None:
        R = _SIM_HOOK(nc, in_maps)
    else:
        res = run_bass_kernel_spmd(nc, in_maps, core_ids=list(range(ncores)))
        R = res.results

    def ssm(a):
        return np.asarray(a, f).reshape(128, 4, 128).transpose(1, 0, 2).reshape(8, 64, 128)

    y_p = np.stack([np.asarray(r["y"], f) for r in R])
    y_s = np.stack([np.asarray(r["ys"], f) for r in R])
    conv_p = np.stack([np.asarray(r["conv_p"], f) for r in R])[None]
    ssm_p = np.stack([ssm(r["ssm_p"]) for r in R])[None]
    k_p = np.stack([np.asarray(r["k_p"], f).reshape(128, 2, 64) for r in R])[None]
    v_p = np.stack([np.asarray(r["v_p"], f).reshape(128, 2, 64) for r in R])[None]
    conv_s = np.stack([np.asarray(r["conv_s"], f) for r in R])[None]
    ssm_s = np.stack([ssm(r["ssm_s"]) for r in R])[None]
    k_s = np.stack([np.asarray(r["k_s"], f).reshape(128, 2, 64) for r in R])[None]
    v_s = np.stack([np.asarray(r["v_s"], f).reshape(128, 2, 64) for r in R])[None]
    return (y_p, y_s, conv_p, ssm_p, k_p, v_p, conv_s, ssm_s, k_s, v_s)


def kernel(**inputs):
    inputs = {k: np.asarray(v) for k, v in inputs.items()}
    NP = inputs["x_prompt"].shape[1] // 128
    return _run(NP, **inputs)
```
